# Optimizing a Trainium2 kernel written in Bass

```python
import math
import jax, jax.numpy as jnp
from jax import lax
import numpy as np

D_MODEL = 2048
BATCH = 8
SEQ = 2048
DEPTH = 1

MOBA_HEADS = 8
MOBA_HEAD_DIM = 128
MOBA_W = MOBA_HEADS * MOBA_HEAD_DIM
MOBA_BLOCK = 256
MOBA_TOPK = 3
MOBA_Q_CHUNK = 16
REL_BUCKETS = 32
REL_MAX_DIST = 128
GDN_QK_HEADS = 8
GDN_V_HEADS = 16
GDN_HEAD_DIM = 128
GDN_QK_W = GDN_QK_HEADS * GDN_HEAD_DIM
GDN_V_W = GDN_V_HEADS * GDN_HEAD_DIM
GDN_CONV_CH = 2 * GDN_QK_W + GDN_V_W
GDN_CONV = 4
GDN_CHUNK = 64
N_IN = 3 * MOBA_W + GDN_CONV_CH + GDN_V_W + 2 * GDN_V_HEADS + 2 * D_MODEL
D_FF = -(-8 * D_MODEL // (3 * 256)) * 256
DEEPNORM_ALPHA = (2.0 * DEPTH) ** 0.25
DEEPNORM_BETA = (8.0 * DEPTH) ** -0.25
LN_EPS = 1e-5
RMS_EPS = 1e-6
NEG_INF = -1e30

kernel_name = 'hybrid_moba_gdn_deepnorm_adaln_layer'


def layer_norm(x, gain, bias):
    xf = x.astype(jnp.float32)
    mu = xf.mean(-1, keepdims=True)
    var = jnp.square(xf - mu).mean(-1, keepdims=True)
    return ((xf - mu) * lax.rsqrt(var + LN_EPS) * gain + bias).astype(x.dtype)


def l2norm(x):
    return x * lax.rsqrt(jnp.sum(x * x, -1, keepdims=True) + RMS_EPS)


def rel_bucket(dist):
    max_exact = REL_BUCKETS // 2
    n = jnp.maximum(dist, 0)
    nf = jnp.maximum(n, 1).astype(jnp.float32)
    large = max_exact + (jnp.log(nf / max_exact) / math.log(REL_MAX_DIST / max_exact)
                         * (REL_BUCKETS - max_exact)).astype(jnp.int32)
    large = jnp.minimum(large, REL_BUCKETS - 1)
    return jnp.where(n < max_exact, n, large)


def moba_attention(q, k, v, rel_bias):
    B, S, H, Dh = q.shape
    s_pad = -(-S // MOBA_BLOCK) * MOBA_BLOCK
    pad = ((0, 0), (0, s_pad - S), (0, 0), (0, 0))
    q, k, v = [jnp.pad(t, pad).transpose(0, 2, 1, 3) for t in (q, k, v)]
    nb = s_pad // MOBA_BLOCK
    topk = min(MOBA_TOPK, nb)
    kb = k.reshape(B, H, nb, MOBA_BLOCK, Dh)
    vb = v.reshape(B, H, nb, MOBA_BLOCK, Dh)
    k_mean = kb.astype(jnp.float32).mean(3)
    scale = Dh ** -0.5
    tab_h = rel_bias.T
    b_idx = jnp.arange(B)[:, None, None, None]
    h_idx = jnp.arange(H)[None, :, None, None]
    offs = jnp.arange(MOBA_BLOCK)

    def chunk(ci):
        start = ci * MOBA_Q_CHUNK
        qc = lax.dynamic_slice_in_dim(q, start, MOBA_Q_CHUNK, axis=2)
        qpos = start + jnp.arange(MOBA_Q_CHUNK)
        qblk = start // MOBA_BLOCK
        route = jnp.einsum('bhqd,bhnd->bhqn', qc.astype(jnp.float32), k_mean)
        route = jnp.where(jnp.arange(nb) < qblk, route, NEG_INF)
        _, sel = lax.top_k(route, topk)
        valid = sel < qblk
        k_sel = kb[b_idx, h_idx, sel]
        v_sel = vb[b_idx, h_idx, sel]
        kpos_sel = sel[..., None] * MOBA_BLOCK + offs
        s_sel = jnp.einsum('bhqd,bhqnkd->bhqnk', qc, k_sel).astype(jnp.float32) * scale
        s_sel = s_sel + tab_h[h_idx[..., None], rel_bucket(qpos[:, None, None] - kpos_sel)]
        s_sel = jnp.where(valid[..., None], s_sel, NEG_INF)
        k_own = lax.dynamic_slice_in_dim(k, qblk * MOBA_BLOCK, MOBA_BLOCK, axis=2)
        v_own = lax.dynamic_slice_in_dim(v, qblk * MOBA_BLOCK, MOBA_BLOCK, axis=2)
        dist = qpos[:, None] - (qblk * MOBA_BLOCK + offs)[None, :]
        s_own = jnp.einsum('bhqd,bhkd->bhqk', qc, k_own).astype(jnp.float32) * scale
        s_own = jnp.where(dist >= 0, s_own + tab_h[:, rel_bucket(dist)], NEG_INF)
        logits = jnp.concatenate([s_own, s_sel.reshape(B, H, MOBA_Q_CHUNK, topk * MOBA_BLOCK)], -1)
        p = jax.nn.softmax(logits, axis=-1).astype(v.dtype)
        p_own = p[..., :MOBA_BLOCK]
        p_sel = p[..., MOBA_BLOCK:].reshape(B, H, MOBA_Q_CHUNK, topk, MOBA_BLOCK)
        return (jnp.einsum('bhqk,bhkd->bhqd', p_own, v_own)
                + jnp.einsum('bhqnk,bhqnkd->bhqd', p_sel, v_sel))

    out = lax.map(chunk, jnp.arange(S // MOBA_Q_CHUNK))
    return out.transpose(1, 0, 3, 2, 4).reshape(B, S, H * Dh)


def causal_dwconv_silu(x, w):
    y = lax.conv_general_dilated(x, w[:, None, :], window_strides=(1,),
                                 padding=((GDN_CONV - 1, 0),),
                                 dimension_numbers=('NWC', 'WIO', 'NWC'),
                                 feature_group_count=x.shape[-1])
    return jax.nn.silu(y)


def gated_delta_rule(q, k, v, g, beta):
    B, H, S, Dk = k.shape
    Dv = v.shape[-1]
    C = GDN_CHUNK
    N = S // C
    q = q.reshape(B, H, N, C, Dk)
    k = k.reshape(B, H, N, C, Dk)
    v = v.reshape(B, H, N, C, Dv)
    beta = beta.reshape(B, H, N, C)
    g_cum = jnp.cumsum(g.reshape(B, H, N, C), axis=-1)
    tril = jnp.tril(jnp.ones((C, C), bool))
    strict = jnp.tril(jnp.ones((C, C), bool), -1)
    diff = g_cum[..., :, None] - g_cum[..., None, :]
    decay = jnp.where(tril, jnp.exp(jnp.where(tril, diff, 0.0)), 0.0)
    k_beta = k * beta[..., None]
    v_beta = v * beta[..., None]
    L = jnp.where(strict, jnp.einsum('bhncd,bhnjd->bhncj', k_beta, k) * decay, 0.0)
    A = L + jnp.eye(C, dtype=jnp.float32)
    rhs = jnp.concatenate([v_beta, k_beta * jnp.exp(g_cum)[..., None]], -1)
    sol = lax.linalg.triangular_solve(A, rhs, left_side=True, lower=True, unit_diagonal=True)
    u = sol[..., :Dv]
    w = sol[..., Dv:]
    attn = jnp.where(tril, jnp.einsum('bhncd,bhnjd->bhncj', q, k) * decay, 0.0)
    q_dec = q * jnp.exp(g_cum)[..., None]
    k_dec = k * jnp.exp(g_cum[..., -1:] - g_cum)[..., None]
    chunk_decay = jnp.exp(g_cum[..., -1])

    def step(state, xs):
        u_n, w_n, a_n, qd_n, kd_n, cd_n = xs
        v_new = u_n - jnp.einsum('bhcd,bhde->bhce', w_n, state)
        o = jnp.einsum('bhcd,bhde->bhce', qd_n, state) + jnp.einsum('bhcj,bhje->bhce', a_n, v_new)
        state = state * cd_n[..., None, None] + jnp.einsum('bhcd,bhce->bhde', kd_n, v_new)
        return state, o

    xs = tuple(jnp.moveaxis(t, 2, 0) for t in (u, w, attn, q_dec, k_dec, chunk_decay))
    state0 = jnp.zeros((B, H, Dk, Dv), jnp.float32)
    _, o = lax.scan(step, state0, xs)
    return jnp.moveaxis(o, 0, 2).reshape(B, H, S, Dv)


def token_mixer(h, w_in, conv_w, a_log, dt_bias, gdn_norm_w, rel_bias, w_proj_moba, w_proj_gdn, w_out):
    B, S, _ = h.shape
    f32 = jnp.float32
    proj = h @ w_in
    o1 = 3 * MOBA_W
    o2 = o1 + GDN_CONV_CH
    o3 = o2 + GDN_V_W
    o4 = o3 + GDN_V_HEADS
    o5 = o4 + GDN_V_HEADS
    moba_qkv, gdn_qkv, gdn_z, gdn_b, gdn_a, gate_raw = jnp.split(proj, [o1, o2, o3, o4, o5], axis=-1)
    qa, ka, va = [t.reshape(B, S, MOBA_HEADS, MOBA_HEAD_DIM) for t in jnp.split(moba_qkv, 3, axis=-1)]
    y_a = moba_attention(qa, ka, va, rel_bias)
    qkv = causal_dwconv_silu(gdn_qkv, conv_w)
    qb, kb, vb = jnp.split(qkv, [GDN_QK_W, 2 * GDN_QK_W], axis=-1)
    rep = GDN_V_HEADS // GDN_QK_HEADS
    qb = l2norm(qb.reshape(B, S, GDN_QK_HEADS, GDN_HEAD_DIM).astype(f32)) * GDN_HEAD_DIM ** -0.5
    kb = l2norm(kb.reshape(B, S, GDN_QK_HEADS, GDN_HEAD_DIM).astype(f32))
    qb = jnp.repeat(qb, rep, axis=2)
    kb = jnp.repeat(kb, rep, axis=2)
    vb = vb.reshape(B, S, GDN_V_HEADS, GDN_HEAD_DIM).astype(f32)
    beta = jax.nn.sigmoid(gdn_b.astype(f32))
    g = -jnp.exp(a_log.astype(f32)) * jax.nn.softplus(gdn_a.astype(f32) + dt_bias.astype(f32))
    o = gated_delta_rule(qb.transpose(0, 2, 1, 3), kb.transpose(0, 2, 1, 3), vb.transpose(0, 2, 1, 3),
                         g.transpose(0, 2, 1), beta.transpose(0, 2, 1)).transpose(0, 2, 1, 3)
    z = gdn_z.reshape(B, S, GDN_V_HEADS, GDN_HEAD_DIM).astype(f32)
    o = (o * lax.rsqrt(jnp.mean(o * o, -1, keepdims=True) + RMS_EPS) * gdn_norm_w.astype(f32)
         * jax.nn.silu(z))
    y_b = o.reshape(B, S, GDN_V_W).astype(h.dtype)
    gate_a, gate_b = jnp.split(gate_raw, 2, axis=-1)
    merged = jax.nn.sigmoid(gate_a) * (y_a @ w_proj_moba) + jax.nn.sigmoid(gate_b) * (y_b @ w_proj_gdn)
    return merged @ w_out


def swiglu(h, w_ffn_in, w_ffn_out):
    gate, up = jnp.split(h @ w_ffn_in, 2, axis=-1)
    return (jax.nn.silu(gate) * up) @ w_ffn_out


def setup_inputs(seed: int = 0) -> dict:
    key = jax.random.key(seed)
    ks = jax.random.split(key, 20)
    f32 = jnp.float32

    def nrm(k, shape, scale):
        return jax.random.normal(k, shape, f32) * scale

    x = nrm(ks[0], (BATCH, SEQ, D_MODEL), 1.0)
    c = nrm(ks[1], (BATCH, D_MODEL), 1.0)
    w_ada = nrm(ks[2], (DEPTH, D_MODEL, 6 * D_MODEL), D_MODEL ** -0.5)
    b_ada = nrm(ks[3], (DEPTH, 6 * D_MODEL), 0.02)
    w_in = nrm(ks[4], (DEPTH, D_MODEL, N_IN), D_MODEL ** -0.5)
    conv_w = nrm(ks[5], (DEPTH, GDN_CONV, GDN_CONV_CH), GDN_CONV ** -0.5)
    a_log = jnp.log(jax.random.uniform(ks[6], (DEPTH, GDN_V_HEADS), f32, 1.0, 16.0))
    dt = jnp.exp(jax.random.uniform(ks[7], (DEPTH, GDN_V_HEADS), f32, math.log(1e-3), math.log(1e-1)))
    dt_bias = dt + jnp.log(-jnp.expm1(-dt))
    gdn_norm_w = 1.0 + nrm(ks[8], (DEPTH, GDN_HEAD_DIM), 0.02)
    rel_bias = nrm(ks[9], (REL_BUCKETS, MOBA_HEADS), 0.5)
    w_proj_moba = nrm(ks[10], (DEPTH, MOBA_W, D_MODEL), MOBA_W ** -0.5)
    w_proj_gdn = nrm(ks[11], (DEPTH, GDN_V_W, D_MODEL), GDN_V_W ** -0.5)
    w_out = nrm(ks[12], (DEPTH, D_MODEL, D_MODEL), DEEPNORM_BETA * D_MODEL ** -0.5)
    ln1_g = 1.0 + nrm(ks[13], (DEPTH, D_MODEL), 0.02)
    ln1_b = nrm(ks[14], (DEPTH, D_MODEL), 0.02)
    w_ffn_in = nrm(ks[15], (DEPTH, D_MODEL, 2 * D_FF), D_MODEL ** -0.5)
    w_ffn_out = nrm(ks[16], (DEPTH, D_FF, D_MODEL), DEEPNORM_BETA * D_FF ** -0.5)
    ln2_g = 1.0 + nrm(ks[17], (DEPTH, D_MODEL), 0.02)
    ln2_b = nrm(ks[18], (DEPTH, D_MODEL), 0.02)
    return {'x': x, 'c': c, 'w_ada': w_ada, 'b_ada': b_ada, 'w_in': w_in, 'conv_w': conv_w,
            'a_log': a_log, 'dt_bias': dt_bias, 'gdn_norm_w': gdn_norm_w, 'rel_bias': rel_bias,
            'w_proj_moba': w_proj_moba, 'w_proj_gdn': w_proj_gdn, 'w_out': w_out,
            'ln1_g': ln1_g, 'ln1_b': ln1_b, 'w_ffn_in': w_ffn_in, 'w_ffn_out': w_ffn_out,
            'ln2_g': ln2_g, 'ln2_b': ln2_b}


def reference(x, c, w_ada, b_ada, w_in, conv_w, a_log, dt_bias, gdn_norm_w, rel_bias,
              w_proj_moba, w_proj_gdn, w_out, ln1_g, ln1_b, w_ffn_in, w_ffn_out, ln2_g, ln2_b):
    for l in range(DEPTH):
        mod = jax.nn.silu(c) @ w_ada[l] + b_ada[l]
        sh1, sc1, g1, sh2, sc2, g2 = jnp.split(mod[:, None, :], 6, axis=-1)
        h = x * (1.0 + sc1) + sh1
        y = token_mixer(h, w_in[l], conv_w[l], a_log[l], dt_bias[l], gdn_norm_w[l], rel_bias,
                        w_proj_moba[l], w_proj_gdn[l], w_out[l])
        x = layer_norm(DEEPNORM_ALPHA * x + g1 * y, ln1_g[l], ln1_b[l])
        h = x * (1.0 + sc2) + sh2
        y = swiglu(h, w_ffn_in[l], w_ffn_out[l])
        x = layer_norm(DEEPNORM_ALPHA * x + g2 * y, ln2_g[l], ln2_b[l])
    return x
```

```python
import math
from contextlib import ExitStack

import numpy as np
import concourse.bass as bass
import concourse.mybir as mybir
from concourse.bass_utils import run_bass_kernel_spmd

F32 = mybir.dt.float32
BF16 = mybir.dt.bfloat16
AF = mybir.ActivationFunctionType
ALU = mybir.AluOpType
AX = mybir.AxisListType

D = 2048
S = 2048
NT = 16
KC = 16
NH_M = 8
HV = 16
DFF = 5632
FC = DFF // 128
N_IN = 13344
ALPHA = 2.0 ** 0.25
LN_EPS = 1e-5
RMS_EPS = 1e-6
SCALE_M = 128 ** -0.5

C_ID, C_TRI, C_AST, C_SUBD, C_SLBD, C_M1, C_M2, C_UI, C_ONES = range(9)
NCONST = 9


class Reg:
    __slots__ = ("w", "r")

    def __init__(self):
        self.w = None
        self.r = {}


class KB:
    def __init__(self, nc, es):
        self.nc = nc
        self.engs = {"pe": nc.tensor, "act": nc.scalar, "dve": nc.vector, "pool": nc.gpsimd, "sp": nc.sync}
        self.semobj = {}
        self.cnt = {}
        self.waited = {}
        for name in self.engs:
            self.semobj[name] = es.enter_context(nc.semaphore("s_" + name))
            self.cnt[name] = 0
            self.waited[name] = {}
        self.nd = 12
        self.dnext = {"sp": 0, "pool": 0}
        self.dcnt = {}
        for q in ("sp", "pool"):
            for k in range(self.nd):
                key = ("d", q, k)
                self.semobj[key] = es.enter_context(nc.semaphore("d_%s%d" % (q, k)))
                self.dcnt[key] = 0
        self.n_wait = 0
        self.n_inst = 0

    def _collect(self, reads, writes):
        deps = {}
        for r in reads:
            if r.w is not None:
                k, v = r.w
                if deps.get(k, 0) < v:
                    deps[k] = v
        for w in writes:
            if w.w is not None:
                k, v = w.w
                if deps.get(k, 0) < v:
                    deps[k] = v
            for k, v in w.r.items():
                if deps.get(k, 0) < v:
                    deps[k] = v
        return deps

    def _wait(self, eng, deps, attach=False):
        wd = self.waited[eng]
        need = []
        for k, v in deps.items():
            if eng == "pe" and k == "pe":
                continue
            if wd.get(k, 0) < v:
                need.append((k, v))
                wd[k] = v
        pend = None
        if attach and need:
            pend = need.pop()
        for k, v in need:
            self.engs[eng].wait_ge(self.semobj[k], v)
            self.n_wait += 1
        return pend

    def _update(self, tok, reads, writes):
        k, v = tok
        for w in writes:
            w.w = tok
            w.r = {}
        for r in reads:
            if r.r.get(k, 0) < v:
                r.r[k] = v

    def op(self, eng, fn, reads=(), writes=()):
        pend = self._wait(eng, self._collect(reads, writes), attach=True)
        inst = fn(self.engs[eng])
        if pend is not None:
            inst._wait_ge(self.semobj[pend[0]], pend[1])
        inst.then_inc(self.semobj[eng], 1)
        self.cnt[eng] += 1
        self.n_inst += 1
        self._update((eng, self.cnt[eng]), reads, writes)

    def dma(self, q, out, in_, reads=(), writes=()):
        k = self.dnext[q]
        self.dnext[q] = (k + 1) % self.nd
        key = ("d", q, k)
        deps = self._collect(reads, writes)
        if self.dcnt[key] > 0:
            deps[key] = max(deps.get(key, 0), 16 * self.dcnt[key])
        pend = self._wait(q, deps, attach=True)
        inst = self.engs[q].dma_start(out=out, in_=in_)
        if pend is not None:
            inst._wait_ge(self.semobj[pend[0]], pend[1])
        inst.then_inc(self.semobj[key], 16)
        self.dcnt[key] += 1
        self.n_inst += 1
        tok = (key, 16 * self.dcnt[key])
        self._update(tok, reads, writes)
        return tok

    def barrier(self):
        deps = {}
        for name in self.engs:
            if self.cnt[name] > 0:
                deps[name] = self.cnt[name]
        for key, n in self.dcnt.items():
            if n > 0:
                deps[key] = 16 * n
        for eng in self.engs:
            self._wait(eng, dict(deps))

    def wait_tok(self, eng, tok):
        self._wait(eng, {tok[0]: tok[1]})


class T:
    def __init__(self, t, nreg=1):
        self.t = t
        self.regs = [Reg() for _ in range(nreg)]

    @property
    def r(self):
        return self.regs[0]


class View:
    def __init__(self, ap, reg):
        self.t = ap
        self.regs = [reg]

    @property
    def r(self):
        return self.regs[0]


def build_nc(stop_after=None, dbg=False):
    nc = bass.Bass("TRN2", target_bir_lowering=False)

    def din(name, shape, dt=F32):
        return nc.dram_tensor(name, list(shape), dt, kind="ExternalInput").ap()

    xT_d = din("xT", [KC, 128, S])
    x_d = din("x", [NT, 128, D])
    c_d = din("c_lay", [128, KC])
    wada_d = din("wada_lay", [24, 128, 4 * KC * 128])
    bada_d = din("bada_lay", [128, 96])
    consts_d = din("consts", [128, NCONST, 128])
    wmoba_d = din("wmoba_lay", [NH_M, 128, KC * 384])
    mbias_d = din("mbias_lay", [128, 16, 128])
    c31_d = din("c31_rep", [128, NH_M])
    wgdn_d = din("wgdn_lay", [8, 128, KC * 768])
    wab_d = din("wab_lay", [128, KC * 32])
    conv_d = din("conv_lay", [128, 32 * 4])
    alog_d = din("alog_rep", [128, HV])
    dtb_d = din("dtb_rep", [128, HV])
    normw_d = din("normw_rep", [128, 128])
    wgate_d = din("wgate_lay", [16, 128, KC * 256])
    wpm_d = din("wpm_lay", [16, 128, 8 * 128])
    wpg_d = din("wpg_lay", [16, 128, 16 * 128])
    wout_d = din("wout_lay", [4, 128, KC * 512])
    lnrep_d = din("ln_rep", [4, 128, D])
    wffi_d = din("wffi_lay", [FC, 128, KC * 256])
    wffo_d = din("wffo_lay", [16, 128, FC * 128])

    out_d = nc.dram_tensor("out", [NT, 128, D], F32, kind="ExternalOutput").ap()

    def dscr(name, shape, dt):
        return nc.dram_tensor(name, list(shape), dt, kind="Internal").ap()

    yaT_d = dscr("yaT_s", [NH_M, 128, S], BF16)
    ybT_d = dscr("ybT_s", [HV, 128, S], BF16)
    sgT_d = dscr("sgT_s", [32, 128, S], BF16)
    mgT_d = dscr("mgT_s", [KC, 128, S], BF16)
    x1_d = dscr("x1_s", [NT, 128, D], F32)
    y2_d = dscr("y2_s", [NT, 128, D], F32)
    gq_d = dscr("gq_s", [8, 128, S], BF16)
    gk_d = dscr("gk_s", [8, 128, S], BF16)
    gkt_d = dscr("gkt_s", [8, 128, NT * 128], BF16)
    gvt_d = dscr("gvt_s", [8, 128, NT * 256], BF16)
    gzs_d = dscr("gzs_s", [8, 128, NT * 256], BF16)
    dbg_outs = {}

    with ExitStack() as es:
        kb = KB(nc, es)

        def sb(name, shape, dt=F32, nreg=1):
            return T(es.enter_context(nc.sbuf_tensor(name, list(shape), dt)), nreg)

        banks = [T(es.enter_context(nc.psum_tensor("ps%d" % i, [128, 512], F32))) for i in range(8)]

        cst = sb("cst", [128, NCONST, 128])
        kb.dma("sp", cst.t[:], consts_d, writes=[cst.r])
        cstb = sb("cstb", [128, NCONST, 128], BF16)
        kb.op("dve", lambda e: e.tensor_copy(out=cstb.t[:], in_=cst.t[:]), reads=[cst.r], writes=[cstb.r])

        epsr = sb("epsr", [128, 2])
        kb.op("dve", lambda e: e.memset(epsr.t[:, 0:1], RMS_EPS), writes=[epsr.r])
        kb.op("dve", lambda e: e.memset(epsr.t[:, 1:2], LN_EPS), writes=[epsr.r])

        def CF(i):
            return cst.t[:, i, :]

        def CB(i):
            return cstb.t[:, i, :]

        c_sb = sb("c_sb", [128, KC])
        sc_bf = sb("sc_bf", [128, KC], BF16)
        kb.dma("sp", c_sb.t[:], c_d, writes=[c_sb.r])
        kb.op("act", lambda e: e.activation(out=sc_bf.t[:], in_=c_sb.t[:], func=AF.Silu),
              reads=[c_sb.r], writes=[sc_bf.r])
        bada = sb("bada", [128, 96])
        kb.dma("sp", bada.t[:], bada_d, writes=[bada.r])
        modT = sb("modT", [128, 96])
        hT = sb("hT", [128, KC, S], BF16, nreg=KC)

        es0 = ExitStack()

        def sb0(name, shape, dt=F32, nreg=1):
            return T(es0.enter_context(nc.sbuf_tensor(name, list(shape), dt)), nreg)
        wab = [sb0("wada%d" % i, [128, 4, KC, 128], BF16) for i in range(2)]
        xb = [sb0("xTb%d" % i, [128, S]) for i in range(2)]

        def mod_group(g, pm, col0):
            wb = wab[g % 2]
            kb.dma("pool", wb.t[:].rearrange("p a k c -> p (a k c)"), wada_d[g], writes=[wb.r])
            for a in range(4):
                for k in range(KC):
                    kb.op("pe", lambda e, a=a, k=k: e.matmul(
                        pm.t[:, col0 + a:col0 + a + 1], wb.t[:, a, k, :], sc_bf.t[:, k:k + 1],
                        start=(k == 0), stop=(k == KC - 1)),
                        reads=[wb.r, sc_bf.r], writes=[pm.r])
            kb.op("dve", lambda e: e.tensor_tensor(out=modT.t[:, 4 * g:4 * g + 4], in0=pm.t[:, col0:col0 + 4],
                                                   in1=bada.t[:, 4 * g:4 * g + 4], op=ALU.add),
                  reads=[bada.r], writes=[modT.r, pm.r])
        for g in range(8):
            mod_group(g, banks[0], 4 * g)
        kb.op("dve", lambda e: e.tensor_scalar_add(out=modT.t[:, 16:32], in0=modT.t[:, 16:32], scalar1=1.0),
              reads=[modT.r], writes=[modT.r])
        for k in range(KC):
            b_ = xb[k % 2]
            kb.dma("sp", b_.t[:], xT_d[k], writes=[b_.r])
            kb.op("act", lambda e, k=k, b_=b_: e.activation(
                out=hT.t[:, k, :], in_=b_.t[:], func=AF.Identity,
                scale=modT.t[:, 16 + k:17 + k], bias=modT.t[:, k:k + 1]),
                reads=[b_.r, modT.r], writes=[hT.regs[k]])

        if dbg:
            dbg_outs["modT"] = nc.dram_tensor("dbg_modT", [128, 96], F32, kind="ExternalOutput").ap()
            kb.dma("sp", dbg_outs["modT"], modT.t[:], reads=[modT.r])

        hT_all = hT.regs
        final_toks = []

        if stop_after != "p0":
            with ExitStack() as es1:
                def sb1(name, shape, dt=F32, nreg=1):
                    return T(es1.enter_context(nc.sbuf_tensor(name, list(shape), dt)), nreg)
                wm = [sb1("wm%d" % i, [128, KC, 384], BF16) for i in range(2)]
                qT = sb1("qT", [128, S], BF16, nreg=4)
                kT = sb1("kT", [128, S], BF16, nreg=4)
                V1 = sb1("V1", [128, NT, 129], BF16, nreg=5)
                kmf = sb1("kmf", [128, 8])
                kmT = sb1("kmT", [128, 8], BF16)
                PT = [sb1("PT%d" % i, [128, 256], BF16) for i in range(4)]
                tmpE = [sb1("tmpE%d" % i, [128, 256], BF16) for i in range(2)]
                acc = [sb1("acc%d" % i, [128, 129]) for i in range(2)]
                rt = [sb1("rt%d" % i, [128, 8]) for i in range(2)]
                cmpb = [sb1("cmp%d" % i, [128, 8, 8]) for i in range(2)]
                rank = [sb1("rank%d" % i, [128, 8]) for i in range(2)]
                sel = [sb1("sel%d" % i, [128, 8]) for i in range(2)]
                rec = [sb1("rec%d" % i, [128, 1]) for i in range(2)]
                ya = [sb1("ya%d" % i, [128, 128], BF16) for i in range(2)]
                yaT = [sb1("yaT%d" % i, [128, S], BF16) for i in range(2)]
                mb = sb1("mb", [128, 16, 128])
                mbe = sb1("mbe", [128, 16, 128])
                Ed = sb1("Ed", [128, 8, 128], BF16)
                Eo = sb1("Eo", [128, 8, 128], BF16)
                c31 = sb1("c31", [128, NH_M])

                kb.dma("sp", mb.t[:], mbias_d, writes=[mb.r])
                kb.dma("sp", c31.t[:], c31_d, writes=[c31.r])
                kb.op("act", lambda e: e.activation(out=mbe.t[:], in_=mb.t[:], func=AF.Exp),
                      reads=[mb.r], writes=[mbe.r])
                kb.op("dve", lambda e: e.tensor_tensor(
                    out=Ed.t[:], in0=mbe.t[:, 0:8, :],
                    in1=cst.t[:, C_UI:C_UI + 1, :].broadcast_to([128, 8, 128]), op=ALU.mult),
                    reads=[mbe.r, cst.r], writes=[Ed.r])
                kb.op("dve", lambda e: e.tensor_copy(out=Eo.t[:], in_=mbe.t[:, 8:16, :]),
                      reads=[mbe.r], writes=[Eo.r])
                kb.op("dve", lambda e: e.memset(V1.t[:, :, 128:129], 1.0), writes=[V1.regs[4]])

                kb.dma("pool", wm[0].t[:].rearrange("p k c -> p (k c)"), wmoba_d[0], writes=[wm[0].r])
                s_slot = 0
                o_slot = 0
                p_slot = 0
                for h in range(NH_M):
                    w = wm[h % 2]
                    if h + 1 < NH_M:
                        wn = wm[(h + 1) % 2]
                        kb.dma("pool", wn.t[:].rearrange("p k c -> p (k c)"), wmoba_d[h + 1], writes=[wn.r])
                    for which, dst in ((0, qT), (1, kT)):
                        for g in range(4):
                            pb = banks[g % 2]
                            for k in range(KC):
                                kb.op("pe", lambda e, k=k, g=g, pb=pb, which=which: e.matmul(
                                    pb.t[:, :], w.t[:, k, which * 128:(which + 1) * 128],
                                    hT.t[:, k, g * 512:(g + 1) * 512], start=(k == 0), stop=(k == KC - 1)),
                                    reads=[w.r, hT_all[k]], writes=[pb.r])
                            kb.op("act", lambda e, g=g, pb=pb, dst=dst: e.copy(
                                out=dst.t[:, g * 512:(g + 1) * 512], in_=pb.t[:, :]),
                                writes=[dst.regs[g], pb.r])
                    for g in range(4):
                        pb = banks[g % 2]
                        for tt in range(4):
                            t = g * 4 + tt
                            for k in range(KC):
                                kb.op("pe", lambda e, k=k, t=t, tt=tt, pb=pb: e.matmul(
                                    pb.t[:, tt * 128:(tt + 1) * 128], hT.t[:, k, t * 128:(t + 1) * 128],
                                    w.t[:, k, 256:384], start=(k == 0), stop=(k == KC - 1)),
                                    reads=[w.r, hT_all[k]], writes=[pb.r])
                        kb.op("dve", lambda e, g=g, pb=pb: e.tensor_copy(
                            out=V1.t[:, g * 4:(g + 1) * 4, 0:128],
                            in_=pb.t[:, :].rearrange("p (a c) -> p a c", c=128)),
                            writes=[V1.regs[g], pb.r])
                    kb.op("dve", lambda e: e.tensor_reduce(
                        out=kmf.t[:], in_=kT.t[:].rearrange("p (n b) -> p n b", b=256), axis=AX.X, op=ALU.add),
                        reads=kT.regs, writes=[kmf.r])
                    kb.op("dve", lambda e: e.tensor_scalar_mul(out=kmT.t[:], in0=kmf.t[:], scalar1=1.0 / 256.0),
                          reads=[kmf.r], writes=[kmT.r])
                    yT = yaT[h % 2]
                    for qb in range(8):
                        q0 = qb * 256
                        if qb >= 4:
                            for qi in range(2):
                                rp = banks[7]
                                kb.op("pe", lambda e, qi=qi: e.matmul(
                                    rp.t[:, qi * 8:qi * 8 + 8], qT.t[:, q0 + qi * 128:q0 + (qi + 1) * 128],
                                    kmT.t[:, :], start=True, stop=True),
                                    reads=[qT.regs[qb // 2], kmT.r], writes=[rp.r])
                                kb.op("dve", lambda e, qi=qi: e.tensor_copy(out=rt[qi].t[:], in_=rp.t[:, qi * 8:qi * 8 + 8]),
                                      writes=[rt[qi].r, rp.r])
                                kb.op("dve", lambda e, qi=qi: e.tensor_tensor(
                                    out=cmpb[qi].t[:, 0:qb, 0:qb],
                                    in0=rt[qi].t[:, 0:qb].unsqueeze(1).broadcast_to([128, qb, qb]),
                                    in1=rt[qi].t[:, 0:qb].unsqueeze(2).broadcast_to([128, qb, qb]),
                                    op=ALU.is_gt), reads=[rt[qi].r], writes=[cmpb[qi].r])
                                kb.op("dve", lambda e, qi=qi: e.tensor_reduce(
                                    out=rank[qi].t[:, 0:qb], in_=cmpb[qi].t[:, 0:qb, 0:qb], axis=AX.X, op=ALU.add),
                                    reads=[cmpb[qi].r], writes=[rank[qi].r])
                                kb.op("dve", lambda e, qi=qi: e.tensor_single_scalar(
                                    out=sel[qi].t[:, 0:qb], in_=rank[qi].t[:, 0:qb], scalar=2.5, op=ALU.is_lt),
                                    reads=[rank[qi].r], writes=[sel[qi].r])
                        order = [qb] + list(range(qb))
                        staged = []

                        def emit_scores(n):
                            nonlocal s_slot, p_slot
                            res = []
                            for kt in (2 * n, 2 * n + 1):
                                sbank = banks[s_slot % 4]
                                sreg = sbank.r
                                soff = 0
                                s_slot += 1
                                pt = PT[p_slot % 4]
                                p_slot += 1
                                te = tmpE[p_slot % 2]
                                ksl = slice(kt * 128, (kt + 1) * 128)
                                kreg = kT.regs[kt // 4]
                                qreg = qT.regs[qb // 2]
                                if n < qb:
                                    kb.op("pe", lambda e, ksl=ksl, soff=soff, sbank=sbank: e.matmul(
                                        sbank.t[:, soff:soff + 256], kT.t[:, ksl], qT.t[:, q0:q0 + 256],
                                        start=True, stop=True), reads=[kreg, qreg], writes=[sreg])
                                    if kt == 2 * qb - 1:
                                        kb.op("act", lambda e, soff=soff, sbank=sbank, te=te: e.activation(
                                            out=te.t[:, 0:128], in_=sbank.t[:, soff:soff + 128], func=AF.Exp,
                                            scale=SCALE_M), writes=[te.r, sreg])
                                        kb.op("dve", lambda e, te=te, pt=pt: e.tensor_tensor(
                                            out=pt.t[:, 0:128], in0=te.t[:, 0:128], in1=Eo.t[:, h, :], op=ALU.mult),
                                            reads=[te.r, Eo.r], writes=[pt.r])
                                        kb.op("act", lambda e, soff=soff, sbank=sbank, pt=pt: e.activation(
                                            out=pt.t[:, 128:256], in_=sbank.t[:, soff + 128:soff + 256], func=AF.Exp,
                                            scale=SCALE_M, bias=c31.t[:, h:h + 1]), reads=[c31.r], writes=[pt.r, sreg])
                                    else:
                                        kb.op("act", lambda e, soff=soff, sbank=sbank, pt=pt: e.activation(
                                            out=pt.t[:, :], in_=sbank.t[:, soff:soff + 256], func=AF.Exp,
                                            scale=SCALE_M, bias=c31.t[:, h:h + 1]), reads=[c31.r], writes=[pt.r, sreg])
                                elif kt == 2 * qb:
                                    kb.op("pe", lambda e, ksl=ksl, soff=soff, sbank=sbank: e.matmul(
                                        sbank.t[:, soff:soff + 256], kT.t[:, ksl], qT.t[:, q0:q0 + 256],
                                        start=True, stop=True), reads=[kreg, qreg], writes=[sreg])
                                    kb.op("act", lambda e, soff=soff, sbank=sbank, te=te: e.activation(
                                        out=te.t[:, :], in_=sbank.t[:, soff:soff + 256], func=AF.Exp,
                                        scale=SCALE_M), writes=[te.r, sreg])
                                    kb.op("dve", lambda e, te=te, pt=pt: e.tensor_tensor(
                                        out=pt.t[:, 0:128], in0=te.t[:, 0:128], in1=Ed.t[:, h, :], op=ALU.mult),
                                        reads=[te.r, Ed.r], writes=[pt.r])
                                    kb.op("dve", lambda e, te=te, pt=pt: e.tensor_tensor(
                                        out=pt.t[:, 128:256], in0=te.t[:, 128:256], in1=Eo.t[:, h, :], op=ALU.mult),
                                        reads=[te.r, Eo.r], writes=[pt.r])
                                else:
                                    kb.op("pe", lambda e, ksl=ksl, soff=soff, sbank=sbank: e.matmul(
                                        sbank.t[:, soff + 128:soff + 256], kT.t[:, ksl], qT.t[:, q0 + 128:q0 + 256],
                                        start=True, stop=True), reads=[kreg, qreg], writes=[sreg])
                                    kb.op("act", lambda e, soff=soff, sbank=sbank, te=te: e.activation(
                                        out=te.t[:, 128:256], in_=sbank.t[:, soff + 128:soff + 256], func=AF.Exp,
                                        scale=SCALE_M), writes=[te.r, sreg])
                                    kb.op("dve", lambda e, te=te, pt=pt: e.tensor_tensor(
                                        out=pt.t[:, 128:256], in0=te.t[:, 128:256], in1=Ed.t[:, h, :], op=ALU.mult),
                                        reads=[te.r, Ed.r], writes=[pt.r])
                                res.append((kt, pt))
                            return res

                        def emit_pv(n, pts):
                            nonlocal o_slot
                            for qi in range(2):
                                obank = banks[4 + o_slot % 3]
                                oreg = obank.r
                                ooff = 0
                                o_slot += 1
                                use = [(kt, pt) for (kt, pt) in pts if kt <= 2 * qb + qi]
                                for i, (kt, pt) in enumerate(use):
                                    kb.op("pe", lambda e, kt=kt, pt=pt, i=i, qi=qi, ooff=ooff, obank=obank, use=use: e.matmul(
                                        obank.t[:, ooff:ooff + 129], pt.t[:, qi * 128:(qi + 1) * 128], V1.t[:, kt, :],
                                        start=(i == 0), stop=(i == len(use) - 1)),
                                        reads=[pt.r, V1.regs[kt // 4], V1.regs[4]], writes=[oreg])
                                if n == qb:
                                    kb.op("act", lambda e, qi=qi, ooff=ooff, obank=obank: e.copy(
                                        out=acc[qi].t[:, :], in_=obank.t[:, ooff:ooff + 129]),
                                        writes=[acc[qi].r, oreg])
                                elif qb >= 4:
                                    kb.op("dve", lambda e, qi=qi, ooff=ooff, obank=obank, n=n: e.scalar_tensor_tensor(
                                        out=acc[qi].t[:, :], in0=obank.t[:, ooff:ooff + 129], scalar=sel[qi].t[:, n:n + 1],
                                        in1=acc[qi].t[:, :], op0=ALU.mult, op1=ALU.add),
                                        reads=[sel[qi].r], writes=[acc[qi].r, oreg])
                                else:
                                    kb.op("dve", lambda e, qi=qi, ooff=ooff, obank=obank: e.tensor_tensor(
                                        out=acc[qi].t[:, :], in0=obank.t[:, ooff:ooff + 129], in1=acc[qi].t[:, :], op=ALU.add),
                                        writes=[acc[qi].r, oreg])

                        prev = None
                        for n in order:
                            cur = (n, emit_scores(n))
                            if prev is not None:
                                emit_pv(*prev)
                            prev = cur
                        emit_pv(*prev)
                        for qi in range(2):
                            qt = 2 * qb + qi
                            kb.op("dve", lambda e, qi=qi: e.reciprocal(out=rec[qi].t[:], in_=acc[qi].t[:, 128:129]),
                                  reads=[acc[qi].r], writes=[rec[qi].r])
                            kb.op("dve", lambda e, qi=qi: e.tensor_scalar_mul(
                                out=ya[qi].t[:], in0=acc[qi].t[:, 0:128], scalar1=rec[qi].t[:, 0:1]),
                                reads=[acc[qi].r, rec[qi].r], writes=[ya[qi].r])
                            tb = banks[7]
                            kb.op("pe", lambda e, qi=qi: e.matmul(
                                tb.t[:, qi * 128:(qi + 1) * 128], ya[qi].t[:], CB(C_ID), start=True, stop=True),
                                reads=[ya[qi].r, cstb.r], writes=[tb.r])
                            kb.op("act", lambda e, qi=qi, qt=qt: e.copy(
                                out=yT.t[:, qt * 128:(qt + 1) * 128], in_=tb.t[:, qi * 128:(qi + 1) * 128]),
                                writes=[yT.r, tb.r])
                    kb.dma("sp", yaT_d[h], yT.t[:], reads=[yT.r])
                    mod_group(8 + 2 * h, banks[7], 384)
                    mod_group(9 + 2 * h, banks[7], 384)
                kb.op("dve", lambda e: e.tensor_scalar_add(out=modT.t[:, 64:80], in0=modT.t[:, 64:80], scalar1=1.0),
                      reads=[modT.r], writes=[modT.r])
        es0.close()


        kb.barrier()
        bankctr = [0]

        def nb():
            b = banks[bankctr[0] % 8]
            bankctr[0] += 1
            return b

        if stop_after not in ("p0", "p1a"):
            with ExitStack() as es2:
                def sb2(name, shape, dt=F32, nreg=1):
                    return T(es2.enter_context(nc.sbuf_tensor(name, list(shape), dt)), nreg)
                print("sbuf remaining at GDN start", nc.sbuf_bytes_remaining)
                beta = sb2("beta", [128, NT, HV])
                lb = sb2("lb", [128, NT, HV])
                gg = sb2("gg", [128, NT, HV])
                nea = sb2("nea", [128, HV])
                egc = sb2("egc", [128, NT, HV])
                cdd = sb2("cdd", [128, NT, HV])
                ekd = sb2("ekd", [128, NT, HV])
                bkk = sb2("bkk", [128, NT, HV])
                ghl = sb2("ghl", [128, 2, NT, HV])
                lhl = sb2("lhl", [128, 2, NT, HV])
                cw = sb2("cw", [128, 32, 4])
                alog = sb2("alog", [128, HV])
                dtb = sb2("dtb", [128, HV])
                normw = sb2("normw", [128, 128])
                es2a = ExitStack()

                def sb2a(name, shape, dt=F32, nreg=1):
                    return T(es2a.enter_context(nc.sbuf_tensor(name, list(shape), dt)), nreg)
                wabt = sb2a("wabt", [128, KC, 32], BF16)
                kb.dma("pool", wabt.t[:].rearrange("p k c -> p (k c)"), wab_d, writes=[wabt.r])
                kb.dma("sp", cw.t[:].rearrange("p g t -> p (g t)"), conv_d, writes=[cw.r])
                kb.dma("sp", alog.t[:], alog_d, writes=[alog.r])
                kb.dma("sp", dtb.t[:], dtb_d, writes=[dtb.r])
                kb.dma("sp", normw.t[:], normw_d, writes=[normw.r])
                ab = sb2a("ab", [128, NT, 32])
                bk_ = nb()
                for t in range(NT):
                    for k in range(KC):
                        kb.op("pe", lambda e, t=t, k=k: e.matmul(
                            bk_.t[:, t * 32:(t + 1) * 32], hT.t[:, k, t * 128:(t + 1) * 128], wabt.t[:, k, :],
                            start=(k == 0), stop=(k == KC - 1)), reads=[wabt.r, hT_all[k]], writes=[bk_.r])
                kb.op("dve", lambda e: e.tensor_copy(out=ab.t[:], in_=bk_.t[:, :].rearrange("p (t c) -> p t c", c=32)),
                      writes=[ab.r, bk_.r])
                tmpa = sb2a("tmpa", [128, NT, HV])
                gcs = sb2a("gcs", [128, NT, 32])
                kb.op("act", lambda e: e.activation(out=beta.t[:], in_=ab.t[:, :, 0:16], func=AF.Sigmoid),
                      reads=[ab.r], writes=[beta.r])
                kb.op("act", lambda e: e.activation(out=lb.t[:], in_=beta.t[:], func=AF.Ln),
                      reads=[beta.r], writes=[lb.r])
                kb.op("dve", lambda e: e.tensor_tensor(
                    out=tmpa.t[:], in0=ab.t[:, :, 16:32], in1=dtb.t[:].unsqueeze(1).broadcast_to([128, NT, HV]),
                    op=ALU.add), reads=[ab.r, dtb.r], writes=[tmpa.r])
                kb.op("act", lambda e: e.activation(out=tmpa.t[:], in_=tmpa.t[:], func=AF.Exp),
                      reads=[tmpa.r], writes=[tmpa.r])
                kb.op("act", lambda e: e.activation(out=tmpa.t[:], in_=tmpa.t[:], func=AF.Ln, bias=1.0),
                      reads=[tmpa.r], writes=[tmpa.r])
                kb.op("act", lambda e: e.activation(out=nea.t[:], in_=alog.t[:], func=AF.Exp),
                      reads=[alog.r], writes=[nea.r])
                kb.op("dve", lambda e: e.scalar_tensor_tensor(
                    out=gg.t[:], in0=tmpa.t[:], scalar=-1.0, in1=nea.t[:].unsqueeze(1).broadcast_to([128, NT, HV]),
                    op0=ALU.mult, op1=ALU.mult), reads=[tmpa.r, nea.r], writes=[gg.r])
                bk_ = nb()
                for t in range(NT):
                    kb.op("pe", lambda e, t=t: e.matmul(bk_.t[:, t * 32:t * 32 + 16], CF(C_TRI), gg.t[:, t, :],
                                                        start=True, stop=True), reads=[cst.r, gg.r], writes=[bk_.r])
                    kb.op("pe", lambda e, t=t: e.matmul(bk_.t[:, t * 32 + 16:t * 32 + 32], CF(C_ONES), gg.t[:, t, :],
                                                        start=True, stop=True), reads=[cst.r, gg.r], writes=[bk_.r])
                kb.op("dve", lambda e: e.tensor_copy(out=gcs.t[:], in_=bk_.t[:, :].rearrange("p (t c) -> p t c", c=32)),
                      writes=[gcs.r, bk_.r])
                kb.op("act", lambda e: e.activation(out=egc.t[:], in_=gcs.t[:, :, 0:16], func=AF.Exp),
                      reads=[gcs.r], writes=[egc.r])
                kb.op("act", lambda e: e.activation(out=cdd.t[:], in_=gcs.t[:, :, 16:32], func=AF.Exp),
                      reads=[gcs.r], writes=[cdd.r])
                kb.op("dve", lambda e: e.tensor_tensor(out=ekd.t[:], in0=gcs.t[:, :, 16:32], in1=gcs.t[:, :, 0:16],
                                                       op=ALU.subtract), reads=[gcs.r], writes=[ekd.r])
                kb.op("act", lambda e: e.activation(out=ekd.t[:], in_=ekd.t[:], func=AF.Exp),
                      reads=[ekd.r], writes=[ekd.r])
                kb.op("dve", lambda e: e.tensor_tensor(out=bkk.t[:], in0=beta.t[:], in1=egc.t[:], op=ALU.mult),
                      reads=[beta.r, egc.r], writes=[bkk.r])

                tmpb = sb2a("tmpb", [128, NT, HV], BF16)
                for src, dst in ((gg, ghl), (lb, lhl)):
                    kb.op("dve", lambda e, src=src: e.tensor_copy(out=tmpb.t[:], in_=src.t[:]), reads=[src.r], writes=[tmpb.r])
                    kb.op("dve", lambda e, dst=dst: e.tensor_copy(out=dst.t[:, 0, :, :], in_=tmpb.t[:]), reads=[tmpb.r], writes=[dst.r])
                    kb.op("dve", lambda e, src=src, dst=dst: e.tensor_tensor(out=dst.t[:, 1, :, :], in0=src.t[:], in1=dst.t[:, 0, :, :],
                                                                             op=ALU.subtract), reads=[src.r], writes=[dst.r])
                kb.barrier()
                es2a.close()
                es2b = ExitStack()

                def sb2b(name, shape, dt=F32, nreg=1):
                    return T(es2b.enter_context(nc.sbuf_tensor(name, list(shape), dt)), nreg)
                wgb = [sb2b("wg%d" % i, [128, KC, 768], BF16) for i in range(2)]
                xpre = sb2b("xpre", [128, 4, 3 + S], BF16, nreg=4)
                kb.op("dve", lambda e: e.memset(xpre.t[:, :, 0:3], 0.0), writes=xpre.regs)
                dg = sb2b("dg", [128, 16, 128], BF16, nreg=16)
                qTg = sb2b("qTg", [128, S], BF16)
                kTg = sb2b("kTg", [128, S], BF16)
                ktok = sb2b("ktok", [128, NT, 128], BF16)
                vtok = sb2b("vtok", [128, NT, 256], BF16)
                zs = sb2b("zs", [128, NT, 256], BF16)
                cs = [sb2b("cs%d" % i, [128, 512]) for i in range(2)]
                sq = sb2b("sq", [128, 512], BF16)
                rinv = sb2b("rinv", [128, 512])
                vTt = [sb2b("vTt%d" % i, [128, 512], BF16) for i in range(2)]
                zt = [sb2b("zt%d" % i, [128, 512]) for i in range(2)]

                kb.dma("pool", wgb[0].t[:].rearrange("p k c -> p (k c)"), wgdn_d[0], writes=[wgb[0].r])
                ev = [0]

                def evac(out, in_, reads, writes):
                    ev[0] += 1
                    if ev[0] % 2:
                        kb.op("act", lambda e: e.copy(out=out, in_=in_), reads=reads, writes=writes)
                    else:
                        kb.op("dve", lambda e: e.tensor_copy(out=out, in_=in_), reads=reads, writes=writes)

                for j in range(8):
                    w = wgb[j % 2]
                    if j + 1 < 8:
                        wn_ = wgb[(j + 1) % 2]
                        kb.dma("pool", wn_.t[:].rearrange("p k c -> p (k c)"), wgdn_d[j + 1], writes=[wn_.r])
                    grps = [j, 8 + j, 16 + 2 * j, 17 + 2 * j]
                    for gi in range(4):
                        for tap in range(4):
                            kb.op("dve", lambda e, gi=gi, tap=tap: e.tensor_scalar_mul(
                                out=dg.t[:, gi * 4 + tap, :], in0=CF(C_ID), scalar1=cw.t[:, grps[gi], tap:tap + 1]),
                                reads=[cst.r, cw.r], writes=[dg.regs[gi * 4 + tap]])
                    for gi in range(4):
                        for g in range(4):
                            pb = nb()
                            for k in range(KC):
                                kb.op("pe", lambda e, k=k, g=g, gi=gi, pb=pb: e.matmul(
                                    pb.t[:, :], w.t[:, k, gi * 128:(gi + 1) * 128], hT.t[:, k, g * 512:(g + 1) * 512],
                                    start=(k == 0), stop=(k == KC - 1)), reads=[w.r, hT_all[k]], writes=[pb.r])
                            evac(xpre.t[:, gi, 3 + g * 512:3 + (g + 1) * 512], pb.t[:, :], [], [xpre.regs[gi], pb.r])
                    def z_unit(t2, w=w):
                        pb = nb()
                        for a in range(2):
                            t = t2 * 2 + a
                            for k in range(KC):
                                kb.op("pe", lambda e, k=k, t=t, a=a, pb=pb: e.matmul(
                                    pb.t[:, a * 256:(a + 1) * 256], hT.t[:, k, t * 128:(t + 1) * 128], w.t[:, k, 512:768],
                                    start=(k == 0), stop=(k == KC - 1)), reads=[w.r, hT_all[k]], writes=[pb.r])
                        z_ = zt[t2 % 2]
                        kb.op("act", lambda e, pb=pb, z_=z_: e.activation(out=z_.t[:], in_=pb.t[:, :], func=AF.Silu),
                              writes=[z_.r, pb.r])
                        kb.op("dve", lambda e, z_=z_, t2=t2: e.tensor_tensor(
                            out=zs.t[:, t2 * 2:t2 * 2 + 2, :].rearrange("p a (h c) -> p (a h) c", c=128),
                            in0=z_.t[:].rearrange("p (a c) -> p a c", c=128),
                            in1=normw.t[:].unsqueeze(1).broadcast_to([128, 4, 128]), op=ALU.mult),
                            reads=[z_.r, normw.r], writes=[zs.r])
                    for gi in range(4):
                        for g in range(4):
                            pb = nb()
                            for tap in range(4):
                                kb.op("pe", lambda e, tap=tap, g=g, gi=gi, pb=pb: e.matmul(
                                    pb.t[:, :], dg.t[:, gi * 4 + tap, :], xpre.t[:, gi, g * 512 + tap:g * 512 + tap + 512],
                                    start=(tap == 0), stop=(tap == 3)), reads=[dg.regs[gi * 4 + tap], xpre.regs[gi]], writes=[pb.r])
                            if gi < 2:
                                c_ = cs[g % 2]
                                kb.op("act", lambda e, pb=pb, c_=c_: e.activation(out=c_.t[:], in_=pb.t[:, :], func=AF.Silu),
                                      writes=[c_.r, pb.r])
                                kb.op("dve", lambda e, c_=c_: e.tensor_tensor(out=sq.t[:], in0=c_.t[:], in1=c_.t[:], op=ALU.mult),
                                      reads=[c_.r], writes=[sq.r])
                                pb2 = nb()
                                kb.op("pe", lambda e, pb2=pb2: e.matmul(pb2.t[:, :], CB(C_ONES), sq.t[:], start=True, stop=True),
                                      reads=[cstb.r, sq.r], writes=[pb2.r])
                                kb.op("act", lambda e, pb2=pb2: e.activation(
                                    out=rinv.t[:], in_=pb2.t[:, :], func=AF.Ln, bias=epsr.t[:, 0:1]),
                                    reads=[epsr.r], writes=[rinv.r, pb2.r])
                                kb.op("act", lambda e: e.activation(out=rinv.t[:], in_=rinv.t[:], func=AF.Exp, scale=-0.5),
                                      reads=[rinv.r], writes=[rinv.r])
                                dstT = qTg if gi == 0 else kTg
                                scl = SCALE_M if gi == 0 else 1.0
                                kb.op("dve", lambda e, c_=c_, dstT=dstT, scl=scl, g=g: e.scalar_tensor_tensor(
                                    out=dstT.t[:, g * 512:(g + 1) * 512], in0=c_.t[:], scalar=scl, in1=rinv.t[:],
                                    op0=ALU.mult, op1=ALU.mult), reads=[c_.r, rinv.r], writes=[dstT.r])
                                if gi == 1:
                                    pb3 = nb()
                                    for tt in range(4):
                                        t = g * 4 + tt
                                        kb.op("pe", lambda e, tt=tt, t=t, pb3=pb3: e.matmul(
                                            pb3.t[:, tt * 128:(tt + 1) * 128], kTg.t[:, t * 128:(t + 1) * 128], CB(C_ID),
                                            start=True, stop=True), reads=[kTg.r, cstb.r], writes=[pb3.r])
                                    evac(ktok.t[:, g * 4:(g + 1) * 4, :], pb3.t[:, :].rearrange("p (a c) -> p a c", c=128),
                                         [], [ktok.r, pb3.r])
                            else:
                                vt_ = vTt[g % 2]
                                kb.op("act", lambda e, pb=pb, vt_=vt_: e.activation(out=vt_.t[:], in_=pb.t[:, :], func=AF.Silu),
                                      writes=[vt_.r, pb.r])
                                pb3 = nb()
                                for tt in range(4):
                                    kb.op("pe", lambda e, tt=tt, pb3=pb3, vt_=vt_: e.matmul(
                                        pb3.t[:, tt * 128:(tt + 1) * 128], vt_.t[:, tt * 128:(tt + 1) * 128], CB(C_ID),
                                        start=True, stop=True), reads=[vt_.r, cstb.r], writes=[pb3.r])
                                evac(vtok.t[:, g * 4:(g + 1) * 4, (gi - 2) * 128:(gi - 1) * 128],
                                     pb3.t[:, :].rearrange("p (a c) -> p a c", c=128), [], [vtok.r, pb3.r])

                            if (gi * 4 + g) % 2 == 1:
                                z_unit((gi * 4 + g) // 2)
                    kb.dma("sp", gq_d[j], qTg.t[:], reads=[qTg.r])
                    kb.dma("sp", gk_d[j], kTg.t[:], reads=[kTg.r])
                    kb.dma("sp", gkt_d[j], ktok.t[:].rearrange("p a c -> p (a c)"), reads=[ktok.r])
                    kb.dma("sp", gvt_d[j], vtok.t[:].rearrange("p a c -> p (a c)"), reads=[vtok.r])
                    kb.dma("sp", gzs_d[j], zs.t[:].rearrange("p a c -> p (a c)"), reads=[zs.r])

                kb.barrier()
                es2b.close()
                class HS_:
                    pass
                INB = []
                for i in range(3):
                    b_ = HS_()
                    b_.q = sb2("inq%d" % i, [128, 256], BF16)
                    b_.k = sb2("ink%d" % i, [128, 256], BF16)
                    b_.kt = sb2("inkt%d" % i, [128, 2, 128], BF16)
                    b_.vt = sb2("invt%d" % i, [128, 2, 256], BF16)
                    b_.zs = sb2("inzs%d" % i, [128, 2, 256], BF16)
                    INB.append(b_)

                def load_inputs(gcp):
                    j_, cp_ = divmod(gcp, 8)
                    b_ = INB[gcp % 3]
                    kb.dma("sp", b_.q.t[:], gq_d[j_][:, cp_ * 256:(cp_ + 1) * 256], writes=[b_.q.r])
                    kb.dma("sp", b_.k.t[:], gk_d[j_][:, cp_ * 256:(cp_ + 1) * 256], writes=[b_.k.r])
                    kb.dma("sp", b_.kt.t[:].rearrange("p a c -> p (a c)"), gkt_d[j_][:, cp_ * 256:(cp_ + 1) * 256], writes=[b_.kt.r])
                    kb.dma("sp", b_.vt.t[:].rearrange("p a c -> p (a c)"), gvt_d[j_][:, cp_ * 512:(cp_ + 1) * 512], writes=[b_.vt.r])
                    kb.dma("sp", b_.zs.t[:].rearrange("p a c -> p (a c)"), gzs_d[j_][:, cp_ * 512:(cp_ + 1) * 512], writes=[b_.zs.r])
                KK = [sb2("kkm%d" % i, [128, 4, 128]) for i in range(2)]
                QK = [sb2("qkm%d" % i, [128, 128]) for i in range(2)]

                class HS:
                    pass
                SLOT = []
                for sl in range(2):
                    o_ = HS()
                    n = "s%d_" % sl
                    o_.LAp = sb2(n + "LAp", [128, 2, 2, 256], BF16)
                    o_.LDp = sb2(n + "LDp", [128, 2, 2, 256], BF16)
                    o_.LEp = sb2(n + "LEp", [128, 2, 2, 128], BF16)
                    o_.LFp = sb2(n + "LFp", [128, 2, 256], BF16)
                    o_.LGp = sb2(n + "LGp", [128, 2, 2, 128], BF16)
                    o_.LHp = sb2(n + "LHp", [128, 2, 256], BF16)
                    o_.TTp = [sb2(n + "TTp%d" % i, [128, 2, 128], BF16) for i in range(2)]
                    SLOT.append(o_)
                IT = []
                for sl in range(2):
                    row = []
                    for vh in range(2):
                        o_ = HS()
                        n = "i%d%d_" % (sl, vh)
                        o_.Bh = sb2(n + "Bh", [128, 2, 128], BF16); o_.Dl = sb2(n + "Dl", [128, 2, 128], BF16)
                        o_.E3 = sb2(n + "E3", [128, 384])
                        o_.V32 = sb2(n + "V32", [128, 512])
                        o_.Vall = sb2(n + "Vall", [128, 2, 512], BF16)
                        SP_ = SLOT[sl]
                        o_.LA = View(SP_.LAp.t[:, :, vh, :], SP_.LAp.r); o_.LB = sb2(n + "LB", [128, 2, 384], BF16)
                        o_.LC = sb2(n + "LC", [128, 2, 384], BF16); o_.LD = View(SP_.LDp.t[:, :, vh, :], SP_.LDp.r)
                        o_.LE = View(SP_.LEp.t[:, :, vh, :], SP_.LEp.r); o_.LF = View(SP_.LFp.t[:, vh, :], SP_.LFp.r)
                        o_.LG = View(SP_.LGp.t[:, :, vh, :], SP_.LGp.r); o_.LH = View(SP_.LHp.t[:, vh, :], SP_.LHp.r)
                        o_.kbg = sb2(n + "kbg", [128, 128], BF16)
                        for nm in ("attnT", "vbeta", "kd", "nwT"):
                            setattr(o_, nm, [sb2(n + nm + str(i), [128, 128], BF16) for i in range(2)])
                        o_.TT = [View(SP_.TTp[i].t[:, vh, :], SP_.TTp[i].r) for i in range(2)]
                        row.append(o_)
                    IT.append(row)
                HQ = []
                for vh in range(2):
                    o_ = HS()
                    n = "q%d_" % vh
                    o_.vnew = sb2(n + "vnew", [128, 128], BF16)
                    o_.S32 = sb2(n + "S32", [128, 128]); o_.Sbf = sb2(n + "Sbf", [128, 128], BF16)
                    o_.tmpo = sb2(n + "tmpo", [128, 128]); o_.o = sb2(n + "o", [128, 128]); o_.junk = o_.tmpo
                    o_.ssq = sb2(n + "ssq", [128, 1]); o_.r1 = sb2(n + "r1", [128, 1]); o_.r2 = sb2(n + "r2", [128, 1])
                    o_.yb = sb2(n + "yb", [128, 128], BF16)
                    o_.ybT = sb2(n + "ybT", [128, 512], BF16)
                    HQ.append(o_)

                if True:
                    IDF = CF(C_ID)
                    IDB = CB(C_ID)
                    ASTB = CB(C_AST)
                    def make_stages(j, c, b_):
                        csl = slice(c * 128, (c + 1) * 128)
                        cl = c % 2
                        lsl = slice(cl * 128, (cl + 1) * 128)
                        pp = (c // 2) % 2
                        kkm = KK[c % 2]
                        qkm = QK[c % 2]

                        def st_shared():
                          pk = nb()
                          if True:
                            kb.op("pe", lambda e, pk=pk: e.matmul(pk.t[:, 0:128], b_.k.t[:, lsl], b_.k.t[:, lsl], start=True, stop=True),
                                  reads=[b_.k.r], writes=[pk.r])
                            kb.op("pe", lambda e, pk=pk: e.matmul(pk.t[:, 128:256], b_.k.t[:, lsl], b_.q.t[:, lsl], start=True, stop=True),
                                  reads=[b_.k.r, b_.q.r], writes=[pk.r])
                            kb.op("dve", lambda e, pk=pk: e.tensor_tensor(
                                out=kkm.t[:], in0=pk.t[:, 0:128].unsqueeze(1).broadcast_to([128, 4, 128]),
                                in1=cst.t[:, C_SUBD:C_SUBD + 4, :], op=ALU.mult), reads=[cst.r], writes=[kkm.r, pk.r])
                            kb.op("dve", lambda e, pk=pk: e.tensor_tensor(
                                out=qkm.t[:], in0=pk.t[:, 128:256], in1=CF(C_UI), op=ALU.mult),
                                reads=[cst.r], writes=[qkm.r, pk.r])

                        def st_prep(vh):
                            H = IT[c % 2][vh]; Q = HQ[vh]
                            hv = 2 * j + vh
                            kb.op("dve", lambda e: e.tensor_scalar_mul(out=H.Bh.t[:, 0, :], in0=CF(C_TRI), scalar1=ghl.t[:, 0, c, hv:hv + 1]),
                                  reads=[cst.r, ghl.r], writes=[H.Bh.r])
                            kb.op("dve", lambda e: e.tensor_scalar_mul(out=H.Bh.t[:, 1, :], in0=CF(C_TRI), scalar1=ghl.t[:, 1, c, hv:hv + 1]),
                                  reads=[cst.r, ghl.r], writes=[H.Bh.r])
                            kb.op("act", lambda e: e.activation(out=H.Dl.t[:, 0, :], in_=CF(C_ID), func=AF.Identity, scale=lhl.t[:, 0, c, hv:hv + 1]),
                                  reads=[cst.r, lhl.r], writes=[H.Dl.r])
                            kb.op("act", lambda e: e.activation(out=H.Dl.t[:, 1, :], in_=CF(C_ID), func=AF.Identity, scale=lhl.t[:, 1, c, hv:hv + 1]),
                                  reads=[cst.r, lhl.r], writes=[H.Dl.r])
                            pd = nb()
                            Bh_, Bl_ = H.Bh.t[:, 0, :], H.Bh.t[:, 1, :]
                            Dh_, Dl_ = H.Dl.t[:, 0, :], H.Dl.t[:, 1, :]
                            rB = [cstb.r, H.Bh.r]
                            rD = [cstb.r, H.Dl.r]
                            mm(pd, 0, ASTB, Bh_, rB, True, False)
                            mm(pd, 0, ASTB, Bl_, rB, False, True)
                            mm(pd, 128, ASTB, Bh_, rB, True, False)
                            mm(pd, 128, ASTB, Bl_, rB, False, False)
                            mm(pd, 128, ASTB, Dh_, rD, False, False)
                            mm(pd, 128, ASTB, Dl_, rD, False, True)
                            mm(pd, 256, Bh_, ASTB, rB, True, False)
                            mm(pd, 256, Bl_, ASTB, rB, False, False)
                            mm(pd, 256, Dh_, ASTB, rD, False, False)
                            mm(pd, 256, Dl_, ASTB, rD, False, True)
                            kb.op("act", lambda e: e.activation(out=H.E3.t[:], in_=pd.t[:, 0:384], func=AF.Exp),
                                  writes=[H.E3.r, pd.r])
                            kb.op("dve", lambda e: e.tensor_tensor(out=H.attnT[pp].t[:], in0=H.E3.t[:, 0:128], in1=qkm.t[:], op=ALU.mult),
                                  reads=[H.E3.r, qkm.r], writes=[H.attnT[pp].r])
                            V32 = H.V32.t[:].rearrange("p (a c) -> p a c", c=128)
                            kb.op("dve", lambda e: e.scalar_tensor_tensor(
                                out=V32[:, 0, :], in0=H.E3.t[:, 128:256], scalar=-1.0, in1=kkm.t[:, 0, :], op0=ALU.mult, op1=ALU.mult),
                                reads=[H.E3.r, kkm.r], writes=[H.V32.r])
                            kb.op("dve", lambda e: e.scalar_tensor_tensor(
                                out=V32[:, 1:4, :], in0=H.E3.t[:, 256:384].unsqueeze(1).broadcast_to([128, 3, 128]), scalar=-1.0,
                                in1=kkm.t[:, 1:4, :], op0=ALU.mult, op1=ALU.mult),
                                reads=[H.E3.r, kkm.r], writes=[H.V32.r])
                            kb.op("act", lambda e: e.copy(out=H.Vall.t[:, 0, :], in_=H.V32.t[:]),
                                  reads=[H.V32.r], writes=[H.Vall.r])
                            kb.op("dve", lambda e: e.tensor_tensor(out=H.Vall.t[:, 1, :], in0=H.V32.t[:], in1=H.Vall.t[:, 0, :], op=ALU.subtract),
                                  reads=[H.V32.r], writes=[H.Vall.r])
                            kb.op("act", lambda e: e.activation(out=H.kbg.t[:], in_=b_.kt.t[:, cl, :], func=AF.Identity, scale=bkk.t[:, c, hv:hv + 1]),
                                  reads=[b_.kt.r, bkk.r], writes=[H.kbg.r])
                            kb.op("act", lambda e: e.activation(out=H.vbeta[pp].t[:], in_=b_.vt.t[:, cl, vh * 128:(vh + 1) * 128],
                                                                func=AF.Identity, scale=beta.t[:, c, hv:hv + 1]),
                                  reads=[b_.vt.r, beta.r], writes=[H.vbeta[pp].r])
                            kb.op("act", lambda e: e.activation(out=H.kd[pp].t[:], in_=b_.kt.t[:, cl, :], func=AF.Identity, scale=ekd.t[:, c, hv:hv + 1]),
                                  reads=[b_.kt.r, ekd.r], writes=[H.kd[pp].r])

                        def mm(pb, lo, lhsT, rhs, rds, start=True, stop=True):
                            kb.op("pe", lambda e: e.matmul(pb.t[:, lo:lo + 128], lhsT, rhs, start=start, stop=stop),
                                  reads=rds, writes=[pb.r])

                        def mmf(pb, lo, lhsT, rhs, rds, start=True, stop=True):
                            mm(pb, lo, lhsT, rhs, rds, start, stop)

                        def mm3(pb, lo, A, B, rds, first=True, last=True):
                            (Ah, Al), (Bh, Bl) = A, B
                            mm(pb, lo, Ah, Bh, rds, first, False)
                            mm(pb, lo, Ah, Bl, rds, False, False)
                            mm(pb, lo, Al, Bh, rds, False, last)

                        def mmI(pb, lo, B, rds, first, last):
                            Bh, Bl = B
                            mm(pb, lo, IDB, Bh, rds + [cstb.r], first, False)
                            mm(pb, lo, IDB, Bl, rds + [cstb.r], False, last)

                        def mmT(pb, lo, A, rds, first=True, last=True):
                            Ah, Al = A
                            mm(pb, lo, Ah, IDB, rds + [cstb.r], first, False)
                            mm(pb, lo, Al, IDB, rds + [cstb.r], False, last)

                        def P(tile, lo, w=128):
                            return (tile.t[:, 0, lo:lo + w], tile.t[:, 1, lo:lo + w])

                        def VV(H, a):
                            return (H.Vall.t[:, 0, a * 128:(a + 1) * 128], H.Vall.t[:, 1, a * 128:(a + 1) * 128])

                        def split_psum(dst, pb, W):
                            kb.op("act", lambda e: e.copy(out=dst.t[:, 0, :], in_=pb.t[:, 0:W]), writes=[dst.r, pb.r])
                            kb.op("dve", lambda e: e.tensor_tensor(out=dst.t[:, 1, :], in0=pb.t[:, 0:W], in1=dst.t[:, 0, :],
                                                                   op=ALU.subtract), writes=[dst.r, pb.r])

                        pbs = {}

                        def pair_bank(name, vh):
                            if vh == 0:
                                pbs[name] = nb()
                            return pbs[name], vh * 256

                        def pview(pb, W):
                            return pb.t[:, :].rearrange("p (i w) -> p i w", w=256)[:, :, 0:W]

                        def split_pair(dst, pb, W):
                            kb.op("act", lambda e: e.copy(out=dst.t[:, 0, :, 0:W], in_=pview(pb, W)), writes=[dst.r, pb.r])
                            kb.op("dve", lambda e: e.tensor_tensor(out=dst.t[:, 1, :, 0:W], in0=pview(pb, W), in1=dst.t[:, 0, :, 0:W],
                                                                   op=ALU.subtract), writes=[dst.r, pb.r])

                        def st_A(vh):
                            H = IT[c % 2][vh]; Q = HQ[vh]
                            Vd, Vtd = VV(H, 0), VV(H, 1)
                            pb, off = pair_bank("A", vh)
                            mm3(pb, off + 0, Vtd, Vd, [H.Vall.r])
                            mm3(pb, off + 128, Vd, Vtd, [H.Vall.r])
                            if vh == 1:
                                split_pair(SLOT[c % 2].LAp, pb, 256)

                        def st_B(vh):
                            H = IT[c % 2][vh]; Q = HQ[vh]
                            V2, Vt2 = P(H.LA, 0), P(H.LA, 128)
                            Vd = VV(H, 0)
                            pb = nb()
                            mm3(pb, 0, Vt2, V2, [H.LA.r])
                            mm3(pb, 128, V2, Vt2, [H.LA.r])
                            mm(pb, 256, IDB, IDB, [cstb.r], True, False)
                            mmI(pb, 256, Vd, [H.Vall.r], False, False)
                            mmT(pb, 256, Vt2, [H.LA.r], False, False)
                            mm3(pb, 256, Vt2, Vd, [H.LA.r, H.Vall.r], False, True)
                            split_psum(H.LB, pb, 384)

                        def st_C(vh):
                            H = IT[c % 2][vh]; Q = HQ[vh]
                            V4, Vt4, Y1 = P(H.LB, 0), P(H.LB, 128), P(H.LB, 256)
                            pb = nb()
                            mm3(pb, 0, Vt4, V4, [H.LB.r])
                            mm3(pb, 128, V4, Vt4, [H.LB.r])
                            mm3(pb, 256, Vt4, Y1, [H.LB.r], True, False)
                            mmI(pb, 256, Y1, [H.LB.r], False, True)
                            split_psum(H.LC, pb, 384)

                        def st_D(vh):
                            H = IT[c % 2][vh]; Q = HQ[vh]
                            V8, Vt8, Y2 = P(H.LC, 0), P(H.LC, 128), P(H.LC, 256)
                            pb, off = pair_bank("D", vh)
                            mm3(pb, off + 0, V8, Vt8, [H.LC.r])
                            mm3(pb, off + 128, Vt8, Y2, [H.LC.r], True, False)
                            mmI(pb, off + 128, Y2, [H.LC.r], False, True)
                            if vh == 1:
                                split_pair(SLOT[c % 2].LDp, pb, 256)

                        def st_E(vh):
                            H = IT[c % 2][vh]; Q = HQ[vh]
                            Vt16, Y3 = P(H.LD, 0), P(H.LD, 128)
                            pb, off = pair_bank("E", vh)
                            mm3(pb, off, Vt16, Y3, [H.LD.r], True, False)
                            mmI(pb, off, Y3, [H.LD.r], False, True)
                            if vh == 1:
                                split_pair(SLOT[c % 2].LEp, pb, 128)

                        def st_F(vh):
                            H = IT[c % 2][vh]; Q = HQ[vh]
                            T0t = P(H.LE, 0)
                            pb, off = pair_bank("F", vh)
                            mm(pb, off + 0, T0t[0], IDB, [H.LE.r, cstb.r])
                            mm(pb, off + 128, VV(H, 2)[0], T0t[0], [H.Vall.r, H.LE.r])
                            if vh == 1:
                                LFp = SLOT[c % 2].LFp
                                evac(LFp.t[:, :, :], pview(pb, 256), [], [LFp.r, pb.r])

                        def st_G(vh):
                            H = IT[c % 2][vh]; Q = HQ[vh]
                            pb, off = pair_bank("G", vh)
                            mm(pb, off, H.LF.t[:, 0:128], H.LF.t[:, 128:256], [H.LF.r], True, False)
                            mmI(pb, off, P(H.LE, 0), [H.LE.r], False, True)
                            if vh == 1:
                                split_pair(SLOT[c % 2].LGp, pb, 128)

                        def st_H(vh):
                            H = IT[c % 2][vh]; Q = HQ[vh]
                            T1t = P(H.LG, 0)
                            pb, off = pair_bank("H", vh)
                            mm(pb, off + 0, T1t[0], IDB, [H.LG.r, cstb.r])
                            mm(pb, off + 128, VV(H, 3)[0], T1t[0], [H.Vall.r, H.LG.r])
                            if vh == 1:
                                LHp = SLOT[c % 2].LHp
                                evac(LHp.t[:, :, :], pview(pb, 256), [], [LHp.r, pb.r])

                        def st_I(vh):
                            H = IT[c % 2][vh]; Q = HQ[vh]
                            pb, off = pair_bank("I", vh)
                            mm(pb, off, H.LH.t[:, 0:128], H.LH.t[:, 128:256], [H.LH.r], True, False)
                            mmI(pb, off, P(H.LG, 0), [H.LG.r], False, True)
                            if vh == 1:
                                TTp = SLOT[c % 2].TTp[pp]
                                evac(TTp.t[:, :, :], pview(pb, 128), [], [TTp.r, pb.r])

                        def st_W(vh):
                            H = IT[c % 2][vh]; Q = HQ[vh]
                            pb = nb()
                            mmf(pb, 0, H.kbg.t[:], H.TT[pp].t[:], [H.kbg.r, H.TT[pp].r])
                            kb.op("act", lambda e: e.activation(out=H.nwT[pp].t[:], in_=pb.t[:, 0:128], func=AF.Copy, scale=-1.0),
                                  writes=[H.nwT[pp].r, pb.r])

                        def st_V(vh):
                            H = IT[c % 2][vh]; Q = HQ[vh]
                            pb = nb()
                            mmf(pb, 0, H.TT[pp].t[:], H.vbeta[pp].t[:], [H.TT[pp].r, H.vbeta[pp].r], True, c == 0)
                            if c > 0:
                                mmf(pb, 0, H.nwT[pp].t[:], Q.Sbf.t[:], [H.nwT[pp].r, Q.Sbf.r], False, True)
                            evac(Q.vnew.t[:], pb.t[:, 0:128], [], [Q.vnew.r, pb.r])

                        def st_O(vh):
                            H = IT[c % 2][vh]; Q = HQ[vh]
                            hv = 2 * j + vh
                            pb = nb()
                            mmf(pb, 128, H.attnT[pp].t[:], Q.vnew.t[:], [H.attnT[pp].r, Q.vnew.r])
                            if c > 0:
                                mmf(pb, 0, b_.q.t[:, lsl], Q.Sbf.t[:], [b_.q.r, Q.Sbf.r])
                                kb.op("act", lambda e: e.activation(out=Q.tmpo.t[:], in_=pb.t[:, 0:128], func=AF.Identity,
                                                                    scale=egc.t[:, c, hv:hv + 1]),
                                      reads=[egc.r], writes=[Q.tmpo.r, pb.r])
                                kb.op("dve", lambda e: e.tensor_tensor(out=Q.o.t[:], in0=pb.t[:, 128:256], in1=Q.tmpo.t[:], op=ALU.add),
                                      reads=[Q.tmpo.r], writes=[Q.o.r, pb.r])
                            else:
                                kb.op("dve", lambda e: e.tensor_copy(out=Q.o.t[:], in_=pb.t[:, 128:256]), writes=[Q.o.r, pb.r])

                        def st_S(vh):
                            H = IT[c % 2][vh]; Q = HQ[vh]
                            hv = 2 * j + vh
                            if c == NT - 1:
                                return
                            pb = nb()
                            mmf(pb, 0, H.kd[pp].t[:], Q.vnew.t[:], [H.kd[pp].r, Q.vnew.r])
                            if c > 0:
                                kb.op("dve", lambda e: e.scalar_tensor_tensor(
                                    out=Q.S32.t[:], in0=Q.S32.t[:], scalar=cdd.t[:, c, hv:hv + 1], in1=pb.t[:, 0:128],
                                    op0=ALU.mult, op1=ALU.add), reads=[cdd.r], writes=[Q.S32.r, pb.r])
                            else:
                                kb.op("dve", lambda e: e.tensor_copy(out=Q.S32.t[:], in_=pb.t[:, 0:128]), writes=[Q.S32.r, pb.r])
                            kb.op("act", lambda e: e.copy(out=Q.Sbf.t[:], in_=Q.S32.t[:]), reads=[Q.S32.r], writes=[Q.Sbf.r])

                        def st_Y(vh):
                            H = IT[c % 2][vh]; Q = HQ[vh]
                            hv = 2 * j + vh
                            kb.op("dve", lambda e: e.memset(Q.ssq.t[:], 0.0), writes=[Q.ssq.r])
                            kb.op("act", lambda e: e.activation(out=Q.junk.t[:], in_=Q.o.t[:], func=AF.Square, accum_out=Q.ssq.t[:, 0:1]),
                                  reads=[Q.o.r], writes=[Q.junk.r, Q.ssq.r])
                            kb.op("dve", lambda e: e.tensor_scalar(out=Q.r1.t[:], in0=Q.ssq.t[:], scalar1=1.0 / 128.0, scalar2=RMS_EPS,
                                                                   op0=ALU.mult, op1=ALU.add), reads=[Q.ssq.r], writes=[Q.r1.r])
                            kb.op("act", lambda e: e.activation(out=Q.r2.t[:], in_=Q.r1.t[:], func=AF.Ln),
                                  reads=[Q.r1.r], writes=[Q.r2.r])
                            kb.op("act", lambda e: e.activation(out=Q.r2.t[:], in_=Q.r2.t[:], func=AF.Exp, scale=-0.5),
                                  reads=[Q.r2.r], writes=[Q.r2.r])
                            kb.op("dve", lambda e: e.scalar_tensor_tensor(
                                out=Q.yb.t[:], in0=Q.o.t[:], scalar=Q.r2.t[:, 0:1], in1=b_.zs.t[:, cl, vh * 128:(vh + 1) * 128],
                                op0=ALU.mult, op1=ALU.mult), reads=[Q.o.r, Q.r2.r, b_.zs.r], writes=[Q.yb.r])
                            pb = nb()
                            kb.op("pe", lambda e: e.matmul(pb.t[:, 0:128], Q.yb.t[:], CB(C_ID), start=True, stop=True),
                                  reads=[Q.yb.r, cstb.r], writes=[pb.r])
                            evac(Q.ybT.t[:, (c % 4) * 128:(c % 4 + 1) * 128], pb.t[:, 0:128], [], [Q.ybT.r, pb.r])
                            if c % 4 == 3:
                                kb.dma("sp", ybT_d[hv][:, (c - 3) * 128:(c + 1) * 128], Q.ybT.t[:], reads=[Q.ybT.r])

                        return dict(shared=st_shared, prep=st_prep, A=st_A, B=st_B, C=st_C, D=st_D, E=st_E, F=st_F, G=st_G,
                                    H=st_H, I=st_I, W=st_W, V=st_V, O=st_O, S=st_S, Y=st_Y)

                    chain_order = ["shared", "prep", "A", "B", "C", "D", "E", "F", "G", "H", "I", "W"]
                    seq_order = ["V", "O", "S", "Y"]

                    def run_stage(stg, name):
                        if name == "shared":
                            stg[name]()
                        else:
                            for vh in range(2):
                                stg[name](vh)
                    prev_pair = None
                    NG = 8 * (NT // 2)
                    load_inputs(0)
                    for gcp in range(NG + 1):
                        if gcp + 1 < NG:
                            load_inputs(gcp + 1)
                        if gcp < NG:
                            j_, cp_ = divmod(gcp, 8)
                            cur = [make_stages(j_, 2 * cp_, INB[gcp % 3]), make_stages(j_, 2 * cp_ + 1, INB[gcp % 3])]
                        else:
                            cur = None
                        seqs = []
                        if prev_pair is not None:
                            for stg in prev_pair:
                                for name in seq_order:
                                    seqs.append((stg, name))
                        si_ = 0
                        for ci, name in enumerate(chain_order):
                            if cur is not None:
                                for stg in cur:
                                    run_stage(stg, name)
                            if ci >= 1 and si_ < len(seqs) and (ci <= 4 or ci >= 6):
                                run_stage(*seqs[si_])
                                si_ += 1
                        while si_ < len(seqs):
                            run_stage(*seqs[si_])
                            si_ += 1
                        prev_pair = cur


        if stop_after not in ("p0", "p1a", "p1b"):
            kb.barrier()
            with ExitStack() as es3:
                def sb3(name, shape, dt=F32, nreg=1):
                    return T(es3.enter_context(nc.sbuf_tensor(name, list(shape), dt)), nreg)
                yaR = sb3("yaR", [128, 8, S], BF16)
                ybR = sb3("ybR", [128, 16, S], BF16)
                for h in range(8):
                    kb.dma("sp", yaR.t[:, h, :], yaT_d[h], writes=[yaR.r])
                for h in range(16):
                    kb.dma("sp", ybR.t[:, h, :], ybT_d[h], writes=[ybR.r])
                wg_ = [sb3("w2g%d" % i, [128, KC, 256], BF16) for i in range(2)]
                wa_ = [sb3("w2a%d" % i, [128, 8, 128], BF16) for i in range(2)]
                wb_ = [sb3("w2b%d" % i, [128, 16, 128], BF16) for i in range(2)]
                sga = sb3("sga", [128, 512])
                sgb = sb3("sgb", [128, 512])
                t1 = sb3("t1", [128, 512])
                mst = [sb3("mst%d" % i, [128, 512], BF16) for i in range(2)]

                def load2(cc):
                    i = cc % 2
                    kb.dma("pool", wg_[i].t[:].rearrange("p k c -> p (k c)"), wgate_d[cc], writes=[wg_[i].r])
                    kb.dma("pool", wa_[i].t[:].rearrange("p k c -> p (k c)"), wpm_d[cc], writes=[wa_[i].r])
                    kb.dma("pool", wb_[i].t[:].rearrange("p k c -> p (k c)"), wpg_d[cc], writes=[wb_[i].r])
                load2(0)
                mi = 0
                for cc in range(16):
                    if cc + 1 < 16:
                        load2(cc + 1)
                    i = cc % 2
                    for g in range(4):
                        gs = slice(g * 512, (g + 1) * 512)
                        pA, pB, pC, pD = nb(), nb(), nb(), nb()
                        for k in range(KC):
                            kb.op("pe", lambda e, k=k: e.matmul(pA.t[:, :], wg_[i].t[:, k, 0:128], hT.t[:, k, gs],
                                                                start=(k == 0), stop=(k == KC - 1)),
                                  reads=[wg_[i].r, hT_all[k]], writes=[pA.r])
                        kb.op("act", lambda e: e.activation(out=sga.t[:], in_=pA.t[:, :], func=AF.Sigmoid),
                              writes=[sga.r, pA.r])
                        for k in range(KC):
                            kb.op("pe", lambda e, k=k: e.matmul(pB.t[:, :], wg_[i].t[:, k, 128:256], hT.t[:, k, gs],
                                                                start=(k == 0), stop=(k == KC - 1)),
                                  reads=[wg_[i].r, hT_all[k]], writes=[pB.r])
                        kb.op("act", lambda e: e.activation(out=sgb.t[:], in_=pB.t[:, :], func=AF.Sigmoid),
                              writes=[sgb.r, pB.r])
                        for k in range(8):
                            kb.op("pe", lambda e, k=k: e.matmul(pC.t[:, :], wa_[i].t[:, k, :], yaR.t[:, k, gs],
                                                                start=(k == 0), stop=(k == 7)),
                                  reads=[wa_[i].r, yaR.r], writes=[pC.r])
                        kb.op("dve", lambda e: e.tensor_tensor(out=sga.t[:], in0=pC.t[:, :], in1=sga.t[:], op=ALU.mult),
                              writes=[sga.r, pC.r])
                        for k in range(16):
                            kb.op("pe", lambda e, k=k: e.matmul(pD.t[:, :], wb_[i].t[:, k, :], ybR.t[:, k, gs],
                                                                start=(k == 0), stop=(k == 15)),
                                  reads=[wb_[i].r, ybR.r], writes=[pD.r])
                        kb.op("dve", lambda e: e.tensor_tensor(out=t1.t[:], in0=pD.t[:, :], in1=sgb.t[:], op=ALU.mult),
                              reads=[sgb.r], writes=[t1.r, pD.r])
                        m_ = mst[mi % 2]
                        mi += 1
                        kb.op("dve", lambda e, m_=m_: e.tensor_tensor(out=m_.t[:], in0=t1.t[:], in1=sga.t[:], op=ALU.add),
                              reads=[t1.r, sga.r], writes=[m_.r])
                        kb.dma("sp", mgT_d[cc][:, gs], m_.t[:], reads=[m_.r])

        def layer_norm_tile(pre, junk, st, lng, lnb):
            kb.op("dve", lambda e: e.memset(st.t[:, 0:2], 0.0), writes=[st.r])
            kb.op("act", lambda e: e.activation(out=junk.t[:], in_=pre.t[:], func=AF.Copy, accum_out=st.t[:, 0:1]),
                  reads=[pre.r], writes=[junk.r, st.r])
            kb.op("act", lambda e: e.activation(out=junk.t[:], in_=pre.t[:], func=AF.Square, accum_out=st.t[:, 1:2]),
                  reads=[pre.r], writes=[junk.r, st.r])
            kb.op("dve", lambda e: e.tensor_scalar_mul(out=st.t[:, 2:4], in0=st.t[:, 0:2], scalar1=1.0 / D),
                  reads=[st.r], writes=[st.r])
            kb.op("dve", lambda e: e.tensor_tensor(out=st.t[:, 4:5], in0=st.t[:, 2:3], in1=st.t[:, 2:3], op=ALU.mult),
                  reads=[st.r], writes=[st.r])
            kb.op("dve", lambda e: e.tensor_tensor(out=st.t[:, 5:6], in0=st.t[:, 3:4], in1=st.t[:, 4:5], op=ALU.subtract),
                  reads=[st.r], writes=[st.r])
            kb.op("act", lambda e: e.activation(out=st.t[:, 6:7], in_=st.t[:, 5:6], func=AF.Ln, bias=epsr.t[:, 1:2]),
                  reads=[st.r, epsr.r], writes=[st.r])
            kb.op("act", lambda e: e.activation(out=st.t[:, 6:7], in_=st.t[:, 6:7], func=AF.Exp, scale=-0.5),
                  reads=[st.r], writes=[st.r])
            kb.op("dve", lambda e: e.scalar_tensor_tensor(out=st.t[:, 7:8], in0=st.t[:, 2:3], scalar=-1.0, in1=st.t[:, 6:7],
                                                          op0=ALU.mult, op1=ALU.mult), reads=[st.r], writes=[st.r])
            kb.op("act", lambda e: e.activation(out=pre.t[:], in_=pre.t[:], func=AF.Identity,
                                                scale=st.t[:, 6:7], bias=st.t[:, 7:8]),
                  reads=[st.r], writes=[pre.r])
            kb.op("dve", lambda e: e.tensor_tensor(out=pre.t[:], in0=pre.t[:], in1=lng.t[:], op=ALU.mult),
                  reads=[lng.r], writes=[pre.r])
            kb.op("dve", lambda e: e.tensor_tensor(out=pre.t[:], in0=pre.t[:], in1=lnb.t[:], op=ALU.add),
                  reads=[lnb.r], writes=[pre.r])

        if stop_after not in ("p0", "p1a", "p1b", "p2"):
            kb.barrier()
            with ExitStack() as es4:
                def sb4(name, shape, dt=F32, nreg=1):
                    return T(es4.enter_context(nc.sbuf_tensor(name, list(shape), dt)), nreg)
                wout = sb4("wout", [128, KC, D], BF16)
                for g in range(4):
                    kb.dma("pool", wout.t[:, :, g * 512:(g + 1) * 512], wout_d[g].rearrange("p (k c) -> p k c", c=512),
                           writes=[wout.r])
                mgR = sb4("mgR", [128, KC, 512], BF16)
                g1b = sb4("g1b", [128, D])
                lng = sb4("lng", [128, D])
                lnb = sb4("lnb", [128, D])
                kb.dma("sp", lng.t[:], lnrep_d[0], writes=[lng.r])
                kb.dma("sp", lnb.t[:], lnrep_d[1], writes=[lnb.r])
                xt = [sb4("xt0", [128, D])]
                pres = [sb4("pre%d" % i, [128, D]) for i in range(2)]
                junk = sb4("junk3", [128, D], BF16)
                st = sb4("st3", [128, 8])
                dgt = sb4("dgt", [128, 128])
                for k4 in range(4):
                    pb = nb()
                    for kk in range(4):
                        k = k4 * 4 + kk
                        kb.op("dve", lambda e, k=k: e.tensor_scalar_mul(out=dgt.t[:], in0=CF(C_ID), scalar1=modT.t[:, 32 + k:33 + k]),
                              reads=[cst.r, modT.r], writes=[dgt.r])
                        kb.op("pe", lambda e, kk=kk, pb=pb: e.matmul(pb.t[:, kk * 128:(kk + 1) * 128], CF(C_ONES), dgt.t[:],
                                                                     start=True, stop=True), reads=[cst.r, dgt.r], writes=[pb.r])
                    kb.op("dve", lambda e, k4=k4, pb=pb: e.tensor_copy(out=g1b.t[:, k4 * 512:(k4 + 1) * 512], in_=pb.t[:, :]),
                          writes=[g1b.r, pb.r])
                def p3_main(t):
                    g = t // 4
                    if t % 4 == 0:
                        kb.dma("sp", mgR.t[:], mgT_d[:, :, g * 512:(g + 1) * 512].rearrange("k p c -> p k c"), writes=[mgR.r])
                    x_ = xt[0]
                    pre = pres[t % 2]
                    kb.dma("sp", x_.t[:], x_d[t], writes=[x_.r])
                    tl = slice((t % 4) * 128, (t % 4 + 1) * 128)
                    for cg in range(4):
                        pb = nb()
                        for k in range(KC):
                            kb.op("pe", lambda e, k=k, cg=cg, pb=pb: e.matmul(
                                pb.t[:, :], mgR.t[:, k, tl], wout.t[:, k, cg * 512:(cg + 1) * 512],
                                start=(k == 0), stop=(k == KC - 1)), reads=[mgR.r, wout.r], writes=[pb.r])
                        cs_ = slice(cg * 512, (cg + 1) * 512)
                        kb.op("dve", lambda e, pb=pb, cs_=cs_, pre=pre: e.tensor_tensor(out=pre.t[:, cs_], in0=pb.t[:, :], in1=g1b.t[:, cs_],
                                                                              op=ALU.mult), reads=[g1b.r], writes=[pre.r, pb.r])
                    kb.op("dve", lambda e, x_=x_, pre=pre: e.scalar_tensor_tensor(out=pre.t[:], in0=x_.t[:], scalar=ALPHA, in1=pre.t[:],
                                                                         op0=ALU.mult, op1=ALU.add), reads=[x_.r], writes=[pre.r])
                    layer_norm_tile(pre, junk, st, lng, lnb)
                    kb.dma("pool", x1_d[t], pre.t[:], reads=[pre.r])
                    return pre
                def p3_tr(t, pre):
                    for k4 in range(4):
                        pb = nb()
                        for kk in range(4):
                            k = k4 * 4 + kk
                            kb.op("pe", lambda e, k=k, kk=kk, pb=pb, pre=pre: e.transpose(
                                pb.t[:, kk * 128:(kk + 1) * 128], pre.t[:, k * 128:(k + 1) * 128], CF(C_ID)),
                                reads=[pre.r, cst.r], writes=[pb.r])
                        for kk in range(4):
                            k = k4 * 4 + kk
                            kb.op("act", lambda e, k=k, kk=kk, pb=pb, t=t: e.activation(
                                out=hT.t[:, k, t * 128:(t + 1) * 128], in_=pb.t[:, kk * 128:(kk + 1) * 128], func=AF.Identity,
                                scale=modT.t[:, 64 + k:65 + k], bias=modT.t[:, 48 + k:49 + k]),
                                reads=[modT.r], writes=[hT_all[k], pb.r])
                pend = None
                for t in range(NT + 1):
                    cur = (t, p3_main(t)) if t < NT else None
                    if pend is not None:
                        p3_tr(*pend)
                    pend = cur

            kb.barrier()
            with ExitStack() as es5:
                def sb5(name, shape, dt=F32, nreg=1):
                    return T(es5.enter_context(nc.sbuf_tensor(name, list(shape), dt)), nreg)
                GT = 1024
                actT = sb5("actT", [128, FC, GT], BF16, nreg=2)
                wfi = [sb5("wfi%d" % i, [128, KC, 256], BF16) for i in range(2)]
                wfo = [sb5("wfo%d" % i, [128, FC, 128], BF16) for i in range(2)]
                sgs = [sb5("sg%d" % i, [128, 512]) for i in range(2)]
                y2c = [sb5("y2c%d" % i, [128, 512]) for i in range(2)]
                y2st = [sb5("y2st%d" % i, [128, 4, 128]) for i in range(1)]
                si = 0
                yi = 0
                pend4 = None

                def p4_tr(yc, ys, oc, t0):
                    pT = nb()
                    for tt in range(4):
                        kb.op("pe", lambda e, tt=tt: e.transpose(
                            pT.t[:, tt * 128:(tt + 1) * 128], yc.t[:, tt * 128:(tt + 1) * 128], CF(C_ID)),
                            reads=[yc.r, cst.r], writes=[pT.r])
                    kb.op("dve", lambda e: e.tensor_copy(
                        out=ys.t[:], in_=pT.t[:, :].rearrange("p (a c) -> p a c", c=128)), writes=[ys.r, pT.r])
                    kb.dma("sp", y2_d[t0:t0 + 4, :, oc * 128:(oc + 1) * 128].rearrange("a p c -> p a c"), ys.t[:],
                           reads=[ys.r])
                for g in range(S // GT):
                    for fc in range(FC):
                        w_ = wfi[fc % 2]
                        kb.dma("pool", w_.t[:].rearrange("p k c -> p (k c)"), wffi_d[fc], writes=[w_.r])
                        for hf in range(GT // 512):
                            gs = slice(g * GT + hf * 512, g * GT + (hf + 1) * 512)
                            pG, pU = nb(), nb()
                            sg_ = sgs[si % 2]
                            si += 1
                            for k in range(KC):
                                kb.op("pe", lambda e, k=k, w_=w_, pG=pG, gs=gs: e.matmul(pG.t[:, :], w_.t[:, k, 0:128], hT.t[:, k, gs],
                                                                                        start=(k == 0), stop=(k == KC - 1)),
                                      reads=[w_.r, hT_all[k]], writes=[pG.r])
                            kb.op("act", lambda e, pG=pG, sg_=sg_: e.activation(out=sg_.t[:], in_=pG.t[:, :], func=AF.Silu),
                                  writes=[sg_.r, pG.r])
                            for k in range(KC):
                                kb.op("pe", lambda e, k=k, w_=w_, pU=pU, gs=gs: e.matmul(pU.t[:, :], w_.t[:, k, 128:256], hT.t[:, k, gs],
                                                                                        start=(k == 0), stop=(k == KC - 1)),
                                      reads=[w_.r, hT_all[k]], writes=[pU.r])
                            kb.op("dve", lambda e, pU=pU, fc=fc, hf=hf, sg_=sg_: e.tensor_tensor(
                                out=actT.t[:, fc, hf * 512:(hf + 1) * 512], in0=pU.t[:, :], in1=sg_.t[:], op=ALU.mult),
                                reads=[sg_.r], writes=[actT.regs[hf], pU.r])
                    for oc in range(16):
                        w_ = wfo[oc % 2]
                        kb.dma("pool", w_.t[:].rearrange("p k c -> p (k c)"), wffo_d[oc], writes=[w_.r])
                        for hf in range(GT // 512):
                            pY = nb()
                            for fc in range(FC):
                                kb.op("pe", lambda e, fc=fc, w_=w_, pY=pY, hf=hf: e.matmul(
                                    pY.t[:, :], w_.t[:, fc, :], actT.t[:, fc, hf * 512:(hf + 1) * 512],
                                    start=(fc == 0), stop=(fc == FC - 1)), reads=[w_.r, actT.regs[hf]], writes=[pY.r])
                            yc = y2c[yi % 2]
                            kb.op("act", lambda e, pY=pY, yc=yc, oc=oc: e.activation(out=yc.t[:], in_=pY.t[:, :], func=AF.Identity,
                                                                                     scale=modT.t[:, 80 + oc:81 + oc]),
                                  reads=[modT.r], writes=[yc.r, pY.r])
                            if pend4 is not None:
                                p4_tr(*pend4)
                            pend4 = (yc, y2st[0], oc, (g * GT + hf * 512) // 128)
                            yi += 1
                            continue
                            pT = nb()
                            for tt in range(4):
                                kb.op("pe", lambda e, tt=tt, yc=yc, pT=pT: e.transpose(
                                    pT.t[:, tt * 128:(tt + 1) * 128], yc.t[:, tt * 128:(tt + 1) * 128], CF(C_ID)),
                                    reads=[yc.r, cst.r], writes=[pT.r])
                            ys = y2st[yi % 2]
                            yi += 1
                            kb.op("dve", lambda e, pT=pT, ys=ys: e.tensor_copy(
                                out=ys.t[:], in_=pT.t[:, :].rearrange("p (a c) -> p a c", c=128)), writes=[ys.r, pT.r])
                            t0 = (g * GT + hf * 512) // 128
                            kb.dma("sp", y2_d[t0:t0 + 4, :, oc * 128:(oc + 1) * 128].rearrange("a p c -> p a c"), ys.t[:],
                                   reads=[ys.r])
                if pend4 is not None:
                    p4_tr(*pend4)
                    pend4 = None
            kb.barrier()
            with ExitStack() as es6:
                def sb6(name, shape, dt=F32, nreg=1):
                    return T(es6.enter_context(nc.sbuf_tensor(name, list(shape), dt)), nreg)
                lng2 = sb6("lng2", [128, D])
                lnb2 = sb6("lnb2", [128, D])
                junk2 = sb6("junk6", [128, D], BF16)
                st2 = sb6("st6", [128, 8])
                xqs = [sb6("xq%d" % i, [128, D]) for i in range(2)]
                y2t = [sb6("y2t%d" % i, [128, D]) for i in range(2)]
                kb.dma("sp", lng2.t[:], lnrep_d[2], writes=[lng2.r])
                kb.dma("sp", lnb2.t[:], lnrep_d[3], writes=[lnb2.r])
                for t in range(NT):
                    xq = xqs[t % 2]
                    yt_ = y2t[t % 2]
                    kb.dma("sp", xq.t[:], x1_d[t], writes=[xq.r])
                    kb.dma("sp", yt_.t[:], y2_d[t], writes=[yt_.r])
                    kb.op("dve", lambda e, xq=xq, yt_=yt_: e.scalar_tensor_tensor(out=xq.t[:], in0=xq.t[:], scalar=ALPHA, in1=yt_.t[:],
                                                                                 op0=ALU.mult, op1=ALU.add),
                          reads=[yt_.r], writes=[xq.r])
                    layer_norm_tile(xq, junk2, st2, lng2, lnb2)
                    final_toks.append(kb.dma("pool", out_d[t], xq.t[:], reads=[xq.r]))

        if dbg:
            dbg_outs["yaT"] = nc.dram_tensor("dbg_yaT", [NH_M, 128, S], BF16, kind="ExternalOutput").ap()
            with ExitStack() as esd:
                tmpd = T(esd.enter_context(nc.sbuf_tensor("tmpd", [128, S], BF16)))
                for h in range(NH_M):
                    kb.dma("sp", tmpd.t[:], yaT_d[h], writes=[tmpd.r])
                    final_toks.append(kb.dma("sp", dbg_outs["yaT"][h], tmpd.t[:], reads=[tmpd.r]))
        if dbg and stop_after not in ("p0", "p1a"):
            dbg_outs["ybT"] = nc.dram_tensor("dbg_ybT", [HV, 128, S], BF16, kind="ExternalOutput").ap()
            with ExitStack() as esd:
                tmpd = T(esd.enter_context(nc.sbuf_tensor("tmpd2", [128, S], BF16)))
                for h in range(HV):
                    kb.dma("sp", tmpd.t[:], ybT_d[h], writes=[tmpd.r])
                    final_toks.append(kb.dma("sp", dbg_outs["ybT"][h], tmpd.t[:], reads=[tmpd.r]))
        for tok in final_toks:
            kb.wait_tok("sp", tok)
        for key, n in kb.dcnt.items():
            if n > 0:
                kb.wait_tok("sp", (key, 16 * n))
        print("instructions:", kb.n_inst, "waits:", kb.n_wait, {k: v for k, v in kb.cnt.items()})
    return nc


def _klay(w):
    K, C = w.shape
    return np.ascontiguousarray(w.reshape(K // 128, 128, C).transpose(1, 0, 2))


def _rel_bucket_np(n):
    n = np.asarray(n)
    max_exact = 16
    nn = np.maximum(n, 0)
    nf = np.maximum(nn, 1).astype(np.float32)
    large = max_exact + (np.log(nf / np.float32(max_exact)) / np.float32(math.log(128 / max_exact))
                         * np.float32(32 - max_exact)).astype(np.int32)
    large = np.minimum(large, 31)
    return np.where(nn < max_exact, nn, large)


def make_consts():
    p = np.arange(128)[:, None]
    f = np.arange(128)[None, :]
    c = np.zeros((128, NCONST, 128), np.float32)
    c[:, C_ID] = (p == f)
    c[:, C_TRI] = (p <= f)
    c[:, C_AST] = (p > f)
    bd = (p // 32) == (f // 32)
    c[:, C_SUBD] = (f > p) & bd
    c[:, C_SLBD] = (f < p) & bd
    c[:, C_M1] = ((p // 32) % 2 == 1) & ((f // 32) == (p // 32) - 1)
    c[:, C_M2] = ((p // 32) >= 2) & ((f // 32) < 2)
    c[:, C_UI] = (f >= p)
    c[:, C_ONES] = 1.0
    return c


def prep_shared(inp):
    f32 = np.float32
    sh = {}
    w_ada = inp["w_ada"][0]
    wl = w_ada.reshape(KC, 128, 96, 128).transpose(2, 1, 0, 3)
    wl = wl.reshape(24, 4, 128, KC, 128).transpose(0, 2, 1, 3, 4)
    sh["wada_lay"] = np.ascontiguousarray(wl).reshape(24, 128, 4 * KC * 128)
    sh["bada_lay"] = np.ascontiguousarray(inp["b_ada"][0].reshape(96, 128).T)
    sh["consts"] = make_consts()
    w_in = inp["w_in"][0]
    o1 = 3072
    o2 = o1 + 4096
    o3 = o2 + 2048
    o5 = o3 + 32
    wm = np.empty((NH_M, 128, KC, 384), f32)
    for h in range(NH_M):
        cols = np.concatenate([np.arange(h * 128, (h + 1) * 128), 1024 + np.arange(h * 128, (h + 1) * 128),
                               2048 + np.arange(h * 128, (h + 1) * 128)])
        wm[h] = _klay(w_in[:, cols])
    sh["wmoba_lay"] = wm.reshape(NH_M, 128, KC * 384)
    rb = inp["rel_bias"]
    i = np.arange(128)[:, None]
    j = np.arange(128)[None, :]
    bd = _rel_bucket_np(j - i)
    bo = _rel_bucket_np(128 + j - i)
    mb = np.empty((128, 16, 128), f32)
    for h in range(NH_M):
        mb[:, h, :] = rb[bd, h]
        mb[:, 8 + h, :] = rb[bo, h]
    sh["mbias_lay"] = mb
    sh["c31_rep"] = np.ascontiguousarray(np.broadcast_to(rb[31][None, :], (128, NH_M))).astype(f32)
    wg = np.empty((8, 128, KC, 768), f32)
    for jh in range(8):
        cols = np.concatenate([o1 + np.arange(jh * 128, (jh + 1) * 128),
                               o1 + 1024 + np.arange(jh * 128, (jh + 1) * 128),
                               o1 + 2048 + np.arange(jh * 256, (jh + 1) * 256),
                               o2 + np.arange(jh * 256, (jh + 1) * 256)])
        wg[jh] = _klay(w_in[:, cols])
    sh["wgdn_lay"] = wg.reshape(8, 128, KC * 768)
    sh["wab_lay"] = _klay(w_in[:, o3:o5]).reshape(128, KC * 32)
    cw = inp["conv_w"][0]
    sh["conv_lay"] = np.ascontiguousarray(cw.reshape(4, 32, 128).transpose(2, 1, 0)).reshape(128, 128)
    sh["alog_rep"] = np.ascontiguousarray(np.broadcast_to(inp["a_log"][0][None, :], (128, HV))).astype(f32)
    sh["dtb_rep"] = np.ascontiguousarray(np.broadcast_to(inp["dt_bias"][0][None, :], (128, HV))).astype(f32)
    sh["normw_rep"] = np.ascontiguousarray(np.broadcast_to(inp["gdn_norm_w"][0][None, :], (128, 128))).astype(f32)
    wgate = w_in[:, o5:o5 + 4096]
    wgl = np.empty((16, 128, KC, 256), f32)
    for cc in range(16):
        cols = np.concatenate([np.arange(cc * 128, (cc + 1) * 128), 2048 + np.arange(cc * 128, (cc + 1) * 128)])
        wgl[cc] = _klay(wgate[:, cols])
    sh["wgate_lay"] = wgl.reshape(16, 128, KC * 256)
    wpm = inp["w_proj_moba"][0]
    wpg = inp["w_proj_gdn"][0]
    sh["wpm_lay"] = np.stack([_klay(wpm[:, cc * 128:(cc + 1) * 128]) for cc in range(16)]).reshape(16, 128, 8 * 128)
    sh["wpg_lay"] = np.stack([_klay(wpg[:, cc * 128:(cc + 1) * 128]) for cc in range(16)]).reshape(16, 128, 16 * 128)
    wo = inp["w_out"][0]
    sh["wout_lay"] = np.stack([_klay(wo[:, g * 512:(g + 1) * 512]) for g in range(4)]).reshape(4, 128, KC * 512)
    sh["ln_rep"] = np.stack([np.broadcast_to(inp[k][0][None, :], (128, D)) for k in
                             ("ln1_g", "ln1_b", "ln2_g", "ln2_b")]).astype(f32)
    wfi = inp["w_ffn_in"][0]
    wfl = np.empty((FC, 128, KC, 256), f32)
    for fc in range(FC):
        cols = np.concatenate([np.arange(fc * 128, (fc + 1) * 128), DFF + np.arange(fc * 128, (fc + 1) * 128)])
        wfl[fc] = _klay(wfi[:, cols])
    sh["wffi_lay"] = wfl.reshape(FC, 128, KC * 256)
    wfo = inp["w_ffn_out"][0]
    sh["wffo_lay"] = np.stack([_klay(wfo[:, oc * 128:(oc + 1) * 128]) for oc in range(16)]).reshape(16, 128, FC * 128)
    return sh


def prep_core(inp, b):
    x = inp["x"][b]
    return {
        "xT": np.ascontiguousarray(x.T).reshape(KC, 128, S),
        "x": np.ascontiguousarray(x).reshape(NT, 128, D),
        "c_lay": np.ascontiguousarray(inp["c"][b].reshape(KC, 128).T),
    }


_NC_CACHE = {}


def kernel(**inputs):
    inp = {k: np.asarray(v, dtype=np.float32) for k, v in inputs.items()}
    if "nc" not in _NC_CACHE:
        _NC_CACHE["nc"] = build_nc()
    nc = _NC_CACHE["nc"]
    shared = prep_shared(inp)
    in_maps = []
    for b in range(8):
        m = dict(shared)
        m.update(prep_core(inp, b))
        in_maps.append(m)
    res = run_bass_kernel_spmd(nc, in_maps, core_ids=list(range(8)))
    out = np.stack([np.asarray(r["out"]).reshape(S, D) for r in res.results], axis=0)
    return out.astype(np.float32)
```

```python
import math
from contextlib import ExitStack

import numpy as np
import concourse.bass as bass
import concourse.mybir as mybir
from concourse.bass_utils import run_bass_kernel_spmd

F32 = mybir.dt.float32
BF16 = mybir.dt.bfloat16
AF = mybir.ActivationFunctionType
ALU = mybir.AluOpType
AX = mybir.AxisListType

D = 2048
S = 2048
NT = 16
KC = 16
NH_M = 8
HV = 16
DFF = 5632
FC = DFF // 128
N_IN = 13344
ALPHA = 2.0 ** 0.25
LN_EPS = 1e-5
RMS_EPS = 1e-6
SCALE_M = 128 ** -0.5

C_ID, C_TRI, C_AST, C_SUBD, C_SLBD, C_M1, C_M2, C_UI, C_ONES = range(9)
NCONST = 9


class Reg:
    __slots__ = ("w", "r")

    def __init__(self):
        self.w = None
        self.r = {}


class KB:
    def __init__(self, nc, es):
        self.nc = nc
        self.engs = {"pe": nc.tensor, "act": nc.scalar, "dve": nc.vector, "pool": nc.gpsimd, "sp": nc.sync}
        self.semobj = {}
        self.cnt = {}
        self.waited = {}
        for name in self.engs:
            self.semobj[name] = es.enter_context(nc.semaphore("s_" + name))
            self.cnt[name] = 0
            self.waited[name] = {}
        self.nd = 12
        self.dnext = {"sp": 0, "pool": 0}
        self.dcnt = {}
        for q in ("sp", "pool"):
            for k in range(self.nd):
                key = ("d", q, k)
                self.semobj[key] = es.enter_context(nc.semaphore("d_%s%d" % (q, k)))
                self.dcnt[key] = 0
        self.n_wait = 0
        self.n_inst = 0

    def _collect(self, reads, writes):
        deps = {}
        for r in reads:
            if r.w is not None:
                k, v = r.w
                if deps.get(k, 0) < v:
                    deps[k] = v
        for w in writes:
            if w.w is not None:
                k, v = w.w
                if deps.get(k, 0) < v:
                    deps[k] = v
            for k, v in w.r.items():
                if deps.get(k, 0) < v:
                    deps[k] = v
        return deps

    def _wait(self, eng, deps, attach=False):
        wd = self.waited[eng]
        need = []
        for k, v in deps.items():
            if eng == "pe" and k == "pe":
                continue
            if wd.get(k, 0) < v:
                need.append((k, v))
                wd[k] = v
        pend = None
        if attach and need:
            pend = need.pop()
        for k, v in need:
            self.engs[eng].wait_ge(self.semobj[k], v)
            self.n_wait += 1
        return pend

    def _update(self, tok, reads, writes):
        k, v = tok
        for w in writes:
            w.w = tok
            w.r = {}
        for r in reads:
            if r.r.get(k, 0) < v:
                r.r[k] = v

    def op(self, eng, fn, reads=(), writes=()):
        pend = self._wait(eng, self._collect(reads, writes), attach=True)
        inst = fn(self.engs[eng])
        if pend is not None:
            inst._wait_ge(self.semobj[pend[0]], pend[1])
        inst.then_inc(self.semobj[eng], 1)
        self.cnt[eng] += 1
        self.n_inst += 1
        self._update((eng, self.cnt[eng]), reads, writes)

    def dma(self, q, out, in_, reads=(), writes=()):
        k = self.dnext[q]
        self.dnext[q] = (k + 1) % self.nd
        key = ("d", q, k)
        deps = self._collect(reads, writes)
        if self.dcnt[key] > 0:
            deps[key] = max(deps.get(key, 0), 16 * self.dcnt[key])
        pend = self._wait(q, deps, attach=True)
        inst = self.engs[q].dma_start(out=out, in_=in_)
        if pend is not None:
            inst._wait_ge(self.semobj[pend[0]], pend[1])
        inst.then_inc(self.semobj[key], 16)
        self.dcnt[key] += 1
        self.n_inst += 1
        tok = (key, 16 * self.dcnt[key])
        self._update(tok, reads, writes)
        return tok

    def barrier(self):
        deps = {}
        for name in self.engs:
            if self.cnt[name] > 0:
                deps[name] = self.cnt[name]
        for key, n in self.dcnt.items():
            if n > 0:
                deps[key] = 16 * n
        for eng in self.engs:
            self._wait(eng, dict(deps))

    def wait_tok(self, eng, tok):
        self._wait(eng, {tok[0]: tok[1]})


class T:
    def __init__(self, t, nreg=1):
        self.t = t
        self.regs = [Reg() for _ in range(nreg)]

    @property
    def r(self):
        return self.regs[0]


class View:
    def __init__(self, ap, reg):
        self.t = ap
        self.regs = [reg]

    @property
    def r(self):
        return self.regs[0]


def build_nc(stop_after=None, dbg=False):
    nc = bass.Bass("TRN2", target_bir_lowering=False)

    def din(name, shape, dt=F32):
        return nc.dram_tensor(name, list(shape), dt, kind="ExternalInput").ap()

    xT_d = din("xT", [KC, 128, S])
    x_d = din("x", [NT, 128, D])
    c_d = din("c_lay", [128, KC])
    wada_d = din("wada_lay", [24, 128, 4 * KC * 128])
    bada_d = din("bada_lay", [128, 96])
    consts_d = din("consts", [128, NCONST, 128])
    wmoba_d = din("wmoba_lay", [NH_M, 128, KC * 384])
    mbias_d = din("mbias_lay", [128, 16, 128])
    c31_d = din("c31_rep", [128, NH_M])
    wgdn_d = din("wgdn_lay", [8, 128, KC * 768])
    wab_d = din("wab_lay", [128, KC * 32])
    conv_d = din("conv_lay", [128, 32 * 4])
    alog_d = din("alog_rep", [128, HV])
    dtb_d = din("dtb_rep", [128, HV])
    normw_d = din("normw_rep", [128, 128])
    wgate_d = din("wgate_lay", [16, 128, KC * 256])
    wpm_d = din("wpm_lay", [16, 128, 8 * 128])
    wpg_d = din("wpg_lay", [16, 128, 16 * 128])
    wout_d = din("wout_lay", [4, 128, KC * 512])
    lnrep_d = din("ln_rep", [4, 128, D])
    wffi_d = din("wffi_lay", [FC, 128, KC * 256])
    wffo_d = din("wffo_lay", [16, 128, FC * 128])

    out_d = nc.dram_tensor("out", [NT, 128, D], F32, kind="ExternalOutput").ap()

    def dscr(name, shape, dt):
        return nc.dram_tensor(name, list(shape), dt, kind="Internal").ap()

    yaT_d = dscr("yaT_s", [NH_M, 128, S], BF16)
    ybT_d = dscr("ybT_s", [HV, 128, S], BF16)
    sgT_d = dscr("sgT_s", [32, 128, S], BF16)
    mgT_d = dscr("mgT_s", [KC, 128, S], BF16)
    x1_d = dscr("x1_s", [NT, 128, D], F32)
    y2_d = dscr("y2_s", [NT, 128, D], F32)
    gq_d = dscr("gq_s", [8, 128, S], BF16)
    gk_d = dscr("gk_s", [8, 128, S], BF16)
    gkt_d = dscr("gkt_s", [8, 128, NT * 128], BF16)
    gvt_d = dscr("gvt_s", [8, 128, NT * 256], BF16)
    gzs_d = dscr("gzs_s", [8, 128, NT * 256], BF16)
    dbg_outs = {}

    with ExitStack() as es:
        kb = KB(nc, es)

        def sb(name, shape, dt=F32, nreg=1):
            return T(es.enter_context(nc.sbuf_tensor(name, list(shape), dt)), nreg)

        banks = [T(es.enter_context(nc.psum_tensor("ps%d" % i, [128, 512], F32))) for i in range(8)]

        cst = sb("cst", [128, NCONST, 128])
        kb.dma("sp", cst.t[:], consts_d, writes=[cst.r])
        cstb = sb("cstb", [128, NCONST, 128], BF16)
        kb.op("dve", lambda e: e.tensor_copy(out=cstb.t[:], in_=cst.t[:]), reads=[cst.r], writes=[cstb.r])

        epsr = sb("epsr", [128, 2])
        kb.op("dve", lambda e: e.memset(epsr.t[:, 0:1], RMS_EPS), writes=[epsr.r])
        kb.op("dve", lambda e: e.memset(epsr.t[:, 1:2], LN_EPS), writes=[epsr.r])

        def CF(i):
            return cst.t[:, i, :]

        def CB(i):
            return cstb.t[:, i, :]

        c_sb = sb("c_sb", [128, KC])
        sc_bf = sb("sc_bf", [128, KC], BF16)
        kb.dma("sp", c_sb.t[:], c_d, writes=[c_sb.r])
        kb.op("act", lambda e: e.activation(out=sc_bf.t[:], in_=c_sb.t[:], func=AF.Silu),
              reads=[c_sb.r], writes=[sc_bf.r])
        bada = sb("bada", [128, 96])
        kb.dma("sp", bada.t[:], bada_d, writes=[bada.r])
        modT = sb("modT", [128, 96])
        hT = sb("hT", [128, KC, S], BF16, nreg=KC)

        es0 = ExitStack()

        def sb0(name, shape, dt=F32, nreg=1):
            return T(es0.enter_context(nc.sbuf_tensor(name, list(shape), dt)), nreg)
        wab = [sb0("wada%d" % i, [128, 4, KC, 128], BF16) for i in range(2)]
        xb = [sb0("xTb%d" % i, [128, S]) for i in range(2)]

        def mod_group(g, pm, col0):
            wb = wab[g % 2]
            kb.dma("pool", wb.t[:].rearrange("p a k c -> p (a k c)"), wada_d[g], writes=[wb.r])
            for a in range(4):
                for k in range(KC):
                    kb.op("pe", lambda e, a=a, k=k: e.matmul(
                        pm.t[:, col0 + a:col0 + a + 1], wb.t[:, a, k, :], sc_bf.t[:, k:k + 1],
                        start=(k == 0), stop=(k == KC - 1)),
                        reads=[wb.r, sc_bf.r], writes=[pm.r])
            kb.op("dve", lambda e: e.tensor_tensor(out=modT.t[:, 4 * g:4 * g + 4], in0=pm.t[:, col0:col0 + 4],
                                                   in1=bada.t[:, 4 * g:4 * g + 4], op=ALU.add),
                  reads=[bada.r], writes=[modT.r, pm.r])
        for g in range(8):
            mod_group(g, banks[0], 4 * g)
        kb.op("dve", lambda e: e.tensor_scalar_add(out=modT.t[:, 16:32], in0=modT.t[:, 16:32], scalar1=1.0),
              reads=[modT.r], writes=[modT.r])
        for k in range(KC):
            b_ = xb[k % 2]
            kb.dma("sp", b_.t[:], xT_d[k], writes=[b_.r])
            kb.op("act", lambda e, k=k, b_=b_: e.activation(
                out=hT.t[:, k, :], in_=b_.t[:], func=AF.Identity,
                scale=modT.t[:, 16 + k:17 + k], bias=modT.t[:, k:k + 1]),
                reads=[b_.r, modT.r], writes=[hT.regs[k]])

        if dbg:
            dbg_outs["modT"] = nc.dram_tensor("dbg_modT", [128, 96], F32, kind="ExternalOutput").ap()
            kb.dma("sp", dbg_outs["modT"], modT.t[:], reads=[modT.r])

        hT_all = hT.regs
        final_toks = []

        if stop_after != "p0":
            with ExitStack() as es1:
                def sb1(name, shape, dt=F32, nreg=1):
                    return T(es1.enter_context(nc.sbuf_tensor(name, list(shape), dt)), nreg)
                wm = [sb1("wm%d" % i, [128, KC, 384], BF16) for i in range(2)]
                qT = sb1("qT", [128, S], BF16, nreg=4)
                kT = sb1("kT", [128, S], BF16, nreg=4)
                V1 = sb1("V1", [128, NT, 129], BF16, nreg=5)
                kmf = sb1("kmf", [128, 8])
                kmT = sb1("kmT", [128, 8], BF16)
                PT = [sb1("PT%d" % i, [128, 256], BF16) for i in range(4)]
                tmpE = [sb1("tmpE%d" % i, [128, 256], BF16) for i in range(2)]
                acc = [sb1("acc%d" % i, [128, 129]) for i in range(2)]
                rt = [sb1("rt%d" % i, [128, 8]) for i in range(2)]
                cmpb = [sb1("cmp%d" % i, [128, 8, 8]) for i in range(2)]
                rank = [sb1("rank%d" % i, [128, 8]) for i in range(2)]
                sel = [sb1("sel%d" % i, [128, 8]) for i in range(2)]
                rec = [sb1("rec%d" % i, [128, 1]) for i in range(2)]
                ya = [sb1("ya%d" % i, [128, 128], BF16) for i in range(2)]
                yaT = [sb1("yaT%d" % i, [128, S], BF16) for i in range(2)]
                mb = sb1("mb", [128, 16, 128])
                mbe = sb1("mbe", [128, 16, 128])
                Ed = sb1("Ed", [128, 8, 128], BF16)
                Eo = sb1("Eo", [128, 8, 128], BF16)
                c31 = sb1("c31", [128, NH_M])

                kb.dma("sp", mb.t[:], mbias_d, writes=[mb.r])
                kb.dma("sp", c31.t[:], c31_d, writes=[c31.r])
                kb.op("act", lambda e: e.activation(out=mbe.t[:], in_=mb.t[:], func=AF.Exp),
                      reads=[mb.r], writes=[mbe.r])
                kb.op("dve", lambda e: e.tensor_tensor(
                    out=Ed.t[:], in0=mbe.t[:, 0:8, :],
                    in1=cst.t[:, C_UI:C_UI + 1, :].broadcast_to([128, 8, 128]), op=ALU.mult),
                    reads=[mbe.r, cst.r], writes=[Ed.r])
                kb.op("dve", lambda e: e.tensor_copy(out=Eo.t[:], in_=mbe.t[:, 8:16, :]),
                      reads=[mbe.r], writes=[Eo.r])
                kb.op("dve", lambda e: e.memset(V1.t[:, :, 128:129], 1.0), writes=[V1.regs[4]])

                kb.dma("pool", wm[0].t[:].rearrange("p k c -> p (k c)"), wmoba_d[0], writes=[wm[0].r])
                s_slot = 0
                o_slot = 0
                p_slot = 0
                for h in range(NH_M):
                    w = wm[h % 2]
                    if h + 1 < NH_M:
                        wn = wm[(h + 1) % 2]
                        kb.dma("pool", wn.t[:].rearrange("p k c -> p (k c)"), wmoba_d[h + 1], writes=[wn.r])
                    for which, dst in ((0, qT), (1, kT)):
                        for g in range(4):
                            pb = banks[g % 2]
                            for k in range(KC):
                                kb.op("pe", lambda e, k=k, g=g, pb=pb, which=which: e.matmul(
                                    pb.t[:, :], w.t[:, k, which * 128:(which + 1) * 128],
                                    hT.t[:, k, g * 512:(g + 1) * 512], start=(k == 0), stop=(k == KC - 1)),
                                    reads=[w.r, hT_all[k]], writes=[pb.r])
                            kb.op("act", lambda e, g=g, pb=pb, dst=dst: e.copy(
                                out=dst.t[:, g * 512:(g + 1) * 512], in_=pb.t[:, :]),
                                writes=[dst.regs[g], pb.r])
                    for g in range(4):
                        pb = banks[g % 2]
                        for tt in range(4):
                            t = g * 4 + tt
                            for k in range(KC):
                                kb.op("pe", lambda e, k=k, t=t, tt=tt, pb=pb: e.matmul(
                                    pb.t[:, tt * 128:(tt + 1) * 128], hT.t[:, k, t * 128:(t + 1) * 128],
                                    w.t[:, k, 256:384], start=(k == 0), stop=(k == KC - 1)),
                                    reads=[w.r, hT_all[k]], writes=[pb.r])
                        kb.op("dve", lambda e, g=g, pb=pb: e.tensor_copy(
                            out=V1.t[:, g * 4:(g + 1) * 4, 0:128],
                            in_=pb.t[:, :].rearrange("p (a c) -> p a c", c=128)),
                            writes=[V1.regs[g], pb.r])
                    kb.op("dve", lambda e: e.tensor_reduce(
                        out=kmf.t[:], in_=kT.t[:].rearrange("p (n b) -> p n b", b=256), axis=AX.X, op=ALU.add),
                        reads=kT.regs, writes=[kmf.r])
                    kb.op("dve", lambda e: e.tensor_scalar_mul(out=kmT.t[:], in0=kmf.t[:], scalar1=1.0 / 256.0),
                          reads=[kmf.r], writes=[kmT.r])
                    yT = yaT[h % 2]
                    for qb in range(8):
                        q0 = qb * 256
                        if qb >= 4:
                            for qi in range(2):
                                rp = banks[7]
                                kb.op("pe", lambda e, qi=qi: e.matmul(
                                    rp.t[:, qi * 8:qi * 8 + 8], qT.t[:, q0 + qi * 128:q0 + (qi + 1) * 128],
                                    kmT.t[:, :], start=True, stop=True),
                                    reads=[qT.regs[qb // 2], kmT.r], writes=[rp.r])
                                kb.op("dve", lambda e, qi=qi: e.tensor_copy(out=rt[qi].t[:], in_=rp.t[:, qi * 8:qi * 8 + 8]),
                                      writes=[rt[qi].r, rp.r])
                                kb.op("dve", lambda e, qi=qi: e.tensor_tensor(
                                    out=cmpb[qi].t[:, 0:qb, 0:qb],
                                    in0=rt[qi].t[:, 0:qb].unsqueeze(1).broadcast_to([128, qb, qb]),
                                    in1=rt[qi].t[:, 0:qb].unsqueeze(2).broadcast_to([128, qb, qb]),
                                    op=ALU.is_gt), reads=[rt[qi].r], writes=[cmpb[qi].r])
                                kb.op("dve", lambda e, qi=qi: e.tensor_reduce(
                                    out=rank[qi].t[:, 0:qb], in_=cmpb[qi].t[:, 0:qb, 0:qb], axis=AX.X, op=ALU.add),
                                    reads=[cmpb[qi].r], writes=[rank[qi].r])
                                kb.op("dve", lambda e, qi=qi: e.tensor_single_scalar(
                                    out=sel[qi].t[:, 0:qb], in_=rank[qi].t[:, 0:qb], scalar=2.5, op=ALU.is_lt),
                                    reads=[rank[qi].r], writes=[sel[qi].r])
                        order = [qb] + list(range(qb))
                        staged = []

                        def emit_scores(n):
                            nonlocal s_slot, p_slot
                            res = []
                            for kt in (2 * n, 2 * n + 1):
                                sbank = banks[s_slot % 4]
                                sreg = sbank.r
                                soff = 0
                                s_slot += 1
                                pt = PT[p_slot % 4]
                                p_slot += 1
                                te = tmpE[p_slot % 2]
                                ksl = slice(kt * 128, (kt + 1) * 128)
                                kreg = kT.regs[kt // 4]
                                qreg = qT.regs[qb // 2]
                                if n < qb:
                                    kb.op("pe", lambda e, ksl=ksl, soff=soff, sbank=sbank: e.matmul(
                                        sbank.t[:, soff:soff + 256], kT.t[:, ksl], qT.t[:, q0:q0 + 256],
                                        start=True, stop=True), reads=[kreg, qreg], writes=[sreg])
                                    if kt == 2 * qb - 1:
                                        kb.op("act", lambda e, soff=soff, sbank=sbank, te=te: e.activation(
                                            out=te.t[:, 0:128], in_=sbank.t[:, soff:soff + 128], func=AF.Exp,
                                            scale=SCALE_M), writes=[te.r, sreg])
                                        kb.op("dve", lambda e, te=te, pt=pt: e.tensor_tensor(
                                            out=pt.t[:, 0:128], in0=te.t[:, 0:128], in1=Eo.t[:, h, :], op=ALU.mult),
                                            reads=[te.r, Eo.r], writes=[pt.r])
                                        kb.op("act", lambda e, soff=soff, sbank=sbank, pt=pt: e.activation(
                                            out=pt.t[:, 128:256], in_=sbank.t[:, soff + 128:soff + 256], func=AF.Exp,
                                            scale=SCALE_M, bias=c31.t[:, h:h + 1]), reads=[c31.r], writes=[pt.r, sreg])
                                    else:
                                        kb.op("act", lambda e, soff=soff, sbank=sbank, pt=pt: e.activation(
                                            out=pt.t[:, :], in_=sbank.t[:, soff:soff + 256], func=AF.Exp,
                                            scale=SCALE_M, bias=c31.t[:, h:h + 1]), reads=[c31.r], writes=[pt.r, sreg])
                                elif kt == 2 * qb:
                                    kb.op("pe", lambda e, ksl=ksl, soff=soff, sbank=sbank: e.matmul(
                                        sbank.t[:, soff:soff + 256], kT.t[:, ksl], qT.t[:, q0:q0 + 256],
                                        start=True, stop=True), reads=[kreg, qreg], writes=[sreg])
                                    kb.op("act", lambda e, soff=soff, sbank=sbank, te=te: e.activation(
                                        out=te.t[:, :], in_=sbank.t[:, soff:soff + 256], func=AF.Exp,
                                        scale=SCALE_M), writes=[te.r, sreg])
                                    kb.op("dve", lambda e, te=te, pt=pt: e.tensor_tensor(
                                        out=pt.t[:, 0:128], in0=te.t[:, 0:128], in1=Ed.t[:, h, :], op=ALU.mult),
                                        reads=[te.r, Ed.r], writes=[pt.r])
                                    kb.op("dve", lambda e, te=te, pt=pt: e.tensor_tensor(
                                        out=pt.t[:, 128:256], in0=te.t[:, 128:256], in1=Eo.t[:, h, :], op=ALU.mult),
                                        reads=[te.r, Eo.r], writes=[pt.r])
                                else:
                                    kb.op("pe", lambda e, ksl=ksl, soff=soff, sbank=sbank: e.matmul(
                                        sbank.t[:, soff + 128:soff + 256], kT.t[:, ksl], qT.t[:, q0 + 128:q0 + 256],
                                        start=True, stop=True), reads=[kreg, qreg], writes=[sreg])
                                    kb.op("act", lambda e, soff=soff, sbank=sbank, te=te: e.activation(
                                        out=te.t[:, 128:256], in_=sbank.t[:, soff + 128:soff + 256], func=AF.Exp,
                                        scale=SCALE_M), writes=[te.r, sreg])
                                    kb.op("dve", lambda e, te=te, pt=pt: e.tensor_tensor(
                                        out=pt.t[:, 128:256], in0=te.t[:, 128:256], in1=Ed.t[:, h, :], op=ALU.mult),
                                        reads=[te.r, Ed.r], writes=[pt.r])
                                res.append((kt, pt))
                            return res

                        def emit_pv(n, pts):
                            nonlocal o_slot
                            for qi in range(2):
                                obank = banks[4 + o_slot % 3]
                                oreg = obank.r
                                ooff = 0
                                o_slot += 1
                                use = [(kt, pt) for (kt, pt) in pts if kt <= 2 * qb + qi]
                                for i, (kt, pt) in enumerate(use):
                                    kb.op("pe", lambda e, kt=kt, pt=pt, i=i, qi=qi, ooff=ooff, obank=obank, use=use: e.matmul(
                                        obank.t[:, ooff:ooff + 129], pt.t[:, qi * 128:(qi + 1) * 128], V1.t[:, kt, :],
                                        start=(i == 0), stop=(i == len(use) - 1)),
                                        reads=[pt.r, V1.regs[kt // 4], V1.regs[4]], writes=[oreg])
                                if n == qb:
                                    kb.op("act", lambda e, qi=qi, ooff=ooff, obank=obank: e.copy(
                                        out=acc[qi].t[:, :], in_=obank.t[:, ooff:ooff + 129]),
                                        writes=[acc[qi].r, oreg])
                                elif qb >= 4:
                                    kb.op("dve", lambda e, qi=qi, ooff=ooff, obank=obank, n=n: e.scalar_tensor_tensor(
                                        out=acc[qi].t[:, :], in0=obank.t[:, ooff:ooff + 129], scalar=sel[qi].t[:, n:n + 1],
                                        in1=acc[qi].t[:, :], op0=ALU.mult, op1=ALU.add),
                                        reads=[sel[qi].r], writes=[acc[qi].r, oreg])
                                else:
                                    kb.op("dve", lambda e, qi=qi, ooff=ooff, obank=obank: e.tensor_tensor(
                                        out=acc[qi].t[:, :], in0=obank.t[:, ooff:ooff + 129], in1=acc[qi].t[:, :], op=ALU.add),
                                        writes=[acc[qi].r, oreg])

                        prev = None
                        for n in order:
                            cur = (n, emit_scores(n))
                            if prev is not None:
                                emit_pv(*prev)
                            prev = cur
                        emit_pv(*prev)
                        for qi in range(2):
                            qt = 2 * qb + qi
                            kb.op("dve", lambda e, qi=qi: e.reciprocal(out=rec[qi].t[:], in_=acc[qi].t[:, 128:129]),
                                  reads=[acc[qi].r], writes=[rec[qi].r])
                            kb.op("dve", lambda e, qi=qi: e.tensor_scalar_mul(
                                out=ya[qi].t[:], in0=acc[qi].t[:, 0:128], scalar1=rec[qi].t[:, 0:1]),
                                reads=[acc[qi].r, rec[qi].r], writes=[ya[qi].r])
                            tb = banks[7]
                            kb.op("pe", lambda e, qi=qi: e.matmul(
                                tb.t[:, qi * 128:(qi + 1) * 128], ya[qi].t[:], CB(C_ID), start=True, stop=True),
                                reads=[ya[qi].r, cstb.r], writes=[tb.r])
                            kb.op("act", lambda e, qi=qi, qt=qt: e.copy(
                                out=yT.t[:, qt * 128:(qt + 1) * 128], in_=tb.t[:, qi * 128:(qi + 1) * 128]),
                                writes=[yT.r, tb.r])
                    kb.dma("sp", yaT_d[h], yT.t[:], reads=[yT.r])
                    mod_group(8 + 2 * h, banks[7], 384)
                    mod_group(9 + 2 * h, banks[7], 384)
                kb.op("dve", lambda e: e.tensor_scalar_add(out=modT.t[:, 64:80], in0=modT.t[:, 64:80], scalar1=1.0),
                      reads=[modT.r], writes=[modT.r])
        es0.close()


        kb.barrier()
        bankctr = [0]

        def nb():
            b = banks[bankctr[0] % 8]
            bankctr[0] += 1
            return b

        if stop_after not in ("p0", "p1a"):
            with ExitStack() as es2:
                def sb2(name, shape, dt=F32, nreg=1):
                    return T(es2.enter_context(nc.sbuf_tensor(name, list(shape), dt)), nreg)
                print("sbuf remaining at GDN start", nc.sbuf_bytes_remaining)
                beta = sb2("beta", [128, NT, HV])
                lb = sb2("lb", [128, NT, HV])
                gg = sb2("gg", [128, NT, HV])
                nea = sb2("nea", [128, HV])
                egc = sb2("egc", [128, NT, HV])
                cdd = sb2("cdd", [128, NT, HV])
                ekd = sb2("ekd", [128, NT, HV])
                bkk = sb2("bkk", [128, NT, HV])
                ghl = sb2("ghl", [128, 2, NT, HV])
                lhl = sb2("lhl", [128, 2, NT, HV])
                cw = sb2("cw", [128, 32, 4])
                alog = sb2("alog", [128, HV])
                dtb = sb2("dtb", [128, HV])
                normw = sb2("normw", [128, 128])
                es2a = ExitStack()

                def sb2a(name, shape, dt=F32, nreg=1):
                    return T(es2a.enter_context(nc.sbuf_tensor(name, list(shape), dt)), nreg)
                wabt = sb2a("wabt", [128, KC, 32], BF16)
                kb.dma("pool", wabt.t[:].rearrange("p k c -> p (k c)"), wab_d, writes=[wabt.r])
                kb.dma("sp", cw.t[:].rearrange("p g t -> p (g t)"), conv_d, writes=[cw.r])
                kb.dma("sp", alog.t[:], alog_d, writes=[alog.r])
                kb.dma("sp", dtb.t[:], dtb_d, writes=[dtb.r])
                kb.dma("sp", normw.t[:], normw_d, writes=[normw.r])
                ab = sb2a("ab", [128, NT, 32])
                bk_ = nb()
                for t in range(NT):
                    for k in range(KC):
                        kb.op("pe", lambda e, t=t, k=k: e.matmul(
                            bk_.t[:, t * 32:(t + 1) * 32], hT.t[:, k, t * 128:(t + 1) * 128], wabt.t[:, k, :],
                            start=(k == 0), stop=(k == KC - 1)), reads=[wabt.r, hT_all[k]], writes=[bk_.r])
                kb.op("dve", lambda e: e.tensor_copy(out=ab.t[:], in_=bk_.t[:, :].rearrange("p (t c) -> p t c", c=32)),
                      writes=[ab.r, bk_.r])
                tmpa = sb2a("tmpa", [128, NT, HV])
                gcs = sb2a("gcs", [128, NT, 32])
                kb.op("act", lambda e: e.activation(out=beta.t[:], in_=ab.t[:, :, 0:16], func=AF.Sigmoid),
                      reads=[ab.r], writes=[beta.r])
                kb.op("act", lambda e: e.activation(out=lb.t[:], in_=beta.t[:], func=AF.Ln),
                      reads=[beta.r], writes=[lb.r])
                kb.op("dve", lambda e: e.tensor_tensor(
                    out=tmpa.t[:], in0=ab.t[:, :, 16:32], in1=dtb.t[:].unsqueeze(1).broadcast_to([128, NT, HV]),
                    op=ALU.add), reads=[ab.r, dtb.r], writes=[tmpa.r])
                kb.op("act", lambda e: e.activation(out=tmpa.t[:], in_=tmpa.t[:], func=AF.Exp),
                      reads=[tmpa.r], writes=[tmpa.r])
                kb.op("act", lambda e: e.activation(out=tmpa.t[:], in_=tmpa.t[:], func=AF.Ln, bias=1.0),
                      reads=[tmpa.r], writes=[tmpa.r])
                kb.op("act", lambda e: e.activation(out=nea.t[:], in_=alog.t[:], func=AF.Exp),
                      reads=[alog.r], writes=[nea.r])
                kb.op("dve", lambda e: e.scalar_tensor_tensor(
                    out=gg.t[:], in0=tmpa.t[:], scalar=-1.0, in1=nea.t[:].unsqueeze(1).broadcast_to([128, NT, HV]),
                    op0=ALU.mult, op1=ALU.mult), reads=[tmpa.r, nea.r], writes=[gg.r])
                bk_ = nb()
                for t in range(NT):
                    kb.op("pe", lambda e, t=t: e.matmul(bk_.t[:, t * 32:t * 32 + 16], CF(C_TRI), gg.t[:, t, :],
                                                        start=True, stop=True), reads=[cst.r, gg.r], writes=[bk_.r])
                    kb.op("pe", lambda e, t=t: e.matmul(bk_.t[:, t * 32 + 16:t * 32 + 32], CF(C_ONES), gg.t[:, t, :],
                                                        start=True, stop=True), reads=[cst.r, gg.r], writes=[bk_.r])
                kb.op("dve", lambda e: e.tensor_copy(out=gcs.t[:], in_=bk_.t[:, :].rearrange("p (t c) -> p t c", c=32)),
                      writes=[gcs.r, bk_.r])
                kb.op("act", lambda e: e.activation(out=egc.t[:], in_=gcs.t[:, :, 0:16], func=AF.Exp),
                      reads=[gcs.r], writes=[egc.r])
                kb.op("act", lambda e: e.activation(out=cdd.t[:], in_=gcs.t[:, :, 16:32], func=AF.Exp),
                      reads=[gcs.r], writes=[cdd.r])
                kb.op("dve", lambda e: e.tensor_tensor(out=ekd.t[:], in0=gcs.t[:, :, 16:32], in1=gcs.t[:, :, 0:16],
                                                       op=ALU.subtract), reads=[gcs.r], writes=[ekd.r])
                kb.op("act", lambda e: e.activation(out=ekd.t[:], in_=ekd.t[:], func=AF.Exp),
                      reads=[ekd.r], writes=[ekd.r])
                kb.op("dve", lambda e: e.tensor_tensor(out=bkk.t[:], in0=beta.t[:], in1=egc.t[:], op=ALU.mult),
                      reads=[beta.r, egc.r], writes=[bkk.r])

                tmpb = sb2a("tmpb", [128, NT, HV], BF16)
                for src, dst in ((gg, ghl), (lb, lhl)):
                    kb.op("dve", lambda e, src=src: e.tensor_copy(out=tmpb.t[:], in_=src.t[:]), reads=[src.r], writes=[tmpb.r])
                    kb.op("dve", lambda e, dst=dst: e.tensor_copy(out=dst.t[:, 0, :, :], in_=tmpb.t[:]), reads=[tmpb.r], writes=[dst.r])
                    kb.op("dve", lambda e, src=src, dst=dst: e.tensor_tensor(out=dst.t[:, 1, :, :], in0=src.t[:], in1=dst.t[:, 0, :, :],
                                                                             op=ALU.subtract), reads=[src.r], writes=[dst.r])
                kb.barrier()
                es2a.close()
                es2b = ExitStack()

                def sb2b(name, shape, dt=F32, nreg=1):
                    return T(es2b.enter_context(nc.sbuf_tensor(name, list(shape), dt)), nreg)
                wgb = [sb2b("wg%d" % i, [128, KC, 768], BF16) for i in range(2)]
                xpre = sb2b("xpre", [128, 4, 3 + S], BF16, nreg=4)
                kb.op("dve", lambda e: e.memset(xpre.t[:, :, 0:3], 0.0), writes=xpre.regs)
                dg = sb2b("dg", [128, 16, 128], BF16, nreg=16)
                qTg = sb2b("qTg", [128, S], BF16)
                kTg = sb2b("kTg", [128, S], BF16)
                ktok = sb2b("ktok", [128, NT, 128], BF16)
                vtok = sb2b("vtok", [128, NT, 256], BF16)
                zs = sb2b("zs", [128, NT, 256], BF16)
                cs = [sb2b("cs%d" % i, [128, 512]) for i in range(2)]
                sq = sb2b("sq", [128, 512], BF16)
                rinv = sb2b("rinv", [128, 512])
                vTt = [sb2b("vTt%d" % i, [128, 512], BF16) for i in range(2)]
                zt = [sb2b("zt%d" % i, [128, 512]) for i in range(2)]

                kb.dma("pool", wgb[0].t[:].rearrange("p k c -> p (k c)"), wgdn_d[0], writes=[wgb[0].r])
                ev = [0]

                def evac(out, in_, reads, writes):
                    ev[0] += 1
                    if ev[0] % 2:
                        kb.op("act", lambda e: e.copy(out=out, in_=in_), reads=reads, writes=writes)
                    else:
                        kb.op("dve", lambda e: e.tensor_copy(out=out, in_=in_), reads=reads, writes=writes)

                for j in range(8):
                    w = wgb[j % 2]
                    if j + 1 < 8:
                        wn_ = wgb[(j + 1) % 2]
                        kb.dma("pool", wn_.t[:].rearrange("p k c -> p (k c)"), wgdn_d[j + 1], writes=[wn_.r])
                    grps = [j, 8 + j, 16 + 2 * j, 17 + 2 * j]
                    for gi in range(4):
                        for tap in range(4):
                            kb.op("dve", lambda e, gi=gi, tap=tap: e.tensor_scalar_mul(
                                out=dg.t[:, gi * 4 + tap, :], in0=CF(C_ID), scalar1=cw.t[:, grps[gi], tap:tap + 1]),
                                reads=[cst.r, cw.r], writes=[dg.regs[gi * 4 + tap]])
                    for gi in range(4):
                        for g in range(4):
                            pb = nb()
                            for k in range(KC):
                                kb.op("pe", lambda e, k=k, g=g, gi=gi, pb=pb: e.matmul(
                                    pb.t[:, :], w.t[:, k, gi * 128:(gi + 1) * 128], hT.t[:, k, g * 512:(g + 1) * 512],
                                    start=(k == 0), stop=(k == KC - 1)), reads=[w.r, hT_all[k]], writes=[pb.r])
                            evac(xpre.t[:, gi, 3 + g * 512:3 + (g + 1) * 512], pb.t[:, :], [], [xpre.regs[gi], pb.r])
                    def z_unit(t2, w=w):
                        pb = nb()
                        for a in range(2):
                            t = t2 * 2 + a
                            for k in range(KC):
                                kb.op("pe", lambda e, k=k, t=t, a=a, pb=pb: e.matmul(
                                    pb.t[:, a * 256:(a + 1) * 256], hT.t[:, k, t * 128:(t + 1) * 128], w.t[:, k, 512:768],
                                    start=(k == 0), stop=(k == KC - 1)), reads=[w.r, hT_all[k]], writes=[pb.r])
                        z_ = zt[t2 % 2]
                        kb.op("act", lambda e, pb=pb, z_=z_: e.activation(out=z_.t[:], in_=pb.t[:, :], func=AF.Silu),
                              writes=[z_.r, pb.r])
                        kb.op("dve", lambda e, z_=z_, t2=t2: e.tensor_tensor(
                            out=zs.t[:, t2 * 2:t2 * 2 + 2, :].rearrange("p a (h c) -> p (a h) c", c=128),
                            in0=z_.t[:].rearrange("p (a c) -> p a c", c=128),
                            in1=normw.t[:].unsqueeze(1).broadcast_to([128, 4, 128]), op=ALU.mult),
                            reads=[z_.r, normw.r], writes=[zs.r])
                    for gi in range(4):
                        for g in range(4):
                            pb = nb()
                            for tap in range(4):
                                kb.op("pe", lambda e, tap=tap, g=g, gi=gi, pb=pb: e.matmul(
                                    pb.t[:, :], dg.t[:, gi * 4 + tap, :], xpre.t[:, gi, g * 512 + tap:g * 512 + tap + 512],
                                    start=(tap == 0), stop=(tap == 3)), reads=[dg.regs[gi * 4 + tap], xpre.regs[gi]], writes=[pb.r])
                            if gi < 2:
                                c_ = cs[g % 2]
                                kb.op("act", lambda e, pb=pb, c_=c_: e.activation(out=c_.t[:], in_=pb.t[:, :], func=AF.Silu),
                                      writes=[c_.r, pb.r])
                                kb.op("dve", lambda e, c_=c_: e.tensor_tensor(out=sq.t[:], in0=c_.t[:], in1=c_.t[:], op=ALU.mult),
                                      reads=[c_.r], writes=[sq.r])
                                pb2 = nb()
                                kb.op("pe", lambda e, pb2=pb2: e.matmul(pb2.t[:, :], CB(C_ONES), sq.t[:], start=True, stop=True),
                                      reads=[cstb.r, sq.r], writes=[pb2.r])
                                kb.op("act", lambda e, pb2=pb2: e.activation(
                                    out=rinv.t[:], in_=pb2.t[:, :], func=AF.Ln, bias=epsr.t[:, 0:1]),
                                    reads=[epsr.r], writes=[rinv.r, pb2.r])
                                kb.op("act", lambda e: e.activation(out=rinv.t[:], in_=rinv.t[:], func=AF.Exp, scale=-0.5),
                                      reads=[rinv.r], writes=[rinv.r])
                                dstT = qTg if gi == 0 else kTg
                                scl = SCALE_M if gi == 0 else 1.0
                                kb.op("dve", lambda e, c_=c_, dstT=dstT, scl=scl, g=g: e.scalar_tensor_tensor(
                                    out=dstT.t[:, g * 512:(g + 1) * 512], in0=c_.t[:], scalar=scl, in1=rinv.t[:],
                                    op0=ALU.mult, op1=ALU.mult), reads=[c_.r, rinv.r], writes=[dstT.r])
                                if gi == 1:
                                    pb3 = nb()
                                    for tt in range(4):
                                        t = g * 4 + tt
                                        kb.op("pe", lambda e, tt=tt, t=t, pb3=pb3: e.matmul(
                                            pb3.t[:, tt * 128:(tt + 1) * 128], kTg.t[:, t * 128:(t + 1) * 128], CB(C_ID),
                                            start=True, stop=True), reads=[kTg.r, cstb.r], writes=[pb3.r])
                                    evac(ktok.t[:, g * 4:(g + 1) * 4, :], pb3.t[:, :].rearrange("p (a c) -> p a c", c=128),
                                         [], [ktok.r, pb3.r])
                            else:
                                vt_ = vTt[g % 2]
                                kb.op("act", lambda e, pb=pb, vt_=vt_: e.activation(out=vt_.t[:], in_=pb.t[:, :], func=AF.Silu),
                                      writes=[vt_.r, pb.r])
                                pb3 = nb()
                                for tt in range(4):
                                    kb.op("pe", lambda e, tt=tt, pb3=pb3, vt_=vt_: e.matmul(
                                        pb3.t[:, tt * 128:(tt + 1) * 128], vt_.t[:, tt * 128:(tt + 1) * 128], CB(C_ID),
                                        start=True, stop=True), reads=[vt_.r, cstb.r], writes=[pb3.r])
                                evac(vtok.t[:, g * 4:(g + 1) * 4, (gi - 2) * 128:(gi - 1) * 128],
                                     pb3.t[:, :].rearrange("p (a c) -> p a c", c=128), [], [vtok.r, pb3.r])

                            if (gi * 4 + g) % 2 == 1:
                                z_unit((gi * 4 + g) // 2)
                    kb.dma("sp", gq_d[j], qTg.t[:], reads=[qTg.r])
                    kb.dma("sp", gk_d[j], kTg.t[:], reads=[kTg.r])
                    kb.dma("sp", gkt_d[j], ktok.t[:].rearrange("p a c -> p (a c)"), reads=[ktok.r])
                    kb.dma("sp", gvt_d[j], vtok.t[:].rearrange("p a c -> p (a c)"), reads=[vtok.r])
                    kb.dma("sp", gzs_d[j], zs.t[:].rearrange("p a c -> p (a c)"), reads=[zs.r])

                kb.barrier()
                es2b.close()
                class HS_:
                    pass
                INB = []
                for i in range(3):
                    b_ = HS_()
                    b_.q = sb2("inq%d" % i, [128, 256], BF16)
                    b_.k = sb2("ink%d" % i, [128, 256], BF16)
                    b_.kt = sb2("inkt%d" % i, [128, 2, 128], BF16)
                    b_.vt = sb2("invt%d" % i, [128, 2, 256], BF16)
                    b_.zs = sb2("inzs%d" % i, [128, 2, 256], BF16)
                    INB.append(b_)

                def load_inputs(gcp):
                    j_, cp_ = divmod(gcp, 8)
                    b_ = INB[gcp % 3]
                    kb.dma("sp", b_.q.t[:], gq_d[j_][:, cp_ * 256:(cp_ + 1) * 256], writes=[b_.q.r])
                    kb.dma("sp", b_.k.t[:], gk_d[j_][:, cp_ * 256:(cp_ + 1) * 256], writes=[b_.k.r])
                    kb.dma("sp", b_.kt.t[:].rearrange("p a c -> p (a c)"), gkt_d[j_][:, cp_ * 256:(cp_ + 1) * 256], writes=[b_.kt.r])
                    kb.dma("sp", b_.vt.t[:].rearrange("p a c -> p (a c)"), gvt_d[j_][:, cp_ * 512:(cp_ + 1) * 512], writes=[b_.vt.r])
                    kb.dma("sp", b_.zs.t[:].rearrange("p a c -> p (a c)"), gzs_d[j_][:, cp_ * 512:(cp_ + 1) * 512], writes=[b_.zs.r])
                KK = [sb2("kkm%d" % i, [128, 4, 128]) for i in range(2)]
                QK = [sb2("qkm%d" % i, [128, 128]) for i in range(2)]

                class HS:
                    pass
                SLOT = []
                for sl in range(2):
                    o_ = HS()
                    n = "s%d_" % sl
                    o_.LAp = sb2(n + "LAp", [128, 2, 2, 256], BF16)
                    o_.LDp = sb2(n + "LDp", [128, 2, 2, 256], BF16)
                    o_.LEp = sb2(n + "LEp", [128, 2, 2, 128], BF16)
                    o_.LFp = sb2(n + "LFp", [128, 2, 256], BF16)
                    o_.LGp = sb2(n + "LGp", [128, 2, 2, 128], BF16)
                    o_.LHp = sb2(n + "LHp", [128, 2, 256], BF16)
                    o_.TTp = [sb2(n + "TTp%d" % i, [128, 2, 128], BF16) for i in range(2)]
                    SLOT.append(o_)
                IT = []
                for sl in range(2):
                    row = []
                    for vh in range(2):
                        o_ = HS()
                        n = "i%d%d_" % (sl, vh)
                        o_.Bh = sb2(n + "Bh", [128, 2, 128], BF16); o_.Dl = sb2(n + "Dl", [128, 2, 128], BF16)
                        o_.E3 = sb2(n + "E3", [128, 384])
                        o_.V32 = sb2(n + "V32", [128, 512])
                        o_.Vall = sb2(n + "Vall", [128, 2, 512], BF16)
                        SP_ = SLOT[sl]
                        o_.LA = View(SP_.LAp.t[:, :, vh, :], SP_.LAp.r); o_.LB = sb2(n + "LB", [128, 2, 384], BF16)
                        o_.LC = sb2(n + "LC", [128, 2, 384], BF16); o_.LD = View(SP_.LDp.t[:, :, vh, :], SP_.LDp.r)
                        o_.LE = View(SP_.LEp.t[:, :, vh, :], SP_.LEp.r); o_.LF = View(SP_.LFp.t[:, vh, :], SP_.LFp.r)
                        o_.LG = View(SP_.LGp.t[:, :, vh, :], SP_.LGp.r); o_.LH = View(SP_.LHp.t[:, vh, :], SP_.LHp.r)
                        o_.kbg = sb2(n + "kbg", [128, 128], BF16)
                        for nm in ("attnT", "vbeta", "kd", "nwT"):
                            setattr(o_, nm, [sb2(n + nm + str(i), [128, 128], BF16) for i in range(2)])
                        o_.TT = [View(SP_.TTp[i].t[:, vh, :], SP_.TTp[i].r) for i in range(2)]
                        row.append(o_)
                    IT.append(row)
                HQ = []
                for vh in range(2):
                    o_ = HS()
                    n = "q%d_" % vh
                    o_.vnew = sb2(n + "vnew", [128, 128], BF16)
                    o_.S32 = sb2(n + "S32", [128, 128]); o_.Sbf = sb2(n + "Sbf", [128, 128], BF16)
                    o_.tmpo = sb2(n + "tmpo", [128, 128]); o_.o = sb2(n + "o", [128, 128]); o_.junk = o_.tmpo
                    o_.ssq = sb2(n + "ssq", [128, 1]); o_.r1 = sb2(n + "r1", [128, 1]); o_.r2 = sb2(n + "r2", [128, 1])
                    o_.yb = sb2(n + "yb", [128, 128], BF16)
                    o_.ybT = sb2(n + "ybT", [128, 512], BF16)
                    HQ.append(o_)

                if True:
                    IDF = CF(C_ID)
                    IDB = CB(C_ID)
                    ASTB = CB(C_AST)
                    def make_stages(j, c, b_):
                        csl = slice(c * 128, (c + 1) * 128)
                        cl = c % 2
                        lsl = slice(cl * 128, (cl + 1) * 128)
                        pp = (c // 2) % 2
                        kkm = KK[c % 2]
                        qkm = QK[c % 2]

                        def st_shared():
                          pk = nb()
                          if True:
                            kb.op("pe", lambda e, pk=pk: e.matmul(pk.t[:, 0:128], b_.k.t[:, lsl], b_.k.t[:, lsl], start=True, stop=True),
                                  reads=[b_.k.r], writes=[pk.r])
                            kb.op("pe", lambda e, pk=pk: e.matmul(pk.t[:, 128:256], b_.k.t[:, lsl], b_.q.t[:, lsl], start=True, stop=True),
                                  reads=[b_.k.r, b_.q.r], writes=[pk.r])
                            kb.op("dve", lambda e, pk=pk: e.tensor_tensor(
                                out=kkm.t[:], in0=pk.t[:, 0:128].unsqueeze(1).broadcast_to([128, 4, 128]),
                                in1=cst.t[:, C_SUBD:C_SUBD + 4, :], op=ALU.mult), reads=[cst.r], writes=[kkm.r, pk.r])
                            kb.op("dve", lambda e, pk=pk: e.tensor_tensor(
                                out=qkm.t[:], in0=pk.t[:, 128:256], in1=CF(C_UI), op=ALU.mult),
                                reads=[cst.r], writes=[qkm.r, pk.r])

                        pdb = {}

                        def st_p0(vh):
                            H = IT[c % 2][vh]; Q = HQ[vh]
                            hv = 2 * j + vh
                            kb.op("dve", lambda e: e.tensor_scalar_mul(out=H.Bh.t[:, 0, :], in0=CF(C_TRI), scalar1=ghl.t[:, 0, c, hv:hv + 1]),
                                  reads=[cst.r, ghl.r], writes=[H.Bh.r])
                            kb.op("dve", lambda e: e.tensor_scalar_mul(out=H.Bh.t[:, 1, :], in0=CF(C_TRI), scalar1=ghl.t[:, 1, c, hv:hv + 1]),
                                  reads=[cst.r, ghl.r], writes=[H.Bh.r])
                            kb.op("act", lambda e: e.activation(out=H.Dl.t[:, 0, :], in_=CF(C_ID), func=AF.Identity, scale=lhl.t[:, 0, c, hv:hv + 1]),
                                  reads=[cst.r, lhl.r], writes=[H.Dl.r])
                            kb.op("act", lambda e: e.activation(out=H.Dl.t[:, 1, :], in_=CF(C_ID), func=AF.Identity, scale=lhl.t[:, 1, c, hv:hv + 1]),
                                  reads=[cst.r, lhl.r], writes=[H.Dl.r])
                            kb.op("act", lambda e: e.activation(out=H.kbg.t[:], in_=b_.kt.t[:, cl, :], func=AF.Identity, scale=bkk.t[:, c, hv:hv + 1]),
                                  reads=[b_.kt.r, bkk.r], writes=[H.kbg.r])
                            kb.op("act", lambda e: e.activation(out=H.vbeta[pp].t[:], in_=b_.vt.t[:, cl, vh * 128:(vh + 1) * 128],
                                                                func=AF.Identity, scale=beta.t[:, c, hv:hv + 1]),
                                  reads=[b_.vt.r, beta.r], writes=[H.vbeta[pp].r])
                            kb.op("act", lambda e: e.activation(out=H.kd[pp].t[:], in_=b_.kt.t[:, cl, :], func=AF.Identity, scale=ekd.t[:, c, hv:hv + 1]),
                                  reads=[b_.kt.r, ekd.r], writes=[H.kd[pp].r])

                        def st_p1(vh):
                            H = IT[c % 2][vh]; Q = HQ[vh]
                            pd = nb()
                            pdb[vh] = pd
                            Bh_, Bl_ = H.Bh.t[:, 0, :], H.Bh.t[:, 1, :]
                            Dh_, Dl_ = H.Dl.t[:, 0, :], H.Dl.t[:, 1, :]
                            rB = [cstb.r, H.Bh.r]
                            rD = [cstb.r, H.Dl.r]
                            mm(pd, 0, ASTB, Bh_, rB, True, False)
                            mm(pd, 0, ASTB, Bl_, rB, False, True)
                            mm(pd, 128, ASTB, Bh_, rB, True, False)
                            mm(pd, 128, ASTB, Bl_, rB, False, False)
                            mm(pd, 128, ASTB, Dh_, rD, False, False)
                            mm(pd, 128, ASTB, Dl_, rD, False, True)
                            mm(pd, 256, Bh_, ASTB, rB, True, False)
                            mm(pd, 256, Bl_, ASTB, rB, False, False)
                            mm(pd, 256, Dh_, ASTB, rD, False, False)
                            mm(pd, 256, Dl_, ASTB, rD, False, True)

                        def st_p2(vh):
                            H = IT[c % 2][vh]; Q = HQ[vh]
                            pd = pdb[vh]
                            kb.op("act", lambda e: e.activation(out=H.E3.t[:], in_=pd.t[:, 0:384], func=AF.Exp),
                                  writes=[H.E3.r, pd.r])

                        def st_p3(vh):
                            H = IT[c % 2][vh]; Q = HQ[vh]
                            kb.op("dve", lambda e: e.tensor_tensor(out=H.attnT[pp].t[:], in0=H.E3.t[:, 0:128], in1=qkm.t[:], op=ALU.mult),
                                  reads=[H.E3.r, qkm.r], writes=[H.attnT[pp].r])
                            V32 = H.V32.t[:].rearrange("p (a c) -> p a c", c=128)
                            kb.op("dve", lambda e: e.scalar_tensor_tensor(
                                out=V32[:, 0, :], in0=H.E3.t[:, 128:256], scalar=-1.0, in1=kkm.t[:, 0, :], op0=ALU.mult, op1=ALU.mult),
                                reads=[H.E3.r, kkm.r], writes=[H.V32.r])
                            kb.op("dve", lambda e: e.scalar_tensor_tensor(
                                out=V32[:, 1:4, :], in0=H.E3.t[:, 256:384].unsqueeze(1).broadcast_to([128, 3, 128]), scalar=-1.0,
                                in1=kkm.t[:, 1:4, :], op0=ALU.mult, op1=ALU.mult),
                                reads=[H.E3.r, kkm.r], writes=[H.V32.r])

                        def st_p4(vh):
                            H = IT[c % 2][vh]; Q = HQ[vh]
                            kb.op("act", lambda e: e.copy(out=H.Vall.t[:, 0, :], in_=H.V32.t[:]),
                                  reads=[H.V32.r], writes=[H.Vall.r])
                            kb.op("dve", lambda e: e.tensor_tensor(out=H.Vall.t[:, 1, :], in0=H.V32.t[:], in1=H.Vall.t[:, 0, :], op=ALU.subtract),
                                  reads=[H.V32.r], writes=[H.Vall.r])

                        def mm(pb, lo, lhsT, rhs, rds, start=True, stop=True):
                            kb.op("pe", lambda e: e.matmul(pb.t[:, lo:lo + 128], lhsT, rhs, start=start, stop=stop),
                                  reads=rds, writes=[pb.r])

                        def mmf(pb, lo, lhsT, rhs, rds, start=True, stop=True):
                            mm(pb, lo, lhsT, rhs, rds, start, stop)

                        def mm3(pb, lo, A, B, rds, first=True, last=True):
                            (Ah, Al), (Bh, Bl) = A, B
                            mm(pb, lo, Ah, Bh, rds, first, False)
                            mm(pb, lo, Ah, Bl, rds, False, False)
                            mm(pb, lo, Al, Bh, rds, False, last)

                        def mmI(pb, lo, B, rds, first, last):
                            Bh, Bl = B
                            mm(pb, lo, IDB, Bh, rds + [cstb.r], first, False)
                            mm(pb, lo, IDB, Bl, rds + [cstb.r], False, last)

                        def mmT(pb, lo, A, rds, first=True, last=True):
                            Ah, Al = A
                            mm(pb, lo, Ah, IDB, rds + [cstb.r], first, False)
                            mm(pb, lo, Al, IDB, rds + [cstb.r], False, last)

                        def P(tile, lo, w=128):
                            return (tile.t[:, 0, lo:lo + w], tile.t[:, 1, lo:lo + w])

                        def VV(H, a):
                            return (H.Vall.t[:, 0, a * 128:(a + 1) * 128], H.Vall.t[:, 1, a * 128:(a + 1) * 128])

                        def split_psum(dst, pb, W):
                            kb.op("act", lambda e: e.copy(out=dst.t[:, 0, :], in_=pb.t[:, 0:W]), writes=[dst.r, pb.r])
                            kb.op("dve", lambda e: e.tensor_tensor(out=dst.t[:, 1, :], in0=pb.t[:, 0:W], in1=dst.t[:, 0, :],
                                                                   op=ALU.subtract), writes=[dst.r, pb.r])

                        pbs = {}

                        def pair_bank(name, vh):
                            if vh == 0:
                                pbs[name] = nb()
                            return pbs[name], vh * 256

                        def pview(pb, W):
                            return pb.t[:, :].rearrange("p (i w) -> p i w", w=256)[:, :, 0:W]

                        def split_pair(dst, pb, W):
                            kb.op("act", lambda e: e.copy(out=dst.t[:, 0, :, 0:W], in_=pview(pb, W)), writes=[dst.r, pb.r])
                            kb.op("dve", lambda e: e.tensor_tensor(out=dst.t[:, 1, :, 0:W], in0=pview(pb, W), in1=dst.t[:, 0, :, 0:W],
                                                                   op=ALU.subtract), writes=[dst.r, pb.r])

                        def st_A(vh):
                            H = IT[c % 2][vh]; Q = HQ[vh]
                            Vd, Vtd = VV(H, 0), VV(H, 1)
                            pb, off = pair_bank("A", vh)
                            mm3(pb, off + 0, Vtd, Vd, [H.Vall.r])
                            mm3(pb, off + 128, Vd, Vtd, [H.Vall.r])
                            if vh == 1:
                                split_pair(SLOT[c % 2].LAp, pb, 256)

                        def st_B(vh):
                            H = IT[c % 2][vh]; Q = HQ[vh]
                            V2, Vt2 = P(H.LA, 0), P(H.LA, 128)
                            Vd = VV(H, 0)
                            pb = nb()
                            mm3(pb, 0, Vt2, V2, [H.LA.r])
                            mm3(pb, 128, V2, Vt2, [H.LA.r])
                            mm(pb, 256, IDB, IDB, [cstb.r], True, False)
                            mmI(pb, 256, Vd, [H.Vall.r], False, False)
                            mmT(pb, 256, Vt2, [H.LA.r], False, False)
                            mm3(pb, 256, Vt2, Vd, [H.LA.r, H.Vall.r], False, True)
                            split_psum(H.LB, pb, 384)

                        def st_C(vh):
                            H = IT[c % 2][vh]; Q = HQ[vh]
                            V4, Vt4, Y1 = P(H.LB, 0), P(H.LB, 128), P(H.LB, 256)
                            pb = nb()
                            mm3(pb, 0, Vt4, V4, [H.LB.r])
                            mm3(pb, 128, V4, Vt4, [H.LB.r])
                            mm3(pb, 256, Vt4, Y1, [H.LB.r], True, False)
                            mmI(pb, 256, Y1, [H.LB.r], False, True)
                            split_psum(H.LC, pb, 384)

                        def st_D(vh):
                            H = IT[c % 2][vh]; Q = HQ[vh]
                            V8, Vt8, Y2 = P(H.LC, 0), P(H.LC, 128), P(H.LC, 256)
                            pb, off = pair_bank("D", vh)
                            mm3(pb, off + 0, V8, Vt8, [H.LC.r])
                            mm3(pb, off + 128, Vt8, Y2, [H.LC.r], True, False)
                            mmI(pb, off + 128, Y2, [H.LC.r], False, True)
                            if vh == 1:
                                split_pair(SLOT[c % 2].LDp, pb, 256)

                        def st_E(vh):
                            H = IT[c % 2][vh]; Q = HQ[vh]
                            Vt16, Y3 = P(H.LD, 0), P(H.LD, 128)
                            pb, off = pair_bank("E", vh)
                            mm3(pb, off, Vt16, Y3, [H.LD.r], True, False)
                            mmI(pb, off, Y3, [H.LD.r], False, True)
                            if vh == 1:
                                split_pair(SLOT[c % 2].LEp, pb, 128)

                        def st_F(vh):
                            H = IT[c % 2][vh]; Q = HQ[vh]
                            T0t = P(H.LE, 0)
                            pb, off = pair_bank("F", vh)
                            mm(pb, off + 0, T0t[0], IDB, [H.LE.r, cstb.r])
                            mm(pb, off + 128, VV(H, 2)[0], T0t[0], [H.Vall.r, H.LE.r])
                            if vh == 1:
                                LFp = SLOT[c % 2].LFp
                                evac(LFp.t[:, :, :], pview(pb, 256), [], [LFp.r, pb.r])

                        def st_G(vh):
                            H = IT[c % 2][vh]; Q = HQ[vh]
                            pb, off = pair_bank("G", vh)
                            mm(pb, off, H.LF.t[:, 0:128], H.LF.t[:, 128:256], [H.LF.r], True, False)
                            mmI(pb, off, P(H.LE, 0), [H.LE.r], False, True)
                            if vh == 1:
                                split_pair(SLOT[c % 2].LGp, pb, 128)

                        def st_H(vh):
                            H = IT[c % 2][vh]; Q = HQ[vh]
                            T1t = P(H.LG, 0)
                            pb, off = pair_bank("H", vh)
                            mm(pb, off + 0, T1t[0], IDB, [H.LG.r, cstb.r])
                            mm(pb, off + 128, VV(H, 3)[0], T1t[0], [H.Vall.r, H.LG.r])
                            if vh == 1:
                                LHp = SLOT[c % 2].LHp
                                evac(LHp.t[:, :, :], pview(pb, 256), [], [LHp.r, pb.r])

                        def st_I(vh):
                            H = IT[c % 2][vh]; Q = HQ[vh]
                            pb, off = pair_bank("I", vh)
                            mm(pb, off, H.LH.t[:, 0:128], H.LH.t[:, 128:256], [H.LH.r], True, False)
                            mmI(pb, off, P(H.LG, 0), [H.LG.r], False, True)
                            if vh == 1:
                                TTp = SLOT[c % 2].TTp[pp]
                                evac(TTp.t[:, :, :], pview(pb, 128), [], [TTp.r, pb.r])

                        def st_W(vh):
                            H = IT[c % 2][vh]; Q = HQ[vh]
                            pb = nb()
                            mmf(pb, 0, H.kbg.t[:], H.TT[pp].t[:], [H.kbg.r, H.TT[pp].r])
                            kb.op("act", lambda e: e.activation(out=H.nwT[pp].t[:], in_=pb.t[:, 0:128], func=AF.Copy, scale=-1.0),
                                  writes=[H.nwT[pp].r, pb.r])

                        def st_V(vh):
                            H = IT[c % 2][vh]; Q = HQ[vh]
                            pb = nb()
                            mmf(pb, 0, H.TT[pp].t[:], H.vbeta[pp].t[:], [H.TT[pp].r, H.vbeta[pp].r], True, c == 0)
                            if c > 0:
                                mmf(pb, 0, H.nwT[pp].t[:], Q.Sbf.t[:], [H.nwT[pp].r, Q.Sbf.r], False, True)
                            evac(Q.vnew.t[:], pb.t[:, 0:128], [], [Q.vnew.r, pb.r])

                        def st_O(vh):
                            H = IT[c % 2][vh]; Q = HQ[vh]
                            hv = 2 * j + vh
                            pb = nb()
                            mmf(pb, 128, H.attnT[pp].t[:], Q.vnew.t[:], [H.attnT[pp].r, Q.vnew.r])
                            if c > 0:
                                mmf(pb, 0, b_.q.t[:, lsl], Q.Sbf.t[:], [b_.q.r, Q.Sbf.r])
                                kb.op("act", lambda e: e.activation(out=Q.tmpo.t[:], in_=pb.t[:, 0:128], func=AF.Identity,
                                                                    scale=egc.t[:, c, hv:hv + 1]),
                                      reads=[egc.r], writes=[Q.tmpo.r, pb.r])
                                kb.op("dve", lambda e: e.tensor_tensor(out=Q.o.t[:], in0=pb.t[:, 128:256], in1=Q.tmpo.t[:], op=ALU.add),
                                      reads=[Q.tmpo.r], writes=[Q.o.r, pb.r])
                            else:
                                kb.op("dve", lambda e: e.tensor_copy(out=Q.o.t[:], in_=pb.t[:, 128:256]), writes=[Q.o.r, pb.r])

                        def st_S(vh):
                            H = IT[c % 2][vh]; Q = HQ[vh]
                            hv = 2 * j + vh
                            if c == NT - 1:
                                return
                            pb = nb()
                            mmf(pb, 0, H.kd[pp].t[:], Q.vnew.t[:], [H.kd[pp].r, Q.vnew.r])
                            if c > 0:
                                kb.op("dve", lambda e: e.scalar_tensor_tensor(
                                    out=Q.S32.t[:], in0=Q.S32.t[:], scalar=cdd.t[:, c, hv:hv + 1], in1=pb.t[:, 0:128],
                                    op0=ALU.mult, op1=ALU.add), reads=[cdd.r], writes=[Q.S32.r, pb.r])
                            else:
                                kb.op("dve", lambda e: e.tensor_copy(out=Q.S32.t[:], in_=pb.t[:, 0:128]), writes=[Q.S32.r, pb.r])
                            kb.op("act", lambda e: e.copy(out=Q.Sbf.t[:], in_=Q.S32.t[:]), reads=[Q.S32.r], writes=[Q.Sbf.r])

                        def st_Y(vh):
                            H = IT[c % 2][vh]; Q = HQ[vh]
                            hv = 2 * j + vh
                            kb.op("dve", lambda e: e.memset(Q.ssq.t[:], 0.0), writes=[Q.ssq.r])
                            kb.op("act", lambda e: e.activation(out=Q.junk.t[:], in_=Q.o.t[:], func=AF.Square, accum_out=Q.ssq.t[:, 0:1]),
                                  reads=[Q.o.r], writes=[Q.junk.r, Q.ssq.r])
                            kb.op("dve", lambda e: e.tensor_scalar(out=Q.r1.t[:], in0=Q.ssq.t[:], scalar1=1.0 / 128.0, scalar2=RMS_EPS,
                                                                   op0=ALU.mult, op1=ALU.add), reads=[Q.ssq.r], writes=[Q.r1.r])
                            kb.op("act", lambda e: e.activation(out=Q.r2.t[:], in_=Q.r1.t[:], func=AF.Ln),
                                  reads=[Q.r1.r], writes=[Q.r2.r])
                            kb.op("act", lambda e: e.activation(out=Q.r2.t[:], in_=Q.r2.t[:], func=AF.Exp, scale=-0.5),
                                  reads=[Q.r2.r], writes=[Q.r2.r])
                            kb.op("dve", lambda e: e.scalar_tensor_tensor(
                                out=Q.yb.t[:], in0=Q.o.t[:], scalar=Q.r2.t[:, 0:1], in1=b_.zs.t[:, cl, vh * 128:(vh + 1) * 128],
                                op0=ALU.mult, op1=ALU.mult), reads=[Q.o.r, Q.r2.r, b_.zs.r], writes=[Q.yb.r])
                            pb = nb()
                            kb.op("pe", lambda e: e.matmul(pb.t[:, 0:128], Q.yb.t[:], CB(C_ID), start=True, stop=True),
                                  reads=[Q.yb.r, cstb.r], writes=[pb.r])
                            evac(Q.ybT.t[:, (c % 4) * 128:(c % 4 + 1) * 128], pb.t[:, 0:128], [], [Q.ybT.r, pb.r])
                            if c % 4 == 3:
                                kb.dma("sp", ybT_d[hv][:, (c - 3) * 128:(c + 1) * 128], Q.ybT.t[:], reads=[Q.ybT.r])

                        return dict(shared=st_shared, p0=st_p0, p1=st_p1, p2=st_p2, p3=st_p3, p4=st_p4, A=st_A, B=st_B, C=st_C, D=st_D, E=st_E, F=st_F, G=st_G,
                                    H=st_H, I=st_I, W=st_W, V=st_V, O=st_O, S=st_S, Y=st_Y)

                    chain_order = ["p0", "shared", "p1", "p2", "p3", "p4", "A", "B", "C", "D", "E", "F", "G", "H", "I", "W"]
                    seq_order = ["V", "O", "S", "Y"]

                    def run_stage(stg, name):
                        if name == "shared":
                            stg[name]()
                        else:
                            for vh in range(2):
                                stg[name](vh)
                    prev_pair = None
                    NG = 8 * (NT // 2)
                    load_inputs(0)
                    for gcp in range(NG + 1):
                        if gcp + 1 < NG:
                            load_inputs(gcp + 1)
                        if gcp < NG:
                            j_, cp_ = divmod(gcp, 8)
                            cur = [make_stages(j_, 2 * cp_, INB[gcp % 3]), make_stages(j_, 2 * cp_ + 1, INB[gcp % 3])]
                        else:
                            cur = None
                        seqs = []
                        if prev_pair is not None:
                            for stg in prev_pair:
                                for name in seq_order:
                                    seqs.append((stg, name))
                        si_ = 0
                        for ci, name in enumerate(chain_order):
                            if cur is not None:
                                for stg in cur:
                                    run_stage(stg, name)
                            if ci >= 2 and si_ < len(seqs) and ci != 9:
                                run_stage(*seqs[si_])
                                si_ += 1
                        while si_ < len(seqs):
                            run_stage(*seqs[si_])
                            si_ += 1
                        prev_pair = cur


        if stop_after not in ("p0", "p1a", "p1b"):
            kb.barrier()
            with ExitStack() as es3:
                def sb3(name, shape, dt=F32, nreg=1):
                    return T(es3.enter_context(nc.sbuf_tensor(name, list(shape), dt)), nreg)
                yaR = sb3("yaR", [128, 8, S], BF16)
                ybR = sb3("ybR", [128, 16, S], BF16)
                for h in range(8):
                    kb.dma("sp", yaR.t[:, h, :], yaT_d[h], writes=[yaR.r])
                for h in range(16):
                    kb.dma("sp", ybR.t[:, h, :], ybT_d[h], writes=[ybR.r])
                wg_ = [sb3("w2g%d" % i, [128, KC, 256], BF16) for i in range(2)]
                wa_ = [sb3("w2a%d" % i, [128, 8, 128], BF16) for i in range(2)]
                wb_ = [sb3("w2b%d" % i, [128, 16, 128], BF16) for i in range(2)]
                sga = sb3("sga", [128, 512])
                sgb = sb3("sgb", [128, 512])
                t1 = sb3("t1", [128, 512])
                mst = [sb3("mst%d" % i, [128, 512], BF16) for i in range(2)]

                def load2(cc):
                    i = cc % 2
                    kb.dma("pool", wg_[i].t[:].rearrange("p k c -> p (k c)"), wgate_d[cc], writes=[wg_[i].r])
                    kb.dma("pool", wa_[i].t[:].rearrange("p k c -> p (k c)"), wpm_d[cc], writes=[wa_[i].r])
                    kb.dma("pool", wb_[i].t[:].rearrange("p k c -> p (k c)"), wpg_d[cc], writes=[wb_[i].r])
                load2(0)
                mi = 0
                for cc in range(16):
                    if cc + 1 < 16:
                        load2(cc + 1)
                    i = cc % 2
                    for g in range(4):
                        gs = slice(g * 512, (g + 1) * 512)
                        pA, pB, pC, pD = nb(), nb(), nb(), nb()
                        for k in range(KC):
                            kb.op("pe", lambda e, k=k: e.matmul(pA.t[:, :], wg_[i].t[:, k, 0:128], hT.t[:, k, gs],
                                                                start=(k == 0), stop=(k == KC - 1)),
                                  reads=[wg_[i].r, hT_all[k]], writes=[pA.r])
                        kb.op("act", lambda e: e.activation(out=sga.t[:], in_=pA.t[:, :], func=AF.Sigmoid),
                              writes=[sga.r, pA.r])
                        for k in range(KC):
                            kb.op("pe", lambda e, k=k: e.matmul(pB.t[:, :], wg_[i].t[:, k, 128:256], hT.t[:, k, gs],
                                                                start=(k == 0), stop=(k == KC - 1)),
                                  reads=[wg_[i].r, hT_all[k]], writes=[pB.r])
                        kb.op("act", lambda e: e.activation(out=sgb.t[:], in_=pB.t[:, :], func=AF.Sigmoid),
                              writes=[sgb.r, pB.r])
                        for k in range(8):
                            kb.op("pe", lambda e, k=k: e.matmul(pC.t[:, :], wa_[i].t[:, k, :], yaR.t[:, k, gs],
                                                                start=(k == 0), stop=(k == 7)),
                                  reads=[wa_[i].r, yaR.r], writes=[pC.r])
                        kb.op("dve", lambda e: e.tensor_tensor(out=sga.t[:], in0=pC.t[:, :], in1=sga.t[:], op=ALU.mult),
                              writes=[sga.r, pC.r])
                        for k in range(16):
                            kb.op("pe", lambda e, k=k: e.matmul(pD.t[:, :], wb_[i].t[:, k, :], ybR.t[:, k, gs],
                                                                start=(k == 0), stop=(k == 15)),
                                  reads=[wb_[i].r, ybR.r], writes=[pD.r])
                        kb.op("dve", lambda e: e.tensor_tensor(out=t1.t[:], in0=pD.t[:, :], in1=sgb.t[:], op=ALU.mult),
                              reads=[sgb.r], writes=[t1.r, pD.r])
                        m_ = mst[mi % 2]
                        mi += 1
                        kb.op("dve", lambda e, m_=m_: e.tensor_tensor(out=m_.t[:], in0=t1.t[:], in1=sga.t[:], op=ALU.add),
                              reads=[t1.r, sga.r], writes=[m_.r])
                        kb.dma("sp", mgT_d[cc][:, gs], m_.t[:], reads=[m_.r])

        def layer_norm_tile(pre, junk, st, lng, lnb):
            kb.op("dve", lambda e: e.memset(st.t[:, 0:2], 0.0), writes=[st.r])
            kb.op("act", lambda e: e.activation(out=junk.t[:], in_=pre.t[:], func=AF.Copy, accum_out=st.t[:, 0:1]),
                  reads=[pre.r], writes=[junk.r, st.r])
            kb.op("act", lambda e: e.activation(out=junk.t[:], in_=pre.t[:], func=AF.Square, accum_out=st.t[:, 1:2]),
                  reads=[pre.r], writes=[junk.r, st.r])
            kb.op("dve", lambda e: e.tensor_scalar_mul(out=st.t[:, 2:4], in0=st.t[:, 0:2], scalar1=1.0 / D),
                  reads=[st.r], writes=[st.r])
            kb.op("dve", lambda e: e.tensor_tensor(out=st.t[:, 4:5], in0=st.t[:, 2:3], in1=st.t[:, 2:3], op=ALU.mult),
                  reads=[st.r], writes=[st.r])
            kb.op("dve", lambda e: e.tensor_tensor(out=st.t[:, 5:6], in0=st.t[:, 3:4], in1=st.t[:, 4:5], op=ALU.subtract),
                  reads=[st.r], writes=[st.r])
            kb.op("act", lambda e: e.activation(out=st.t[:, 6:7], in_=st.t[:, 5:6], func=AF.Ln, bias=epsr.t[:, 1:2]),
                  reads=[st.r, epsr.r], writes=[st.r])
            kb.op("act", lambda e: e.activation(out=st.t[:, 6:7], in_=st.t[:, 6:7], func=AF.Exp, scale=-0.5),
                  reads=[st.r], writes=[st.r])
            kb.op("dve", lambda e: e.scalar_tensor_tensor(out=st.t[:, 7:8], in0=st.t[:, 2:3], scalar=-1.0, in1=st.t[:, 6:7],
                                                          op0=ALU.mult, op1=ALU.mult), reads=[st.r], writes=[st.r])
            kb.op("act", lambda e: e.activation(out=pre.t[:], in_=pre.t[:], func=AF.Identity,
                                                scale=st.t[:, 6:7], bias=st.t[:, 7:8]),
                  reads=[st.r], writes=[pre.r])
            kb.op("dve", lambda e: e.tensor_tensor(out=pre.t[:], in0=pre.t[:], in1=lng.t[:], op=ALU.mult),
                  reads=[lng.r], writes=[pre.r])
            kb.op("dve", lambda e: e.tensor_tensor(out=pre.t[:], in0=pre.t[:], in1=lnb.t[:], op=ALU.add),
                  reads=[lnb.r], writes=[pre.r])

        if stop_after not in ("p0", "p1a", "p1b", "p2"):
            kb.barrier()
            with ExitStack() as es4:
                def sb4(name, shape, dt=F32, nreg=1):
                    return T(es4.enter_context(nc.sbuf_tensor(name, list(shape), dt)), nreg)
                wout = sb4("wout", [128, KC, D], BF16)
                for g in range(4):
                    kb.dma("pool", wout.t[:, :, g * 512:(g + 1) * 512], wout_d[g].rearrange("p (k c) -> p k c", c=512),
                           writes=[wout.r])
                mgR = sb4("mgR", [128, KC, 512], BF16)
                g1b = sb4("g1b", [128, D])
                lng = sb4("lng", [128, D])
                lnb = sb4("lnb", [128, D])
                kb.dma("sp", lng.t[:], lnrep_d[0], writes=[lng.r])
                kb.dma("sp", lnb.t[:], lnrep_d[1], writes=[lnb.r])
                xt = [sb4("xt0", [128, D])]
                pres = [sb4("pre%d" % i, [128, D]) for i in range(2)]
                junk = sb4("junk3", [128, D], BF16)
                st = sb4("st3", [128, 8])
                dgt = sb4("dgt", [128, 128])
                for k4 in range(4):
                    pb = nb()
                    for kk in range(4):
                        k = k4 * 4 + kk
                        kb.op("dve", lambda e, k=k: e.tensor_scalar_mul(out=dgt.t[:], in0=CF(C_ID), scalar1=modT.t[:, 32 + k:33 + k]),
                              reads=[cst.r, modT.r], writes=[dgt.r])
                        kb.op("pe", lambda e, kk=kk, pb=pb: e.matmul(pb.t[:, kk * 128:(kk + 1) * 128], CF(C_ONES), dgt.t[:],
                                                                     start=True, stop=True), reads=[cst.r, dgt.r], writes=[pb.r])
                    kb.op("dve", lambda e, k4=k4, pb=pb: e.tensor_copy(out=g1b.t[:, k4 * 512:(k4 + 1) * 512], in_=pb.t[:, :]),
                          writes=[g1b.r, pb.r])
                def p3_main(t):
                    g = t // 4
                    if t % 4 == 0:
                        kb.dma("sp", mgR.t[:], mgT_d[:, :, g * 512:(g + 1) * 512].rearrange("k p c -> p k c"), writes=[mgR.r])
                    x_ = xt[0]
                    pre = pres[t % 2]
                    kb.dma("sp", x_.t[:], x_d[t], writes=[x_.r])
                    tl = slice((t % 4) * 128, (t % 4 + 1) * 128)
                    for cg in range(4):
                        pb = nb()
                        for k in range(KC):
                            kb.op("pe", lambda e, k=k, cg=cg, pb=pb: e.matmul(
                                pb.t[:, :], mgR.t[:, k, tl], wout.t[:, k, cg * 512:(cg + 1) * 512],
                                start=(k == 0), stop=(k == KC - 1)), reads=[mgR.r, wout.r], writes=[pb.r])
                        cs_ = slice(cg * 512, (cg + 1) * 512)
                        kb.op("dve", lambda e, pb=pb, cs_=cs_, pre=pre: e.tensor_tensor(out=pre.t[:, cs_], in0=pb.t[:, :], in1=g1b.t[:, cs_],
                                                                              op=ALU.mult), reads=[g1b.r], writes=[pre.r, pb.r])
                    kb.op("dve", lambda e, x_=x_, pre=pre: e.scalar_tensor_tensor(out=pre.t[:], in0=x_.t[:], scalar=ALPHA, in1=pre.t[:],
                                                                         op0=ALU.mult, op1=ALU.add), reads=[x_.r], writes=[pre.r])
                    layer_norm_tile(pre, junk, st, lng, lnb)
                    kb.dma("pool", x1_d[t], pre.t[:], reads=[pre.r])
                    return pre
                def p3_tr(t, pre):
                    for k4 in range(4):
                        pb = nb()
                        for kk in range(4):
                            k = k4 * 4 + kk
                            kb.op("pe", lambda e, k=k, kk=kk, pb=pb, pre=pre: e.transpose(
                                pb.t[:, kk * 128:(kk + 1) * 128], pre.t[:, k * 128:(k + 1) * 128], CF(C_ID)),
                                reads=[pre.r, cst.r], writes=[pb.r])
                        for kk in range(4):
                            k = k4 * 4 + kk
                            kb.op("act", lambda e, k=k, kk=kk, pb=pb, t=t: e.activation(
                                out=hT.t[:, k, t * 128:(t + 1) * 128], in_=pb.t[:, kk * 128:(kk + 1) * 128], func=AF.Identity,
                                scale=modT.t[:, 64 + k:65 + k], bias=modT.t[:, 48 + k:49 + k]),
                                reads=[modT.r], writes=[hT_all[k], pb.r])
                pend = None
                for t in range(NT + 1):
                    cur = (t, p3_main(t)) if t < NT else None
                    if pend is not None:
                        p3_tr(*pend)
                    pend = cur

            kb.barrier()
            with ExitStack() as es5:
                def sb5(name, shape, dt=F32, nreg=1):
                    return T(es5.enter_context(nc.sbuf_tensor(name, list(shape), dt)), nreg)
                GT = 1024
                actT = sb5("actT", [128, FC, GT], BF16, nreg=2)
                wfi = [sb5("wfi%d" % i, [128, KC, 256], BF16) for i in range(2)]
                wfo = [sb5("wfo%d" % i, [128, FC, 128], BF16) for i in range(2)]
                sgs = [sb5("sg%d" % i, [128, 512]) for i in range(2)]
                y2c = [sb5("y2c%d" % i, [128, 512]) for i in range(2)]
                y2st = [sb5("y2st%d" % i, [128, 4, 128]) for i in range(1)]
                si = 0
                yi = 0
                pend4 = None

                def p4_tr(yc, ys, oc, t0):
                    pT = nb()
                    for tt in range(4):
                        kb.op("pe", lambda e, tt=tt: e.transpose(
                            pT.t[:, tt * 128:(tt + 1) * 128], yc.t[:, tt * 128:(tt + 1) * 128], CF(C_ID)),
                            reads=[yc.r, cst.r], writes=[pT.r])
                    kb.op("dve", lambda e: e.tensor_copy(
                        out=ys.t[:], in_=pT.t[:, :].rearrange("p (a c) -> p a c", c=128)), writes=[ys.r, pT.r])
                    kb.dma("sp", y2_d[t0:t0 + 4, :, oc * 128:(oc + 1) * 128].rearrange("a p c -> p a c"), ys.t[:],
                           reads=[ys.r])
                for g in range(S // GT):
                    for fc in range(FC):
                        w_ = wfi[fc % 2]
                        kb.dma("pool", w_.t[:].rearrange("p k c -> p (k c)"), wffi_d[fc], writes=[w_.r])
                        for hf in range(GT // 512):
                            gs = slice(g * GT + hf * 512, g * GT + (hf + 1) * 512)
                            pG, pU = nb(), nb()
                            sg_ = sgs[si % 2]
                            si += 1
                            for k in range(KC):
                                kb.op("pe", lambda e, k=k, w_=w_, pG=pG, gs=gs: e.matmul(pG.t[:, :], w_.t[:, k, 0:128], hT.t[:, k, gs],
                                                                                        start=(k == 0), stop=(k == KC - 1)),
                                      reads=[w_.r, hT_all[k]], writes=[pG.r])
                            kb.op("act", lambda e, pG=pG, sg_=sg_: e.activation(out=sg_.t[:], in_=pG.t[:, :], func=AF.Silu),
                                  writes=[sg_.r, pG.r])
                            for k in range(KC):
                                kb.op("pe", lambda e, k=k, w_=w_, pU=pU, gs=gs: e.matmul(pU.t[:, :], w_.t[:, k, 128:256], hT.t[:, k, gs],
                                                                                        start=(k == 0), stop=(k == KC - 1)),
                                      reads=[w_.r, hT_all[k]], writes=[pU.r])
                            kb.op("dve", lambda e, pU=pU, fc=fc, hf=hf, sg_=sg_: e.tensor_tensor(
                                out=actT.t[:, fc, hf * 512:(hf + 1) * 512], in0=pU.t[:, :], in1=sg_.t[:], op=ALU.mult),
                                reads=[sg_.r], writes=[actT.regs[hf], pU.r])
                    for oc in range(16):
                        w_ = wfo[oc % 2]
                        kb.dma("pool", w_.t[:].rearrange("p k c -> p (k c)"), wffo_d[oc], writes=[w_.r])
                        for hf in range(GT // 512):
                            pY = nb()
                            for fc in range(FC):
                                kb.op("pe", lambda e, fc=fc, w_=w_, pY=pY, hf=hf: e.matmul(
                                    pY.t[:, :], w_.t[:, fc, :], actT.t[:, fc, hf * 512:(hf + 1) * 512],
                                    start=(fc == 0), stop=(fc == FC - 1)), reads=[w_.r, actT.regs[hf]], writes=[pY.r])
                            yc = y2c[yi % 2]
                            kb.op("act", lambda e, pY=pY, yc=yc, oc=oc: e.activation(out=yc.t[:], in_=pY.t[:, :], func=AF.Identity,
                                                                                     scale=modT.t[:, 80 + oc:81 + oc]),
                                  reads=[modT.r], writes=[yc.r, pY.r])
                            if pend4 is not None:
                                p4_tr(*pend4)
                            pend4 = (yc, y2st[0], oc, (g * GT + hf * 512) // 128)
                            yi += 1
                            continue
                            pT = nb()
                            for tt in range(4):
                                kb.op("pe", lambda e, tt=tt, yc=yc, pT=pT: e.transpose(
                                    pT.t[:, tt * 128:(tt + 1) * 128], yc.t[:, tt * 128:(tt + 1) * 128], CF(C_ID)),
                                    reads=[yc.r, cst.r], writes=[pT.r])
                            ys = y2st[yi % 2]
                            yi += 1
                            kb.op("dve", lambda e, pT=pT, ys=ys: e.tensor_copy(
                                out=ys.t[:], in_=pT.t[:, :].rearrange("p (a c) -> p a c", c=128)), writes=[ys.r, pT.r])
                            t0 = (g * GT + hf * 512) // 128
                            kb.dma("sp", y2_d[t0:t0 + 4, :, oc * 128:(oc + 1) * 128].rearrange("a p c -> p a c"), ys.t[:],
                                   reads=[ys.r])
                if pend4 is not None:
                    p4_tr(*pend4)
                    pend4 = None
            kb.barrier()
            with ExitStack() as es6:
                def sb6(name, shape, dt=F32, nreg=1):
                    return T(es6.enter_context(nc.sbuf_tensor(name, list(shape), dt)), nreg)
                lng2 = sb6("lng2", [128, D])
                lnb2 = sb6("lnb2", [128, D])
                junk2 = sb6("junk6", [128, D], BF16)
                st2 = sb6("st6", [128, 8])
                xqs = [sb6("xq%d" % i, [128, D]) for i in range(2)]
                y2t = [sb6("y2t%d" % i, [128, D]) for i in range(2)]
                kb.dma("sp", lng2.t[:], lnrep_d[2], writes=[lng2.r])
                kb.dma("sp", lnb2.t[:], lnrep_d[3], writes=[lnb2.r])
                for t in range(NT):
                    xq = xqs[t % 2]
                    yt_ = y2t[t % 2]
                    kb.dma("sp", xq.t[:], x1_d[t], writes=[xq.r])
                    kb.dma("sp", yt_.t[:], y2_d[t], writes=[yt_.r])
                    kb.op("dve", lambda e, xq=xq, yt_=yt_: e.scalar_tensor_tensor(out=xq.t[:], in0=xq.t[:], scalar=ALPHA, in1=yt_.t[:],
                                                                                 op0=ALU.mult, op1=ALU.add),
                          reads=[yt_.r], writes=[xq.r])
                    layer_norm_tile(xq, junk2, st2, lng2, lnb2)
                    final_toks.append(kb.dma("pool", out_d[t], xq.t[:], reads=[xq.r]))

        if dbg:
            dbg_outs["yaT"] = nc.dram_tensor("dbg_yaT", [NH_M, 128, S], BF16, kind="ExternalOutput").ap()
            with ExitStack() as esd:
                tmpd = T(esd.enter_context(nc.sbuf_tensor("tmpd", [128, S], BF16)))
                for h in range(NH_M):
                    kb.dma("sp", tmpd.t[:], yaT_d[h], writes=[tmpd.r])
                    final_toks.append(kb.dma("sp", dbg_outs["yaT"][h], tmpd.t[:], reads=[tmpd.r]))
        if dbg and stop_after not in ("p0", "p1a"):
            dbg_outs["ybT"] = nc.dram_tensor("dbg_ybT", [HV, 128, S], BF16, kind="ExternalOutput").ap()
            with ExitStack() as esd:
                tmpd = T(esd.enter_context(nc.sbuf_tensor("tmpd2", [128, S], BF16)))
                for h in range(HV):
                    kb.dma("sp", tmpd.t[:], ybT_d[h], writes=[tmpd.r])
                    final_toks.append(kb.dma("sp", dbg_outs["ybT"][h], tmpd.t[:], reads=[tmpd.r]))
        for tok in final_toks:
            kb.wait_tok("sp", tok)
        for key, n in kb.dcnt.items():
            if n > 0:
                kb.wait_tok("sp", (key, 16 * n))
        print("instructions:", kb.n_inst, "waits:", kb.n_wait, {k: v for k, v in kb.cnt.items()})
    return nc


def _klay(w):
    K, C = w.shape
    return np.ascontiguousarray(w.reshape(K // 128, 128, C).transpose(1, 0, 2))


def _rel_bucket_np(n):
    n = np.asarray(n)
    max_exact = 16
    nn = np.maximum(n, 0)
    nf = np.maximum(nn, 1).astype(np.float32)
    large = max_exact + (np.log(nf / np.float32(max_exact)) / np.float32(math.log(128 / max_exact))
                         * np.float32(32 - max_exact)).astype(np.int32)
    large = np.minimum(large, 31)
    return np.where(nn < max_exact, nn, large)


def make_consts():
    p = np.arange(128)[:, None]
    f = np.arange(128)[None, :]
    c = np.zeros((128, NCONST, 128), np.float32)
    c[:, C_ID] = (p == f)
    c[:, C_TRI] = (p <= f)
    c[:, C_AST] = (p > f)
    bd = (p // 32) == (f // 32)
    c[:, C_SUBD] = (f > p) & bd
    c[:, C_SLBD] = (f < p) & bd
    c[:, C_M1] = ((p // 32) % 2 == 1) & ((f // 32) == (p // 32) - 1)
    c[:, C_M2] = ((p // 32) >= 2) & ((f // 32) < 2)
    c[:, C_UI] = (f >= p)
    c[:, C_ONES] = 1.0
    return c


def prep_shared(inp):
    f32 = np.float32
    sh = {}
    w_ada = inp["w_ada"][0]
    wl = w_ada.reshape(KC, 128, 96, 128).transpose(2, 1, 0, 3)
    wl = wl.reshape(24, 4, 128, KC, 128).transpose(0, 2, 1, 3, 4)
    sh["wada_lay"] = np.ascontiguousarray(wl).reshape(24, 128, 4 * KC * 128)
    sh["bada_lay"] = np.ascontiguousarray(inp["b_ada"][0].reshape(96, 128).T)
    sh["consts"] = make_consts()
    w_in = inp["w_in"][0]
    o1 = 3072
    o2 = o1 + 4096
    o3 = o2 + 2048
    o5 = o3 + 32
    wm = np.empty((NH_M, 128, KC, 384), f32)
    for h in range(NH_M):
        cols = np.concatenate([np.arange(h * 128, (h + 1) * 128), 1024 + np.arange(h * 128, (h + 1) * 128),
                               2048 + np.arange(h * 128, (h + 1) * 128)])
        wm[h] = _klay(w_in[:, cols])
    sh["wmoba_lay"] = wm.reshape(NH_M, 128, KC * 384)
    rb = inp["rel_bias"]
    i = np.arange(128)[:, None]
    j = np.arange(128)[None, :]
    bd = _rel_bucket_np(j - i)
    bo = _rel_bucket_np(128 + j - i)
    mb = np.empty((128, 16, 128), f32)
    for h in range(NH_M):
        mb[:, h, :] = rb[bd, h]
        mb[:, 8 + h, :] = rb[bo, h]
    sh["mbias_lay"] = mb
    sh["c31_rep"] = np.ascontiguousarray(np.broadcast_to(rb[31][None, :], (128, NH_M))).astype(f32)
    wg = np.empty((8, 128, KC, 768), f32)
    for jh in range(8):
        cols = np.concatenate([o1 + np.arange(jh * 128, (jh + 1) * 128),
                               o1 + 1024 + np.arange(jh * 128, (jh + 1) * 128),
                               o1 + 2048 + np.arange(jh * 256, (jh + 1) * 256),
                               o2 + np.arange(jh * 256, (jh + 1) * 256)])
        wg[jh] = _klay(w_in[:, cols])
    sh["wgdn_lay"] = wg.reshape(8, 128, KC * 768)
    sh["wab_lay"] = _klay(w_in[:, o3:o5]).reshape(128, KC * 32)
    cw = inp["conv_w"][0]
    sh["conv_lay"] = np.ascontiguousarray(cw.reshape(4, 32, 128).transpose(2, 1, 0)).reshape(128, 128)
    sh["alog_rep"] = np.ascontiguousarray(np.broadcast_to(inp["a_log"][0][None, :], (128, HV))).astype(f32)
    sh["dtb_rep"] = np.ascontiguousarray(np.broadcast_to(inp["dt_bias"][0][None, :], (128, HV))).astype(f32)
    sh["normw_rep"] = np.ascontiguousarray(np.broadcast_to(inp["gdn_norm_w"][0][None, :], (128, 128))).astype(f32)
    wgate = w_in[:, o5:o5 + 4096]
    wgl = np.empty((16, 128, KC, 256), f32)
    for cc in range(16):
        cols = np.concatenate([np.arange(cc * 128, (cc + 1) * 128), 2048 + np.arange(cc * 128, (cc + 1) * 128)])
        wgl[cc] = _klay(wgate[:, cols])
    sh["wgate_lay"] = wgl.reshape(16, 128, KC * 256)
    wpm = inp["w_proj_moba"][0]
    wpg = inp["w_proj_gdn"][0]
    sh["wpm_lay"] = np.stack([_klay(wpm[:, cc * 128:(cc + 1) * 128]) for cc in range(16)]).reshape(16, 128, 8 * 128)
    sh["wpg_lay"] = np.stack([_klay(wpg[:, cc * 128:(cc + 1) * 128]) for cc in range(16)]).reshape(16, 128, 16 * 128)
    wo = inp["w_out"][0]
    sh["wout_lay"] = np.stack([_klay(wo[:, g * 512:(g + 1) * 512]) for g in range(4)]).reshape(4, 128, KC * 512)
    sh["ln_rep"] = np.stack([np.broadcast_to(inp[k][0][None, :], (128, D)) for k in
                             ("ln1_g", "ln1_b", "ln2_g", "ln2_b")]).astype(f32)
    wfi = inp["w_ffn_in"][0]
    wfl = np.empty((FC, 128, KC, 256), f32)
    for fc in range(FC):
        cols = np.concatenate([np.arange(fc * 128, (fc + 1) * 128), DFF + np.arange(fc * 128, (fc + 1) * 128)])
        wfl[fc] = _klay(wfi[:, cols])
    sh["wffi_lay"] = wfl.reshape(FC, 128, KC * 256)
    wfo = inp["w_ffn_out"][0]
    sh["wffo_lay"] = np.stack([_klay(wfo[:, oc * 128:(oc + 1) * 128]) for oc in range(16)]).reshape(16, 128, FC * 128)
    return sh


def prep_core(inp, b):
    x = inp["x"][b]
    return {
        "xT": np.ascontiguousarray(x.T).reshape(KC, 128, S),
        "x": np.ascontiguousarray(x).reshape(NT, 128, D),
        "c_lay": np.ascontiguousarray(inp["c"][b].reshape(KC, 128).T),
    }


_NC_CACHE = {}


def kernel(**inputs):
    inp = {k: np.asarray(v, dtype=np.float32) for k, v in inputs.items()}
    if "nc" not in _NC_CACHE:
        _NC_CACHE["nc"] = build_nc()
    nc = _NC_CACHE["nc"]
    shared = prep_shared(inp)
    in_maps = []
    for b in range(8):
        m = dict(shared)
        m.update(prep_core(inp, b))
        in_maps.append(m)
    res = run_bass_kernel_spmd(nc, in_maps, core_ids=list(range(8)))
    out = np.stack([np.asarray(r["out"]).reshape(S, D) for r in res.results], axis=0)
    return out.astype(np.float32)
```

```python
import math
from contextlib import ExitStack

import numpy as np
import concourse.bass as bass
import concourse.mybir as mybir
from concourse.bass_utils import run_bass_kernel_spmd

F32 = mybir.dt.float32
BF16 = mybir.dt.bfloat16
AF = mybir.ActivationFunctionType
ALU = mybir.AluOpType
AX = mybir.AxisListType

D = 2048
S = 2048
NT = 16
KC = 16
NH_M = 8
HV = 16
DFF = 5632
FC = DFF // 128
N_IN = 13344
ALPHA = 2.0 ** 0.25
LN_EPS = 1e-5
RMS_EPS = 1e-6
SCALE_M = 128 ** -0.5

C_ID, C_TRI, C_AST, C_SUBD, C_SLBD, C_M1, C_M2, C_UI, C_ONES = range(9)
NCONST = 9


class Reg:
    __slots__ = ("w", "r")

    def __init__(self):
        self.w = None
        self.r = {}


class KB:
    def __init__(self, nc, es):
        self.nc = nc
        self.engs = {"pe": nc.tensor, "act": nc.scalar, "dve": nc.vector, "pool": nc.gpsimd, "sp": nc.sync}
        self.semobj = {}
        self.cnt = {}
        self.waited = {}
        for name in self.engs:
            self.semobj[name] = es.enter_context(nc.semaphore("s_" + name))
            self.cnt[name] = 0
            self.waited[name] = {}
        self.nd = 12
        self.dnext = {"sp": 0, "pool": 0}
        self.dcnt = {}
        for q in ("sp", "pool"):
            for k in range(self.nd):
                key = ("d", q, k)
                self.semobj[key] = es.enter_context(nc.semaphore("d_%s%d" % (q, k)))
                self.dcnt[key] = 0
        self.n_wait = 0
        self.n_inst = 0

    def _collect(self, reads, writes):
        deps = {}
        for r in reads:
            if r.w is not None:
                k, v = r.w
                if deps.get(k, 0) < v:
                    deps[k] = v
        for w in writes:
            if w.w is not None:
                k, v = w.w
                if deps.get(k, 0) < v:
                    deps[k] = v
            for k, v in w.r.items():
                if deps.get(k, 0) < v:
                    deps[k] = v
        return deps

    def _wait(self, eng, deps, attach=False):
        wd = self.waited[eng]
        need = []
        for k, v in deps.items():
            if eng == "pe" and k == "pe":
                continue
            if wd.get(k, 0) < v:
                need.append((k, v))
                wd[k] = v
        pend = None
        if attach and need:
            pend = need.pop()
        for k, v in need:
            self.engs[eng].wait_ge(self.semobj[k], v)
            self.n_wait += 1
        return pend

    def _update(self, tok, reads, writes):
        k, v = tok
        for w in writes:
            w.w = tok
            w.r = {}
        for r in reads:
            if r.r.get(k, 0) < v:
                r.r[k] = v

    def op(self, eng, fn, reads=(), writes=()):
        pend = self._wait(eng, self._collect(reads, writes), attach=True)
        inst = fn(self.engs[eng])
        if pend is not None:
            inst._wait_ge(self.semobj[pend[0]], pend[1])
        inst.then_inc(self.semobj[eng], 1)
        self.cnt[eng] += 1
        self.n_inst += 1
        self._update((eng, self.cnt[eng]), reads, writes)

    def dma(self, q, out, in_, reads=(), writes=()):
        k = self.dnext[q]
        self.dnext[q] = (k + 1) % self.nd
        key = ("d", q, k)
        deps = self._collect(reads, writes)
        if self.dcnt[key] > 0:
            deps[key] = max(deps.get(key, 0), 16 * self.dcnt[key])
        pend = self._wait(q, deps, attach=True)
        inst = self.engs[q].dma_start(out=out, in_=in_)
        if pend is not None:
            inst._wait_ge(self.semobj[pend[0]], pend[1])
        inst.then_inc(self.semobj[key], 16)
        self.dcnt[key] += 1
        self.n_inst += 1
        tok = (key, 16 * self.dcnt[key])
        self._update(tok, reads, writes)
        return tok

    def barrier(self):
        deps = {}
        for name in self.engs:
            if self.cnt[name] > 0:
                deps[name] = self.cnt[name]
        for key, n in self.dcnt.items():
            if n > 0:
                deps[key] = 16 * n
        for eng in self.engs:
            self._wait(eng, dict(deps))

    def wait_tok(self, eng, tok):
        self._wait(eng, {tok[0]: tok[1]})


class T:
    def __init__(self, t, nreg=1):
        self.t = t
        self.regs = [Reg() for _ in range(nreg)]

    @property
    def r(self):
        return self.regs[0]


class View:
    def __init__(self, ap, reg):
        self.t = ap
        self.regs = [reg]

    @property
    def r(self):
        return self.regs[0]


def build_nc(stop_after=None, dbg=False):
    nc = bass.Bass("TRN2", target_bir_lowering=False)

    def din(name, shape, dt=F32):
        return nc.dram_tensor(name, list(shape), dt, kind="ExternalInput").ap()

    xT_d = din("xT", [KC, 128, S])
    x_d = din("x", [NT, 128, D])
    c_d = din("c_lay", [128, KC])
    wada_d = din("wada_lay", [24, 128, 4 * KC * 128])
    bada_d = din("bada_lay", [128, 96])
    consts_d = din("consts", [128, NCONST, 128])
    wmoba_d = din("wmoba_lay", [NH_M, 128, KC * 384])
    mbias_d = din("mbias_lay", [128, 16, 128])
    c31_d = din("c31_rep", [128, NH_M])
    wgdn_d = din("wgdn_lay", [8, 128, KC * 768])
    wab_d = din("wab_lay", [128, KC * 32])
    conv_d = din("conv_lay", [128, 32 * 4])
    alog_d = din("alog_rep", [128, HV])
    dtb_d = din("dtb_rep", [128, HV])
    normw_d = din("normw_rep", [128, 128])
    wgate_d = din("wgate_lay", [16, 128, KC * 256])
    wpm_d = din("wpm_lay", [16, 128, 8 * 128])
    wpg_d = din("wpg_lay", [16, 128, 16 * 128])
    wout_d = din("wout_lay", [4, 128, KC * 512])
    lnrep_d = din("ln_rep", [4, 128, D])
    wffi_d = din("wffi_lay", [FC, 128, KC * 256])
    wffo_d = din("wffo_lay", [16, 128, FC * 128])

    out_d = nc.dram_tensor("out", [NT, 128, D], F32, kind="ExternalOutput").ap()

    def dscr(name, shape, dt):
        return nc.dram_tensor(name, list(shape), dt, kind="Internal").ap()

    yaT_d = dscr("yaT_s", [NH_M, 128, S], BF16)
    ybT_d = dscr("ybT_s", [HV, 128, S], BF16)
    sgT_d = dscr("sgT_s", [32, 128, S], BF16)
    mgT_d = dscr("mgT_s", [KC, 128, S], BF16)
    x1_d = dscr("x1_s", [NT, 128, D], F32)
    y2_d = dscr("y2_s", [NT, 128, D], F32)
    gq_d = dscr("gq_s", [8, 128, S], BF16)
    gk_d = dscr("gk_s", [8, 128, S], BF16)
    gkt_d = dscr("gkt_s", [8, 128, NT * 128], BF16)
    gvt_d = dscr("gvt_s", [8, 128, NT * 256], BF16)
    gzs_d = dscr("gzs_s", [8, 128, NT * 256], BF16)
    dbg_outs = {}

    with ExitStack() as es:
        kb = KB(nc, es)

        def sb(name, shape, dt=F32, nreg=1):
            return T(es.enter_context(nc.sbuf_tensor(name, list(shape), dt)), nreg)

        banks = [T(es.enter_context(nc.psum_tensor("ps%d" % i, [128, 512], F32))) for i in range(8)]

        cst = sb("cst", [128, NCONST, 128])
        kb.dma("sp", cst.t[:], consts_d, writes=[cst.r])
        cstb = sb("cstb", [128, NCONST, 128], BF16)
        kb.op("dve", lambda e: e.tensor_copy(out=cstb.t[:], in_=cst.t[:]), reads=[cst.r], writes=[cstb.r])

        epsr = sb("epsr", [128, 2])
        kb.op("dve", lambda e: e.memset(epsr.t[:, 0:1], RMS_EPS), writes=[epsr.r])
        kb.op("dve", lambda e: e.memset(epsr.t[:, 1:2], LN_EPS), writes=[epsr.r])

        def CF(i):
            return cst.t[:, i, :]

        def CB(i):
            return cstb.t[:, i, :]

        c_sb = sb("c_sb", [128, KC])
        sc_bf = sb("sc_bf", [128, KC], BF16)
        kb.dma("sp", c_sb.t[:], c_d, writes=[c_sb.r])
        kb.op("act", lambda e: e.activation(out=sc_bf.t[:], in_=c_sb.t[:], func=AF.Silu),
              reads=[c_sb.r], writes=[sc_bf.r])
        bada = sb("bada", [128, 96])
        kb.dma("sp", bada.t[:], bada_d, writes=[bada.r])
        modT = sb("modT", [128, 96])
        hT = sb("hT", [128, KC, S], BF16, nreg=KC)

        es0 = ExitStack()

        def sb0(name, shape, dt=F32, nreg=1):
            return T(es0.enter_context(nc.sbuf_tensor(name, list(shape), dt)), nreg)
        wab = [sb0("wada%d" % i, [128, 4, KC, 128], BF16) for i in range(2)]
        xb = [sb0("xTb%d" % i, [128, S]) for i in range(2)]

        def mod_group(g, pm, col0):
            wb = wab[g % 2]
            kb.dma("pool", wb.t[:].rearrange("p a k c -> p (a k c)"), wada_d[g], writes=[wb.r])
            for a in range(4):
                for k in range(KC):
                    kb.op("pe", lambda e, a=a, k=k: e.matmul(
                        pm.t[:, col0 + a:col0 + a + 1], wb.t[:, a, k, :], sc_bf.t[:, k:k + 1],
                        start=(k == 0), stop=(k == KC - 1)),
                        reads=[wb.r, sc_bf.r], writes=[pm.r])
            kb.op("dve", lambda e: e.tensor_tensor(out=modT.t[:, 4 * g:4 * g + 4], in0=pm.t[:, col0:col0 + 4],
                                                   in1=bada.t[:, 4 * g:4 * g + 4], op=ALU.add),
                  reads=[bada.r], writes=[modT.r, pm.r])
        for g in range(8):
            mod_group(g, banks[0], 4 * g)
        kb.op("dve", lambda e: e.tensor_scalar_add(out=modT.t[:, 16:32], in0=modT.t[:, 16:32], scalar1=1.0),
              reads=[modT.r], writes=[modT.r])
        for k in range(KC):
            b_ = xb[k % 2]
            kb.dma("sp", b_.t[:], xT_d[k], writes=[b_.r])
            kb.op("act", lambda e, k=k, b_=b_: e.activation(
                out=hT.t[:, k, :], in_=b_.t[:], func=AF.Identity,
                scale=modT.t[:, 16 + k:17 + k], bias=modT.t[:, k:k + 1]),
                reads=[b_.r, modT.r], writes=[hT.regs[k]])

        if dbg:
            dbg_outs["modT"] = nc.dram_tensor("dbg_modT", [128, 96], F32, kind="ExternalOutput").ap()
            kb.dma("sp", dbg_outs["modT"], modT.t[:], reads=[modT.r])

        hT_all = hT.regs
        final_toks = []

        if stop_after != "p0":
            with ExitStack() as es1:
                def sb1(name, shape, dt=F32, nreg=1):
                    return T(es1.enter_context(nc.sbuf_tensor(name, list(shape), dt)), nreg)
                wm = [sb1("wm%d" % i, [128, KC, 384], BF16) for i in range(2)]
                qT = sb1("qT", [128, S], BF16, nreg=4)
                kT = sb1("kT", [128, S], BF16, nreg=4)
                V1 = sb1("V1", [128, NT, 129], BF16, nreg=5)
                kmf = sb1("kmf", [128, 8])
                kmT = sb1("kmT", [128, 8], BF16)
                PT = [sb1("PT%d" % i, [128, 256], BF16) for i in range(4)]
                tmpE = [sb1("tmpE%d" % i, [128, 256], BF16) for i in range(2)]
                acc = [sb1("acc%d" % i, [128, 129]) for i in range(2)]
                rt = [sb1("rt%d" % i, [128, 8]) for i in range(2)]
                cmpb = [sb1("cmp%d" % i, [128, 8, 8]) for i in range(2)]
                rank = [sb1("rank%d" % i, [128, 8]) for i in range(2)]
                sel = [sb1("sel%d" % i, [128, 8]) for i in range(2)]
                rec = [sb1("rec%d" % i, [128, 1]) for i in range(2)]
                ya = [sb1("ya%d" % i, [128, 128], BF16) for i in range(2)]
                yaT = [sb1("yaT%d" % i, [128, S], BF16) for i in range(2)]
                mb = sb1("mb", [128, 16, 128])
                mbe = sb1("mbe", [128, 16, 128])
                Ed = sb1("Ed", [128, 8, 128], BF16)
                Eo = sb1("Eo", [128, 8, 128], BF16)
                c31 = sb1("c31", [128, NH_M])

                kb.dma("sp", mb.t[:], mbias_d, writes=[mb.r])
                kb.dma("sp", c31.t[:], c31_d, writes=[c31.r])
                kb.op("act", lambda e: e.activation(out=mbe.t[:], in_=mb.t[:], func=AF.Exp),
                      reads=[mb.r], writes=[mbe.r])
                kb.op("dve", lambda e: e.tensor_tensor(
                    out=Ed.t[:], in0=mbe.t[:, 0:8, :],
                    in1=cst.t[:, C_UI:C_UI + 1, :].broadcast_to([128, 8, 128]), op=ALU.mult),
                    reads=[mbe.r, cst.r], writes=[Ed.r])
                kb.op("dve", lambda e: e.tensor_copy(out=Eo.t[:], in_=mbe.t[:, 8:16, :]),
                      reads=[mbe.r], writes=[Eo.r])
                kb.op("dve", lambda e: e.memset(V1.t[:, :, 128:129], 1.0), writes=[V1.regs[4]])

                kb.dma("pool", wm[0].t[:].rearrange("p k c -> p (k c)"), wmoba_d[0], writes=[wm[0].r])
                s_slot = 0
                o_slot = 0
                p_slot = 0
                for h in range(NH_M):
                    w = wm[h % 2]
                    if h + 1 < NH_M:
                        wn = wm[(h + 1) % 2]
                        kb.dma("pool", wn.t[:].rearrange("p k c -> p (k c)"), wmoba_d[h + 1], writes=[wn.r])
                    for which, dst in ((0, qT), (1, kT)):
                        for g in range(4):
                            pb = banks[g % 2]
                            for k in range(KC):
                                kb.op("pe", lambda e, k=k, g=g, pb=pb, which=which: e.matmul(
                                    pb.t[:, :], w.t[:, k, which * 128:(which + 1) * 128],
                                    hT.t[:, k, g * 512:(g + 1) * 512], start=(k == 0), stop=(k == KC - 1)),
                                    reads=[w.r, hT_all[k]], writes=[pb.r])
                            kb.op("act", lambda e, g=g, pb=pb, dst=dst: e.copy(
                                out=dst.t[:, g * 512:(g + 1) * 512], in_=pb.t[:, :]),
                                writes=[dst.regs[g], pb.r])
                    for g in range(4):
                        pb = banks[g % 2]
                        for tt in range(4):
                            t = g * 4 + tt
                            for k in range(KC):
                                kb.op("pe", lambda e, k=k, t=t, tt=tt, pb=pb: e.matmul(
                                    pb.t[:, tt * 128:(tt + 1) * 128], hT.t[:, k, t * 128:(t + 1) * 128],
                                    w.t[:, k, 256:384], start=(k == 0), stop=(k == KC - 1)),
                                    reads=[w.r, hT_all[k]], writes=[pb.r])
                        kb.op("dve", lambda e, g=g, pb=pb: e.tensor_copy(
                            out=V1.t[:, g * 4:(g + 1) * 4, 0:128],
                            in_=pb.t[:, :].rearrange("p (a c) -> p a c", c=128)),
                            writes=[V1.regs[g], pb.r])
                    kb.op("dve", lambda e: e.tensor_reduce(
                        out=kmf.t[:], in_=kT.t[:].rearrange("p (n b) -> p n b", b=256), axis=AX.X, op=ALU.add),
                        reads=kT.regs, writes=[kmf.r])
                    kb.op("dve", lambda e: e.tensor_scalar_mul(out=kmT.t[:], in0=kmf.t[:], scalar1=1.0 / 256.0),
                          reads=[kmf.r], writes=[kmT.r])
                    yT = yaT[h % 2]
                    for qb in range(8):
                        q0 = qb * 256
                        if qb >= 4:
                            for qi in range(2):
                                rp = banks[7]
                                kb.op("pe", lambda e, qi=qi: e.matmul(
                                    rp.t[:, qi * 8:qi * 8 + 8], qT.t[:, q0 + qi * 128:q0 + (qi + 1) * 128],
                                    kmT.t[:, :], start=True, stop=True),
                                    reads=[qT.regs[qb // 2], kmT.r], writes=[rp.r])
                                kb.op("dve", lambda e, qi=qi: e.tensor_copy(out=rt[qi].t[:], in_=rp.t[:, qi * 8:qi * 8 + 8]),
                                      writes=[rt[qi].r, rp.r])
                                kb.op("dve", lambda e, qi=qi: e.tensor_tensor(
                                    out=cmpb[qi].t[:, 0:qb, 0:qb],
                                    in0=rt[qi].t[:, 0:qb].unsqueeze(1).broadcast_to([128, qb, qb]),
                                    in1=rt[qi].t[:, 0:qb].unsqueeze(2).broadcast_to([128, qb, qb]),
                                    op=ALU.is_gt), reads=[rt[qi].r], writes=[cmpb[qi].r])
                                kb.op("dve", lambda e, qi=qi: e.tensor_reduce(
                                    out=rank[qi].t[:, 0:qb], in_=cmpb[qi].t[:, 0:qb, 0:qb], axis=AX.X, op=ALU.add),
                                    reads=[cmpb[qi].r], writes=[rank[qi].r])
                                kb.op("dve", lambda e, qi=qi: e.tensor_single_scalar(
                                    out=sel[qi].t[:, 0:qb], in_=rank[qi].t[:, 0:qb], scalar=2.5, op=ALU.is_lt),
                                    reads=[rank[qi].r], writes=[sel[qi].r])
                        order = [qb] + list(range(qb))
                        staged = []

                        def emit_scores(n):
                            nonlocal s_slot, p_slot
                            res = []
                            for kt in (2 * n, 2 * n + 1):
                                sbank = banks[s_slot % 4]
                                sreg = sbank.r
                                soff = 0
                                s_slot += 1
                                pt = PT[p_slot % 4]
                                p_slot += 1
                                te = tmpE[p_slot % 2]
                                ksl = slice(kt * 128, (kt + 1) * 128)
                                kreg = kT.regs[kt // 4]
                                qreg = qT.regs[qb // 2]
                                if n < qb:
                                    kb.op("pe", lambda e, ksl=ksl, soff=soff, sbank=sbank: e.matmul(
                                        sbank.t[:, soff:soff + 256], kT.t[:, ksl], qT.t[:, q0:q0 + 256],
                                        start=True, stop=True), reads=[kreg, qreg], writes=[sreg])
                                    if kt == 2 * qb - 1:
                                        kb.op("act", lambda e, soff=soff, sbank=sbank, te=te: e.activation(
                                            out=te.t[:, 0:128], in_=sbank.t[:, soff:soff + 128], func=AF.Exp,
                                            scale=SCALE_M), writes=[te.r, sreg])
                                        kb.op("dve", lambda e, te=te, pt=pt: e.tensor_tensor(
                                            out=pt.t[:, 0:128], in0=te.t[:, 0:128], in1=Eo.t[:, h, :], op=ALU.mult),
                                            reads=[te.r, Eo.r], writes=[pt.r])
                                        kb.op("act", lambda e, soff=soff, sbank=sbank, pt=pt: e.activation(
                                            out=pt.t[:, 128:256], in_=sbank.t[:, soff + 128:soff + 256], func=AF.Exp,
                                            scale=SCALE_M, bias=c31.t[:, h:h + 1]), reads=[c31.r], writes=[pt.r, sreg])
                                    else:
                                        kb.op("act", lambda e, soff=soff, sbank=sbank, pt=pt: e.activation(
                                            out=pt.t[:, :], in_=sbank.t[:, soff:soff + 256], func=AF.Exp,
                                            scale=SCALE_M, bias=c31.t[:, h:h + 1]), reads=[c31.r], writes=[pt.r, sreg])
                                elif kt == 2 * qb:
                                    kb.op("pe", lambda e, ksl=ksl, soff=soff, sbank=sbank: e.matmul(
                                        sbank.t[:, soff:soff + 256], kT.t[:, ksl], qT.t[:, q0:q0 + 256],
                                        start=True, stop=True), reads=[kreg, qreg], writes=[sreg])
                                    kb.op("act", lambda e, soff=soff, sbank=sbank, te=te: e.activation(
                                        out=te.t[:, :], in_=sbank.t[:, soff:soff + 256], func=AF.Exp,
                                        scale=SCALE_M), writes=[te.r, sreg])
                                    kb.op("dve", lambda e, te=te, pt=pt: e.tensor_tensor(
                                        out=pt.t[:, 0:128], in0=te.t[:, 0:128], in1=Ed.t[:, h, :], op=ALU.mult),
                                        reads=[te.r, Ed.r], writes=[pt.r])
                                    kb.op("dve", lambda e, te=te, pt=pt: e.tensor_tensor(
                                        out=pt.t[:, 128:256], in0=te.t[:, 128:256], in1=Eo.t[:, h, :], op=ALU.mult),
                                        reads=[te.r, Eo.r], writes=[pt.r])
                                else:
                                    kb.op("pe", lambda e, ksl=ksl, soff=soff, sbank=sbank: e.matmul(
                                        sbank.t[:, soff + 128:soff + 256], kT.t[:, ksl], qT.t[:, q0 + 128:q0 + 256],
                                        start=True, stop=True), reads=[kreg, qreg], writes=[sreg])
                                    kb.op("act", lambda e, soff=soff, sbank=sbank, te=te: e.activation(
                                        out=te.t[:, 128:256], in_=sbank.t[:, soff + 128:soff + 256], func=AF.Exp,
                                        scale=SCALE_M), writes=[te.r, sreg])
                                    kb.op("dve", lambda e, te=te, pt=pt: e.tensor_tensor(
                                        out=pt.t[:, 128:256], in0=te.t[:, 128:256], in1=Ed.t[:, h, :], op=ALU.mult),
                                        reads=[te.r, Ed.r], writes=[pt.r])
                                res.append((kt, pt))
                            return res

                        def emit_pv(n, pts):
                            nonlocal o_slot
                            for qi in range(2):
                                obank = banks[4 + o_slot % 3]
                                oreg = obank.r
                                ooff = 0
                                o_slot += 1
                                use = [(kt, pt) for (kt, pt) in pts if kt <= 2 * qb + qi]
                                for i, (kt, pt) in enumerate(use):
                                    kb.op("pe", lambda e, kt=kt, pt=pt, i=i, qi=qi, ooff=ooff, obank=obank, use=use: e.matmul(
                                        obank.t[:, ooff:ooff + 129], pt.t[:, qi * 128:(qi + 1) * 128], V1.t[:, kt, :],
                                        start=(i == 0), stop=(i == len(use) - 1)),
                                        reads=[pt.r, V1.regs[kt // 4], V1.regs[4]], writes=[oreg])
                                if n == qb:
                                    kb.op("act", lambda e, qi=qi, ooff=ooff, obank=obank: e.copy(
                                        out=acc[qi].t[:, :], in_=obank.t[:, ooff:ooff + 129]),
                                        writes=[acc[qi].r, oreg])
                                elif qb >= 4:
                                    kb.op("dve", lambda e, qi=qi, ooff=ooff, obank=obank, n=n: e.scalar_tensor_tensor(
                                        out=acc[qi].t[:, :], in0=obank.t[:, ooff:ooff + 129], scalar=sel[qi].t[:, n:n + 1],
                                        in1=acc[qi].t[:, :], op0=ALU.mult, op1=ALU.add),
                                        reads=[sel[qi].r], writes=[acc[qi].r, oreg])
                                else:
                                    kb.op("dve", lambda e, qi=qi, ooff=ooff, obank=obank: e.tensor_tensor(
                                        out=acc[qi].t[:, :], in0=obank.t[:, ooff:ooff + 129], in1=acc[qi].t[:, :], op=ALU.add),
                                        writes=[acc[qi].r, oreg])

                        prev = None
                        for n in order:
                            cur = (n, emit_scores(n))
                            if prev is not None:
                                emit_pv(*prev)
                            prev = cur
                        emit_pv(*prev)
                        for qi in range(2):
                            qt = 2 * qb + qi
                            kb.op("dve", lambda e, qi=qi: e.reciprocal(out=rec[qi].t[:], in_=acc[qi].t[:, 128:129]),
                                  reads=[acc[qi].r], writes=[rec[qi].r])
                            kb.op("dve", lambda e, qi=qi: e.tensor_scalar_mul(
                                out=ya[qi].t[:], in0=acc[qi].t[:, 0:128], scalar1=rec[qi].t[:, 0:1]),
                                reads=[acc[qi].r, rec[qi].r], writes=[ya[qi].r])
                            tb = banks[7]
                            kb.op("pe", lambda e, qi=qi: e.matmul(
                                tb.t[:, qi * 128:(qi + 1) * 128], ya[qi].t[:], CB(C_ID), start=True, stop=True),
                                reads=[ya[qi].r, cstb.r], writes=[tb.r])
                            kb.op("act", lambda e, qi=qi, qt=qt: e.copy(
                                out=yT.t[:, qt * 128:(qt + 1) * 128], in_=tb.t[:, qi * 128:(qi + 1) * 128]),
                                writes=[yT.r, tb.r])
                    kb.dma("sp", yaT_d[h], yT.t[:], reads=[yT.r])
                    mod_group(8 + 2 * h, banks[7], 384)
                    mod_group(9 + 2 * h, banks[7], 384)
                kb.op("dve", lambda e: e.tensor_scalar_add(out=modT.t[:, 64:80], in0=modT.t[:, 64:80], scalar1=1.0),
                      reads=[modT.r], writes=[modT.r])
        es0.close()


        kb.barrier()
        bankctr = [0]

        def nb():
            b = banks[bankctr[0] % 8]
            bankctr[0] += 1
            return b

        if stop_after not in ("p0", "p1a"):
            with ExitStack() as es2:
                def sb2(name, shape, dt=F32, nreg=1):
                    return T(es2.enter_context(nc.sbuf_tensor(name, list(shape), dt)), nreg)
                print("sbuf remaining at GDN start", nc.sbuf_bytes_remaining)
                beta = sb2("beta", [128, NT, HV])
                lb = sb2("lb", [128, NT, HV])
                gg = sb2("gg", [128, NT, HV])
                nea = sb2("nea", [128, HV])
                egc = sb2("egc", [128, NT, HV])
                cdd = sb2("cdd", [128, NT, HV])
                ekd = sb2("ekd", [128, NT, HV])
                bkk = sb2("bkk", [128, NT, HV])
                ghl = sb2("ghl", [128, 2, NT, HV])
                lhl = sb2("lhl", [128, 2, NT, HV])
                cw = sb2("cw", [128, 32, 4])
                alog = sb2("alog", [128, HV])
                dtb = sb2("dtb", [128, HV])
                normw = sb2("normw", [128, 128])
                es2a = ExitStack()

                def sb2a(name, shape, dt=F32, nreg=1):
                    return T(es2a.enter_context(nc.sbuf_tensor(name, list(shape), dt)), nreg)
                wabt = sb2a("wabt", [128, KC, 32], BF16)
                kb.dma("pool", wabt.t[:].rearrange("p k c -> p (k c)"), wab_d, writes=[wabt.r])
                kb.dma("sp", cw.t[:].rearrange("p g t -> p (g t)"), conv_d, writes=[cw.r])
                kb.dma("sp", alog.t[:], alog_d, writes=[alog.r])
                kb.dma("sp", dtb.t[:], dtb_d, writes=[dtb.r])
                kb.dma("sp", normw.t[:], normw_d, writes=[normw.r])
                ab = sb2a("ab", [128, NT, 32])
                bk_ = nb()
                for t in range(NT):
                    for k in range(KC):
                        kb.op("pe", lambda e, t=t, k=k: e.matmul(
                            bk_.t[:, t * 32:(t + 1) * 32], hT.t[:, k, t * 128:(t + 1) * 128], wabt.t[:, k, :],
                            start=(k == 0), stop=(k == KC - 1)), reads=[wabt.r, hT_all[k]], writes=[bk_.r])
                kb.op("dve", lambda e: e.tensor_copy(out=ab.t[:], in_=bk_.t[:, :].rearrange("p (t c) -> p t c", c=32)),
                      writes=[ab.r, bk_.r])
                tmpa = sb2a("tmpa", [128, NT, HV])
                gcs = sb2a("gcs", [128, NT, 32])
                kb.op("act", lambda e: e.activation(out=beta.t[:], in_=ab.t[:, :, 0:16], func=AF.Sigmoid),
                      reads=[ab.r], writes=[beta.r])
                kb.op("act", lambda e: e.activation(out=lb.t[:], in_=beta.t[:], func=AF.Ln),
                      reads=[beta.r], writes=[lb.r])
                kb.op("dve", lambda e: e.tensor_tensor(
                    out=tmpa.t[:], in0=ab.t[:, :, 16:32], in1=dtb.t[:].unsqueeze(1).broadcast_to([128, NT, HV]),
                    op=ALU.add), reads=[ab.r, dtb.r], writes=[tmpa.r])
                kb.op("act", lambda e: e.activation(out=tmpa.t[:], in_=tmpa.t[:], func=AF.Exp),
                      reads=[tmpa.r], writes=[tmpa.r])
                kb.op("act", lambda e: e.activation(out=tmpa.t[:], in_=tmpa.t[:], func=AF.Ln, bias=1.0),
                      reads=[tmpa.r], writes=[tmpa.r])
                kb.op("act", lambda e: e.activation(out=nea.t[:], in_=alog.t[:], func=AF.Exp),
                      reads=[alog.r], writes=[nea.r])
                kb.op("dve", lambda e: e.scalar_tensor_tensor(
                    out=gg.t[:], in0=tmpa.t[:], scalar=-1.0, in1=nea.t[:].unsqueeze(1).broadcast_to([128, NT, HV]),
                    op0=ALU.mult, op1=ALU.mult), reads=[tmpa.r, nea.r], writes=[gg.r])
                bk_ = nb()
                for t in range(NT):
                    kb.op("pe", lambda e, t=t: e.matmul(bk_.t[:, t * 32:t * 32 + 16], CF(C_TRI), gg.t[:, t, :],
                                                        start=True, stop=True), reads=[cst.r, gg.r], writes=[bk_.r])
                    kb.op("pe", lambda e, t=t: e.matmul(bk_.t[:, t * 32 + 16:t * 32 + 32], CF(C_ONES), gg.t[:, t, :],
                                                        start=True, stop=True), reads=[cst.r, gg.r], writes=[bk_.r])
                kb.op("dve", lambda e: e.tensor_copy(out=gcs.t[:], in_=bk_.t[:, :].rearrange("p (t c) -> p t c", c=32)),
                      writes=[gcs.r, bk_.r])
                kb.op("act", lambda e: e.activation(out=egc.t[:], in_=gcs.t[:, :, 0:16], func=AF.Exp),
                      reads=[gcs.r], writes=[egc.r])
                kb.op("act", lambda e: e.activation(out=cdd.t[:], in_=gcs.t[:, :, 16:32], func=AF.Exp),
                      reads=[gcs.r], writes=[cdd.r])
                kb.op("dve", lambda e: e.tensor_tensor(out=ekd.t[:], in0=gcs.t[:, :, 16:32], in1=gcs.t[:, :, 0:16],
                                                       op=ALU.subtract), reads=[gcs.r], writes=[ekd.r])
                kb.op("act", lambda e: e.activation(out=ekd.t[:], in_=ekd.t[:], func=AF.Exp),
                      reads=[ekd.r], writes=[ekd.r])
                kb.op("dve", lambda e: e.tensor_tensor(out=bkk.t[:], in0=beta.t[:], in1=egc.t[:], op=ALU.mult),
                      reads=[beta.r, egc.r], writes=[bkk.r])

                tmpb = sb2a("tmpb", [128, NT, HV], BF16)
                for src, dst in ((gg, ghl), (lb, lhl)):
                    kb.op("dve", lambda e, src=src: e.tensor_copy(out=tmpb.t[:], in_=src.t[:]), reads=[src.r], writes=[tmpb.r])
                    kb.op("dve", lambda e, dst=dst: e.tensor_copy(out=dst.t[:, 0, :, :], in_=tmpb.t[:]), reads=[tmpb.r], writes=[dst.r])
                    kb.op("dve", lambda e, src=src, dst=dst: e.tensor_tensor(out=dst.t[:, 1, :, :], in0=src.t[:], in1=dst.t[:, 0, :, :],
                                                                             op=ALU.subtract), reads=[src.r], writes=[dst.r])
                kb.barrier()
                es2a.close()
                es2b = ExitStack()

                def sb2b(name, shape, dt=F32, nreg=1):
                    return T(es2b.enter_context(nc.sbuf_tensor(name, list(shape), dt)), nreg)
                wgb = [sb2b("wg%d" % i, [128, KC, 768], BF16) for i in range(2)]
                xpre = sb2b("xpre", [128, 4, 3 + S], BF16, nreg=4)
                kb.op("dve", lambda e: e.memset(xpre.t[:, :, 0:3], 0.0), writes=xpre.regs)
                dg = sb2b("dg", [128, 16, 128], BF16, nreg=16)
                qTg = sb2b("qTg", [128, S], BF16)
                kTg = sb2b("kTg", [128, S], BF16)
                ktok = sb2b("ktok", [128, NT, 128], BF16)
                vtok = sb2b("vtok", [128, NT, 256], BF16)
                zs = sb2b("zs", [128, NT, 256], BF16)
                cs = [sb2b("cs%d" % i, [128, 512]) for i in range(2)]
                sq = sb2b("sq", [128, 512], BF16)
                rinv = sb2b("rinv", [128, 512])
                vTt = [sb2b("vTt%d" % i, [128, 512], BF16) for i in range(2)]
                zt = [sb2b("zt%d" % i, [128, 512]) for i in range(2)]

                kb.dma("pool", wgb[0].t[:].rearrange("p k c -> p (k c)"), wgdn_d[0], writes=[wgb[0].r])
                ev = [0]

                def evac(out, in_, reads, writes):
                    ev[0] += 1
                    if ev[0] % 2:
                        kb.op("act", lambda e: e.copy(out=out, in_=in_), reads=reads, writes=writes)
                    else:
                        kb.op("dve", lambda e: e.tensor_copy(out=out, in_=in_), reads=reads, writes=writes)

                for j in range(8):
                    w = wgb[j % 2]
                    if j + 1 < 8:
                        wn_ = wgb[(j + 1) % 2]
                        kb.dma("pool", wn_.t[:].rearrange("p k c -> p (k c)"), wgdn_d[j + 1], writes=[wn_.r])
                    grps = [j, 8 + j, 16 + 2 * j, 17 + 2 * j]
                    for gi in range(4):
                        for tap in range(4):
                            kb.op("dve", lambda e, gi=gi, tap=tap: e.tensor_scalar_mul(
                                out=dg.t[:, gi * 4 + tap, :], in0=CF(C_ID), scalar1=cw.t[:, grps[gi], tap:tap + 1]),
                                reads=[cst.r, cw.r], writes=[dg.regs[gi * 4 + tap]])
                    for gi in range(4):
                        for g in range(4):
                            pb = nb()
                            for k in range(KC):
                                kb.op("pe", lambda e, k=k, g=g, gi=gi, pb=pb: e.matmul(
                                    pb.t[:, :], w.t[:, k, gi * 128:(gi + 1) * 128], hT.t[:, k, g * 512:(g + 1) * 512],
                                    start=(k == 0), stop=(k == KC - 1)), reads=[w.r, hT_all[k]], writes=[pb.r])
                            evac(xpre.t[:, gi, 3 + g * 512:3 + (g + 1) * 512], pb.t[:, :], [], [xpre.regs[gi], pb.r])
                    def z_unit(t2, w=w):
                        pb = nb()
                        for a in range(2):
                            t = t2 * 2 + a
                            for k in range(KC):
                                kb.op("pe", lambda e, k=k, t=t, a=a, pb=pb: e.matmul(
                                    pb.t[:, a * 256:(a + 1) * 256], hT.t[:, k, t * 128:(t + 1) * 128], w.t[:, k, 512:768],
                                    start=(k == 0), stop=(k == KC - 1)), reads=[w.r, hT_all[k]], writes=[pb.r])
                        z_ = zt[t2 % 2]
                        kb.op("act", lambda e, pb=pb, z_=z_: e.activation(out=z_.t[:], in_=pb.t[:, :], func=AF.Silu),
                              writes=[z_.r, pb.r])
                        kb.op("dve", lambda e, z_=z_, t2=t2: e.tensor_tensor(
                            out=zs.t[:, t2 * 2:t2 * 2 + 2, :].rearrange("p a (h c) -> p (a h) c", c=128),
                            in0=z_.t[:].rearrange("p (a c) -> p a c", c=128),
                            in1=normw.t[:].unsqueeze(1).broadcast_to([128, 4, 128]), op=ALU.mult),
                            reads=[z_.r, normw.r], writes=[zs.r])
                    for gi in range(4):
                        for g in range(4):
                            pb = nb()
                            for tap in range(4):
                                kb.op("pe", lambda e, tap=tap, g=g, gi=gi, pb=pb: e.matmul(
                                    pb.t[:, :], dg.t[:, gi * 4 + tap, :], xpre.t[:, gi, g * 512 + tap:g * 512 + tap + 512],
                                    start=(tap == 0), stop=(tap == 3)), reads=[dg.regs[gi * 4 + tap], xpre.regs[gi]], writes=[pb.r])
                            if gi < 2:
                                c_ = cs[g % 2]
                                kb.op("act", lambda e, pb=pb, c_=c_: e.activation(out=c_.t[:], in_=pb.t[:, :], func=AF.Silu),
                                      writes=[c_.r, pb.r])
                                kb.op("dve", lambda e, c_=c_: e.tensor_tensor(out=sq.t[:], in0=c_.t[:], in1=c_.t[:], op=ALU.mult),
                                      reads=[c_.r], writes=[sq.r])
                                pb2 = nb()
                                kb.op("pe", lambda e, pb2=pb2: e.matmul(pb2.t[:, :], CB(C_ONES), sq.t[:], start=True, stop=True),
                                      reads=[cstb.r, sq.r], writes=[pb2.r])
                                kb.op("act", lambda e, pb2=pb2: e.activation(
                                    out=rinv.t[:], in_=pb2.t[:, :], func=AF.Ln, bias=epsr.t[:, 0:1]),
                                    reads=[epsr.r], writes=[rinv.r, pb2.r])
                                kb.op("act", lambda e: e.activation(out=rinv.t[:], in_=rinv.t[:], func=AF.Exp, scale=-0.5),
                                      reads=[rinv.r], writes=[rinv.r])
                                dstT = qTg if gi == 0 else kTg
                                scl = SCALE_M if gi == 0 else 1.0
                                kb.op("dve", lambda e, c_=c_, dstT=dstT, scl=scl, g=g: e.scalar_tensor_tensor(
                                    out=dstT.t[:, g * 512:(g + 1) * 512], in0=c_.t[:], scalar=scl, in1=rinv.t[:],
                                    op0=ALU.mult, op1=ALU.mult), reads=[c_.r, rinv.r], writes=[dstT.r])
                                if gi == 1:
                                    pb3 = nb()
                                    for tt in range(4):
                                        t = g * 4 + tt
                                        kb.op("pe", lambda e, tt=tt, t=t, pb3=pb3: e.matmul(
                                            pb3.t[:, tt * 128:(tt + 1) * 128], kTg.t[:, t * 128:(t + 1) * 128], CB(C_ID),
                                            start=True, stop=True), reads=[kTg.r, cstb.r], writes=[pb3.r])
                                    evac(ktok.t[:, g * 4:(g + 1) * 4, :], pb3.t[:, :].rearrange("p (a c) -> p a c", c=128),
                                         [], [ktok.r, pb3.r])
                            else:
                                vt_ = vTt[g % 2]
                                kb.op("act", lambda e, pb=pb, vt_=vt_: e.activation(out=vt_.t[:], in_=pb.t[:, :], func=AF.Silu),
                                      writes=[vt_.r, pb.r])
                                pb3 = nb()
                                for tt in range(4):
                                    kb.op("pe", lambda e, tt=tt, pb3=pb3, vt_=vt_: e.matmul(
                                        pb3.t[:, tt * 128:(tt + 1) * 128], vt_.t[:, tt * 128:(tt + 1) * 128], CB(C_ID),
                                        start=True, stop=True), reads=[vt_.r, cstb.r], writes=[pb3.r])
                                evac(vtok.t[:, g * 4:(g + 1) * 4, (gi - 2) * 128:(gi - 1) * 128],
                                     pb3.t[:, :].rearrange("p (a c) -> p a c", c=128), [], [vtok.r, pb3.r])

                            if (gi * 4 + g) % 2 == 1:
                                z_unit((gi * 4 + g) // 2)
                    kb.dma("sp", gq_d[j], qTg.t[:], reads=[qTg.r])
                    kb.dma("sp", gk_d[j], kTg.t[:], reads=[kTg.r])
                    kb.dma("sp", gkt_d[j], ktok.t[:].rearrange("p a c -> p (a c)"), reads=[ktok.r])
                    kb.dma("sp", gvt_d[j], vtok.t[:].rearrange("p a c -> p (a c)"), reads=[vtok.r])
                    kb.dma("sp", gzs_d[j], zs.t[:].rearrange("p a c -> p (a c)"), reads=[zs.r])

                kb.barrier()
                es2b.close()
                class HS_:
                    pass
                INB = []
                for i in range(3):
                    b_ = HS_()
                    b_.q = sb2("inq%d" % i, [128, 256], BF16)
                    b_.k = sb2("ink%d" % i, [128, 256], BF16)
                    b_.kt = sb2("inkt%d" % i, [128, 2, 128], BF16)
                    b_.vt = sb2("invt%d" % i, [128, 2, 256], BF16)
                    b_.zs = sb2("inzs%d" % i, [128, 2, 256], BF16)
                    INB.append(b_)

                def load_inputs(gcp):
                    j_, cp_ = divmod(gcp, 8)
                    b_ = INB[gcp % 3]
                    kb.dma("sp", b_.q.t[:], gq_d[j_][:, cp_ * 256:(cp_ + 1) * 256], writes=[b_.q.r])
                    kb.dma("sp", b_.k.t[:], gk_d[j_][:, cp_ * 256:(cp_ + 1) * 256], writes=[b_.k.r])
                    kb.dma("sp", b_.kt.t[:].rearrange("p a c -> p (a c)"), gkt_d[j_][:, cp_ * 256:(cp_ + 1) * 256], writes=[b_.kt.r])
                    kb.dma("sp", b_.vt.t[:].rearrange("p a c -> p (a c)"), gvt_d[j_][:, cp_ * 512:(cp_ + 1) * 512], writes=[b_.vt.r])
                    kb.dma("sp", b_.zs.t[:].rearrange("p a c -> p (a c)"), gzs_d[j_][:, cp_ * 512:(cp_ + 1) * 512], writes=[b_.zs.r])
                KK = [sb2("kkm%d" % i, [128, 4, 128]) for i in range(2)]
                QK = [sb2("qkm%d" % i, [128, 128]) for i in range(2)]

                class HS:
                    pass
                SLOT = []
                for sl in range(2):
                    o_ = HS()
                    n = "s%d_" % sl
                    o_.LAp = sb2(n + "LAp", [128, 2, 2, 256], BF16)
                    o_.LDp = sb2(n + "LDp", [128, 2, 2, 256], BF16)
                    o_.LEp = sb2(n + "LEp", [128, 2, 2, 128], BF16)
                    o_.LFp = sb2(n + "LFp", [128, 2, 256], BF16)
                    o_.LGp = sb2(n + "LGp", [128, 2, 2, 128], BF16)
                    o_.LHp = sb2(n + "LHp", [128, 2, 256], BF16)
                    o_.TTp = [sb2(n + "TTp%d" % i, [128, 2, 128], BF16) for i in range(2)]
                    SLOT.append(o_)
                IT = []
                for sl in range(2):
                    row = []
                    for vh in range(2):
                        o_ = HS()
                        n = "i%d%d_" % (sl, vh)
                        o_.Bh = sb2(n + "Bh", [128, 2, 128], BF16); o_.Dl = sb2(n + "Dl", [128, 2, 128], BF16)
                        o_.E3 = sb2(n + "E3", [128, 384])
                        o_.V32 = sb2(n + "V32", [128, 512])
                        o_.Vall = sb2(n + "Vall", [128, 2, 512], BF16)
                        SP_ = SLOT[sl]
                        o_.LA = View(SP_.LAp.t[:, :, vh, :], SP_.LAp.r); o_.LB = sb2(n + "LB", [128, 2, 384], BF16)
                        o_.LC = sb2(n + "LC", [128, 2, 384], BF16); o_.LD = View(SP_.LDp.t[:, :, vh, :], SP_.LDp.r)
                        o_.LE = View(SP_.LEp.t[:, :, vh, :], SP_.LEp.r); o_.LF = View(SP_.LFp.t[:, vh, :], SP_.LFp.r)
                        o_.LG = View(SP_.LGp.t[:, :, vh, :], SP_.LGp.r); o_.LH = View(SP_.LHp.t[:, vh, :], SP_.LHp.r)
                        o_.kbg = sb2(n + "kbg", [128, 128], BF16)
                        for nm in ("attnT", "vbeta", "kd", "nwT"):
                            setattr(o_, nm, [sb2(n + nm + str(i), [128, 128], BF16) for i in range(2)])
                        o_.TT = [View(SP_.TTp[i].t[:, vh, :], SP_.TTp[i].r) for i in range(2)]
                        row.append(o_)
                    IT.append(row)
                HQ = []
                for vh in range(2):
                    o_ = HS()
                    n = "q%d_" % vh
                    o_.vnew = sb2(n + "vnew", [128, 128], BF16)
                    o_.S32 = sb2(n + "S32", [128, 128]); o_.Sbf = sb2(n + "Sbf", [128, 128], BF16)
                    o_.tmpo = [sb2(n + "tmpo%d" % i, [128, 128]) for i in range(2)]
                    o_.o = [sb2(n + "o%d" % i, [128, 128]) for i in range(2)]
                    o_.junk = o_.tmpo
                    o_.ssq = [sb2(n + "ssq%d" % i, [128, 1]) for i in range(2)]
                    o_.r1 = [sb2(n + "r1%d" % i, [128, 1]) for i in range(2)]
                    o_.r2 = [sb2(n + "r2%d" % i, [128, 1]) for i in range(2)]
                    o_.yb = [sb2(n + "yb%d" % i, [128, 128], BF16) for i in range(2)]
                    o_.ybT = sb2(n + "ybT", [128, 512], BF16)
                    HQ.append(o_)

                if True:
                    IDF = CF(C_ID)
                    IDB = CB(C_ID)
                    ASTB = CB(C_AST)
                    def make_stages(j, c, b_):
                        csl = slice(c * 128, (c + 1) * 128)
                        cl = c % 2
                        lsl = slice(cl * 128, (cl + 1) * 128)
                        pp = (c // 2) % 2
                        kkm = KK[c % 2]
                        qkm = QK[c % 2]

                        def st_shared():
                          pk = nb()
                          if True:
                            kb.op("pe", lambda e, pk=pk: e.matmul(pk.t[:, 0:128], b_.k.t[:, lsl], b_.k.t[:, lsl], start=True, stop=True),
                                  reads=[b_.k.r], writes=[pk.r])
                            kb.op("pe", lambda e, pk=pk: e.matmul(pk.t[:, 128:256], b_.k.t[:, lsl], b_.q.t[:, lsl], start=True, stop=True),
                                  reads=[b_.k.r, b_.q.r], writes=[pk.r])
                            kb.op("dve", lambda e, pk=pk: e.tensor_tensor(
                                out=kkm.t[:], in0=pk.t[:, 0:128].unsqueeze(1).broadcast_to([128, 4, 128]),
                                in1=cst.t[:, C_SUBD:C_SUBD + 4, :], op=ALU.mult), reads=[cst.r], writes=[kkm.r, pk.r])
                            kb.op("dve", lambda e, pk=pk: e.tensor_tensor(
                                out=qkm.t[:], in0=pk.t[:, 128:256], in1=CF(C_UI), op=ALU.mult),
                                reads=[cst.r], writes=[qkm.r, pk.r])

                        pdb = {}

                        def st_p0(vh):
                            H = IT[c % 2][vh]; Q = HQ[vh]
                            hv = 2 * j + vh
                            kb.op("dve", lambda e: e.tensor_scalar_mul(out=H.Bh.t[:, 0, :], in0=CF(C_TRI), scalar1=ghl.t[:, 0, c, hv:hv + 1]),
                                  reads=[cst.r, ghl.r], writes=[H.Bh.r])
                            kb.op("dve", lambda e: e.tensor_scalar_mul(out=H.Bh.t[:, 1, :], in0=CF(C_TRI), scalar1=ghl.t[:, 1, c, hv:hv + 1]),
                                  reads=[cst.r, ghl.r], writes=[H.Bh.r])
                            kb.op("act", lambda e: e.activation(out=H.Dl.t[:, 0, :], in_=CF(C_ID), func=AF.Identity, scale=lhl.t[:, 0, c, hv:hv + 1]),
                                  reads=[cst.r, lhl.r], writes=[H.Dl.r])
                            kb.op("act", lambda e: e.activation(out=H.Dl.t[:, 1, :], in_=CF(C_ID), func=AF.Identity, scale=lhl.t[:, 1, c, hv:hv + 1]),
                                  reads=[cst.r, lhl.r], writes=[H.Dl.r])
                            kb.op("act", lambda e: e.activation(out=H.kbg.t[:], in_=b_.kt.t[:, cl, :], func=AF.Identity, scale=bkk.t[:, c, hv:hv + 1]),
                                  reads=[b_.kt.r, bkk.r], writes=[H.kbg.r])
                            kb.op("act", lambda e: e.activation(out=H.vbeta[pp].t[:], in_=b_.vt.t[:, cl, vh * 128:(vh + 1) * 128],
                                                                func=AF.Identity, scale=beta.t[:, c, hv:hv + 1]),
                                  reads=[b_.vt.r, beta.r], writes=[H.vbeta[pp].r])
                            kb.op("act", lambda e: e.activation(out=H.kd[pp].t[:], in_=b_.kt.t[:, cl, :], func=AF.Identity, scale=ekd.t[:, c, hv:hv + 1]),
                                  reads=[b_.kt.r, ekd.r], writes=[H.kd[pp].r])

                        def st_p1(vh):
                            H = IT[c % 2][vh]; Q = HQ[vh]
                            pd = nb()
                            pdb[vh] = pd
                            Bh_, Bl_ = H.Bh.t[:, 0, :], H.Bh.t[:, 1, :]
                            Dh_, Dl_ = H.Dl.t[:, 0, :], H.Dl.t[:, 1, :]
                            rB = [cstb.r, H.Bh.r]
                            rD = [cstb.r, H.Dl.r]
                            mm(pd, 0, ASTB, Bh_, rB, True, False)
                            mm(pd, 0, ASTB, Bl_, rB, False, True)
                            mm(pd, 128, ASTB, Bh_, rB, True, False)
                            mm(pd, 128, ASTB, Bl_, rB, False, False)
                            mm(pd, 128, ASTB, Dh_, rD, False, False)
                            mm(pd, 128, ASTB, Dl_, rD, False, True)
                            mm(pd, 256, Bh_, ASTB, rB, True, False)
                            mm(pd, 256, Bl_, ASTB, rB, False, False)
                            mm(pd, 256, Dh_, ASTB, rD, False, False)
                            mm(pd, 256, Dl_, ASTB, rD, False, True)

                        def st_p2(vh):
                            H = IT[c % 2][vh]; Q = HQ[vh]
                            pd = pdb[vh]
                            kb.op("act", lambda e: e.activation(out=H.E3.t[:], in_=pd.t[:, 0:384], func=AF.Exp),
                                  writes=[H.E3.r, pd.r])

                        def st_p3(vh):
                            H = IT[c % 2][vh]; Q = HQ[vh]
                            kb.op("dve", lambda e: e.tensor_tensor(out=H.attnT[pp].t[:], in0=H.E3.t[:, 0:128], in1=qkm.t[:], op=ALU.mult),
                                  reads=[H.E3.r, qkm.r], writes=[H.attnT[pp].r])
                            V32 = H.V32.t[:].rearrange("p (a c) -> p a c", c=128)
                            kb.op("dve", lambda e: e.scalar_tensor_tensor(
                                out=V32[:, 0, :], in0=H.E3.t[:, 128:256], scalar=-1.0, in1=kkm.t[:, 0, :], op0=ALU.mult, op1=ALU.mult),
                                reads=[H.E3.r, kkm.r], writes=[H.V32.r])
                            kb.op("dve", lambda e: e.scalar_tensor_tensor(
                                out=V32[:, 1:4, :], in0=H.E3.t[:, 256:384].unsqueeze(1).broadcast_to([128, 3, 128]), scalar=-1.0,
                                in1=kkm.t[:, 1:4, :], op0=ALU.mult, op1=ALU.mult),
                                reads=[H.E3.r, kkm.r], writes=[H.V32.r])

                        def st_p4(vh):
                            H = IT[c % 2][vh]; Q = HQ[vh]
                            kb.op("act", lambda e: e.copy(out=H.Vall.t[:, 0, :], in_=H.V32.t[:]),
                                  reads=[H.V32.r], writes=[H.Vall.r])
                            kb.op("dve", lambda e: e.tensor_tensor(out=H.Vall.t[:, 1, :], in0=H.V32.t[:], in1=H.Vall.t[:, 0, :], op=ALU.subtract),
                                  reads=[H.V32.r], writes=[H.Vall.r])

                        def mm(pb, lo, lhsT, rhs, rds, start=True, stop=True):
                            kb.op("pe", lambda e: e.matmul(pb.t[:, lo:lo + 128], lhsT, rhs, start=start, stop=stop),
                                  reads=rds, writes=[pb.r])

                        def mmf(pb, lo, lhsT, rhs, rds, start=True, stop=True):
                            mm(pb, lo, lhsT, rhs, rds, start, stop)

                        def mm3(pb, lo, A, B, rds, first=True, last=True):
                            (Ah, Al), (Bh, Bl) = A, B
                            mm(pb, lo, Ah, Bh, rds, first, False)
                            mm(pb, lo, Ah, Bl, rds, False, False)
                            mm(pb, lo, Al, Bh, rds, False, last)

                        def mmI(pb, lo, B, rds, first, last):
                            Bh, Bl = B
                            mm(pb, lo, IDB, Bh, rds + [cstb.r], first, False)
                            mm(pb, lo, IDB, Bl, rds + [cstb.r], False, last)

                        def mmT(pb, lo, A, rds, first=True, last=True):
                            Ah, Al = A
                            mm(pb, lo, Ah, IDB, rds + [cstb.r], first, False)
                            mm(pb, lo, Al, IDB, rds + [cstb.r], False, last)

                        def P(tile, lo, w=128):
                            return (tile.t[:, 0, lo:lo + w], tile.t[:, 1, lo:lo + w])

                        def VV(H, a):
                            return (H.Vall.t[:, 0, a * 128:(a + 1) * 128], H.Vall.t[:, 1, a * 128:(a + 1) * 128])

                        def split_psum(dst, pb, W):
                            kb.op("act", lambda e: e.copy(out=dst.t[:, 0, :], in_=pb.t[:, 0:W]), writes=[dst.r, pb.r])
                            kb.op("dve", lambda e: e.tensor_tensor(out=dst.t[:, 1, :], in0=pb.t[:, 0:W], in1=dst.t[:, 0, :],
                                                                   op=ALU.subtract), writes=[dst.r, pb.r])

                        pbs = {}

                        def pair_bank(name, vh):
                            if vh == 0:
                                pbs[name] = nb()
                            return pbs[name], vh * 256

                        def pview(pb, W):
                            return pb.t[:, :].rearrange("p (i w) -> p i w", w=256)[:, :, 0:W]

                        def split_pair(dst, pb, W):
                            kb.op("act", lambda e: e.copy(out=dst.t[:, 0, :, 0:W], in_=pview(pb, W)), writes=[dst.r, pb.r])
                            kb.op("dve", lambda e: e.tensor_tensor(out=dst.t[:, 1, :, 0:W], in0=pview(pb, W), in1=dst.t[:, 0, :, 0:W],
                                                                   op=ALU.subtract), writes=[dst.r, pb.r])

                        def st_A(vh):
                            H = IT[c % 2][vh]; Q = HQ[vh]
                            Vd, Vtd = VV(H, 0), VV(H, 1)
                            pb, off = pair_bank("A", vh)
                            mm3(pb, off + 0, Vtd, Vd, [H.Vall.r])
                            mm3(pb, off + 128, Vd, Vtd, [H.Vall.r])
                            if vh == 1:
                                split_pair(SLOT[c % 2].LAp, pb, 256)

                        def st_B(vh):
                            H = IT[c % 2][vh]; Q = HQ[vh]
                            V2, Vt2 = P(H.LA, 0), P(H.LA, 128)
                            Vd = VV(H, 0)
                            pb = nb()
                            mm3(pb, 0, Vt2, V2, [H.LA.r])
                            mm3(pb, 128, V2, Vt2, [H.LA.r])
                            mm(pb, 256, IDB, IDB, [cstb.r], True, False)
                            mmI(pb, 256, Vd, [H.Vall.r], False, False)
                            mmT(pb, 256, Vt2, [H.LA.r], False, False)
                            mm3(pb, 256, Vt2, Vd, [H.LA.r, H.Vall.r], False, True)
                            split_psum(H.LB, pb, 384)

                        def st_C(vh):
                            H = IT[c % 2][vh]; Q = HQ[vh]
                            V4, Vt4, Y1 = P(H.LB, 0), P(H.LB, 128), P(H.LB, 256)
                            pb = nb()
                            mm3(pb, 0, Vt4, V4, [H.LB.r])
                            mm3(pb, 128, V4, Vt4, [H.LB.r])
                            mm3(pb, 256, Vt4, Y1, [H.LB.r], True, False)
                            mmI(pb, 256, Y1, [H.LB.r], False, True)
                            split_psum(H.LC, pb, 384)

                        def st_D(vh):
                            H = IT[c % 2][vh]; Q = HQ[vh]
                            V8, Vt8, Y2 = P(H.LC, 0), P(H.LC, 128), P(H.LC, 256)
                            pb, off = pair_bank("D", vh)
                            mm3(pb, off + 0, V8, Vt8, [H.LC.r])
                            mm3(pb, off + 128, Vt8, Y2, [H.LC.r], True, False)
                            mmI(pb, off + 128, Y2, [H.LC.r], False, True)
                            if vh == 1:
                                split_pair(SLOT[c % 2].LDp, pb, 256)

                        def st_E(vh):
                            H = IT[c % 2][vh]; Q = HQ[vh]
                            Vt16, Y3 = P(H.LD, 0), P(H.LD, 128)
                            pb, off = pair_bank("E", vh)
                            mm3(pb, off, Vt16, Y3, [H.LD.r], True, False)
                            mmI(pb, off, Y3, [H.LD.r], False, True)
                            if vh == 1:
                                split_pair(SLOT[c % 2].LEp, pb, 128)

                        def st_F(vh):
                            H = IT[c % 2][vh]; Q = HQ[vh]
                            T0t = P(H.LE, 0)
                            pb, off = pair_bank("F", vh)
                            mm(pb, off + 0, T0t[0], IDB, [H.LE.r, cstb.r])
                            mm(pb, off + 128, VV(H, 2)[0], T0t[0], [H.Vall.r, H.LE.r])
                            if vh == 1:
                                LFp = SLOT[c % 2].LFp
                                evac(LFp.t[:, :, :], pview(pb, 256), [], [LFp.r, pb.r])

                        def st_G(vh):
                            H = IT[c % 2][vh]; Q = HQ[vh]
                            pb, off = pair_bank("G", vh)
                            mm(pb, off, H.LF.t[:, 0:128], H.LF.t[:, 128:256], [H.LF.r], True, False)
                            mmI(pb, off, P(H.LE, 0), [H.LE.r], False, True)
                            if vh == 1:
                                split_pair(SLOT[c % 2].LGp, pb, 128)

                        def st_H(vh):
                            H = IT[c % 2][vh]; Q = HQ[vh]
                            T1t = P(H.LG, 0)
                            pb, off = pair_bank("H", vh)
                            mm(pb, off + 0, T1t[0], IDB, [H.LG.r, cstb.r])
                            mm(pb, off + 128, VV(H, 3)[0], T1t[0], [H.Vall.r, H.LG.r])
                            if vh == 1:
                                LHp = SLOT[c % 2].LHp
                                evac(LHp.t[:, :, :], pview(pb, 256), [], [LHp.r, pb.r])

                        def st_I(vh):
                            H = IT[c % 2][vh]; Q = HQ[vh]
                            pb, off = pair_bank("I", vh)
                            mm(pb, off, H.LH.t[:, 0:128], H.LH.t[:, 128:256], [H.LH.r], True, False)
                            mmI(pb, off, P(H.LG, 0), [H.LG.r], False, True)
                            if vh == 1:
                                TTp = SLOT[c % 2].TTp[pp]
                                evac(TTp.t[:, :, :], pview(pb, 128), [], [TTp.r, pb.r])

                        def st_W(vh):
                            H = IT[c % 2][vh]; Q = HQ[vh]
                            pb = nb()
                            mmf(pb, 0, H.kbg.t[:], H.TT[pp].t[:], [H.kbg.r, H.TT[pp].r])
                            kb.op("act", lambda e: e.activation(out=H.nwT[pp].t[:], in_=pb.t[:, 0:128], func=AF.Copy, scale=-1.0),
                                  writes=[H.nwT[pp].r, pb.r])

                        sqb = {}

                        def st_V1(vh):
                            H = IT[c % 2][vh]; Q = HQ[vh]
                            pb = nb()
                            sqb[("V", vh)] = pb
                            mmf(pb, 0, H.TT[pp].t[:], H.vbeta[pp].t[:], [H.TT[pp].r, H.vbeta[pp].r], True, c == 0)
                            if c > 0:
                                mmf(pb, 0, H.nwT[pp].t[:], Q.Sbf.t[:], [H.nwT[pp].r, Q.Sbf.r], False, True)

                        def st_V2(vh):
                            H = IT[c % 2][vh]; Q = HQ[vh]
                            pb = sqb[("V", vh)]
                            evac(Q.vnew.t[:], pb.t[:, 0:128], [], [Q.vnew.r, pb.r])

                        def st_OS1(vh):
                            H = IT[c % 2][vh]; Q = HQ[vh]
                            pb = nb()
                            sqb[("O", vh)] = pb
                            mmf(pb, 128, H.attnT[pp].t[:], Q.vnew.t[:], [H.attnT[pp].r, Q.vnew.r])
                            if c > 0:
                                mmf(pb, 0, b_.q.t[:, lsl], Q.Sbf.t[:], [b_.q.r, Q.Sbf.r])
                            if c < NT - 1:
                                ps_ = nb()
                                sqb[("S", vh)] = ps_
                                mmf(ps_, 0, H.kd[pp].t[:], Q.vnew.t[:], [H.kd[pp].r, Q.vnew.r])

                        def st_OS2(vh):
                            H = IT[c % 2][vh]; Q = HQ[vh]
                            hv = 2 * j + vh
                            pb = sqb[("O", vh)]
                            if c > 0:
                                kb.op("act", lambda e: e.activation(out=Q.tmpo[c % 2].t[:], in_=pb.t[:, 0:128], func=AF.Identity,
                                                                    scale=egc.t[:, c, hv:hv + 1]),
                                      reads=[egc.r], writes=[Q.tmpo[c % 2].r, pb.r])
                            if c < NT - 1:
                                ps_ = sqb[("S", vh)]
                                if c > 0:
                                    kb.op("dve", lambda e: e.scalar_tensor_tensor(
                                        out=Q.S32.t[:], in0=Q.S32.t[:], scalar=cdd.t[:, c, hv:hv + 1], in1=ps_.t[:, 0:128],
                                        op0=ALU.mult, op1=ALU.add), reads=[cdd.r], writes=[Q.S32.r, ps_.r])
                                else:
                                    kb.op("dve", lambda e: e.tensor_copy(out=Q.S32.t[:], in_=ps_.t[:, 0:128]), writes=[Q.S32.r, ps_.r])

                        def st_OS3(vh):
                            H = IT[c % 2][vh]; Q = HQ[vh]
                            pb = sqb[("O", vh)]
                            if c > 0:
                                kb.op("dve", lambda e: e.tensor_tensor(out=Q.o[c % 2].t[:], in0=pb.t[:, 128:256], in1=Q.tmpo[c % 2].t[:], op=ALU.add),
                                      reads=[Q.tmpo[c % 2].r], writes=[Q.o[c % 2].r, pb.r])
                            else:
                                kb.op("dve", lambda e: e.tensor_copy(out=Q.o[c % 2].t[:], in_=pb.t[:, 128:256]), writes=[Q.o[c % 2].r, pb.r])
                            if c < NT - 1:
                                kb.op("act", lambda e: e.copy(out=Q.Sbf.t[:], in_=Q.S32.t[:]), reads=[Q.S32.r], writes=[Q.Sbf.r])

                        def st_Y1(vh):
                            H = IT[c % 2][vh]; Q = HQ[vh]
                            kb.op("dve", lambda e: e.memset(Q.ssq[c % 2].t[:], 0.0), writes=[Q.ssq[c % 2].r])
                            kb.op("act", lambda e: e.activation(out=Q.junk[c % 2].t[:], in_=Q.o[c % 2].t[:], func=AF.Square, accum_out=Q.ssq[c % 2].t[:, 0:1]),
                                  reads=[Q.o[c % 2].r], writes=[Q.junk[c % 2].r, Q.ssq[c % 2].r])

                        def st_Y2(vh):
                            H = IT[c % 2][vh]; Q = HQ[vh]
                            kb.op("dve", lambda e: e.tensor_scalar(out=Q.r1[c % 2].t[:], in0=Q.ssq[c % 2].t[:], scalar1=1.0 / 128.0, scalar2=RMS_EPS,
                                                                   op0=ALU.mult, op1=ALU.add), reads=[Q.ssq[c % 2].r], writes=[Q.r1[c % 2].r])

                        def st_Y3(vh):
                            H = IT[c % 2][vh]; Q = HQ[vh]
                            kb.op("act", lambda e: e.activation(out=Q.r2[c % 2].t[:], in_=Q.r1[c % 2].t[:], func=AF.Ln),
                                  reads=[Q.r1[c % 2].r], writes=[Q.r2[c % 2].r])
                            kb.op("act", lambda e: e.activation(out=Q.r2[c % 2].t[:], in_=Q.r2[c % 2].t[:], func=AF.Exp, scale=-0.5),
                                  reads=[Q.r2[c % 2].r], writes=[Q.r2[c % 2].r])

                        def st_Y4(vh):
                            H = IT[c % 2][vh]; Q = HQ[vh]
                            kb.op("dve", lambda e: e.scalar_tensor_tensor(
                                out=Q.yb[c % 2].t[:], in0=Q.o[c % 2].t[:], scalar=Q.r2[c % 2].t[:, 0:1], in1=b_.zs.t[:, cl, vh * 128:(vh + 1) * 128],
                                op0=ALU.mult, op1=ALU.mult), reads=[Q.o[c % 2].r, Q.r2[c % 2].r, b_.zs.r], writes=[Q.yb[c % 2].r])

                        def st_Y5(vh):
                            H = IT[c % 2][vh]; Q = HQ[vh]
                            pb = nb()
                            sqb[("Y", vh)] = pb
                            kb.op("pe", lambda e: e.matmul(pb.t[:, 0:128], Q.yb[c % 2].t[:], CB(C_ID), start=True, stop=True),
                                  reads=[Q.yb[c % 2].r, cstb.r], writes=[pb.r])

                        def st_Y6(vh):
                            H = IT[c % 2][vh]; Q = HQ[vh]
                            hv = 2 * j + vh
                            pb = sqb[("Y", vh)]
                            evac(Q.ybT.t[:, (c % 4) * 128:(c % 4 + 1) * 128], pb.t[:, 0:128], [], [Q.ybT.r, pb.r])
                            if c % 4 == 3:
                                kb.dma("sp", ybT_d[hv][:, (c - 3) * 128:(c + 1) * 128], Q.ybT.t[:], reads=[Q.ybT.r])

                        return dict(shared=st_shared, p0=st_p0, p1=st_p1, p2=st_p2, p3=st_p3, p4=st_p4, A=st_A, B=st_B, C=st_C, D=st_D, E=st_E, F=st_F, G=st_G,
                                    H=st_H, I=st_I, W=st_W,
                                    V=lambda vh: (st_V1(vh), st_V2(vh)),
                                    OS=lambda vh: (st_OS1(vh), st_OS2(vh), st_OS3(vh)),
                                    Y1=st_Y1, Y2=st_Y2, Y3=st_Y3, Y4=st_Y4,
                                    Y56=lambda vh: (st_Y5(vh), st_Y6(vh)))

                    chain_order = ["p0", "shared", "p1", "p2", "p3", "p4", "A", "B", "C", "D", "E", "F", "G", "H", "I", "W"]

                    def run_stage(stg, name):
                        if name == "shared":
                            stg[name]()
                        else:
                            for vh in range(2):
                                stg[name](vh)
                    prev_pair = None
                    NG = 8 * (NT // 2)
                    load_inputs(0)
                    for gcp in range(NG + 1):
                        if gcp + 1 < NG:
                            load_inputs(gcp + 1)
                        if gcp < NG:
                            j_, cp_ = divmod(gcp, 8)
                            cur = [make_stages(j_, 2 * cp_, INB[gcp % 3]), make_stages(j_, 2 * cp_ + 1, INB[gcp % 3])]
                        else:
                            cur = None
                        steps = []
                        if prev_pair is not None:
                            A_, B_ = prev_pair
                            steps = [[(A_, "V")], [(A_, "OS")], [(A_, "Y1"), (B_, "V")], [(A_, "Y2"), (B_, "OS")],
                                     [(A_, "Y3"), (B_, "Y1")], [(A_, "Y4"), (B_, "Y2")], [(A_, "Y56"), (B_, "Y3")],
                                     [(B_, "Y4")], [(B_, "Y56")]]
                            steps = [[], []] + steps
                        for ci, name in enumerate(chain_order):
                            if cur is not None:
                                for stg in cur:
                                    run_stage(stg, name)
                            if ci < len(steps):
                                for stg, nm in steps[ci]:
                                    run_stage(stg, nm)
                        for extra in steps[len(chain_order):]:
                            for stg, nm in extra:
                                run_stage(stg, nm)
                        prev_pair = cur


        if stop_after not in ("p0", "p1a", "p1b"):
            kb.barrier()
            with ExitStack() as es3:
                def sb3(name, shape, dt=F32, nreg=1):
                    return T(es3.enter_context(nc.sbuf_tensor(name, list(shape), dt)), nreg)
                yaR = sb3("yaR", [128, 8, S], BF16)
                ybR = sb3("ybR", [128, 16, S], BF16)
                for h in range(8):
                    kb.dma("sp", yaR.t[:, h, :], yaT_d[h], writes=[yaR.r])
                for h in range(16):
                    kb.dma("sp", ybR.t[:, h, :], ybT_d[h], writes=[ybR.r])
                wg_ = [sb3("w2g%d" % i, [128, KC, 256], BF16) for i in range(2)]
                wa_ = [sb3("w2a%d" % i, [128, 8, 128], BF16) for i in range(2)]
                wb_ = [sb3("w2b%d" % i, [128, 16, 128], BF16) for i in range(2)]
                sga = sb3("sga", [128, 512])
                sgb = sb3("sgb", [128, 512])
                t1 = sb3("t1", [128, 512])
                mst = [sb3("mst%d" % i, [128, 512], BF16) for i in range(2)]

                def load2(cc):
                    i = cc % 2
                    kb.dma("pool", wg_[i].t[:].rearrange("p k c -> p (k c)"), wgate_d[cc], writes=[wg_[i].r])
                    kb.dma("pool", wa_[i].t[:].rearrange("p k c -> p (k c)"), wpm_d[cc], writes=[wa_[i].r])
                    kb.dma("pool", wb_[i].t[:].rearrange("p k c -> p (k c)"), wpg_d[cc], writes=[wb_[i].r])
                load2(0)
                mi = 0
                for cc in range(16):
                    if cc + 1 < 16:
                        load2(cc + 1)
                    i = cc % 2
                    for g in range(4):
                        gs = slice(g * 512, (g + 1) * 512)
                        pA, pB, pC, pD = nb(), nb(), nb(), nb()
                        for k in range(KC):
                            kb.op("pe", lambda e, k=k: e.matmul(pA.t[:, :], wg_[i].t[:, k, 0:128], hT.t[:, k, gs],
                                                                start=(k == 0), stop=(k == KC - 1)),
                                  reads=[wg_[i].r, hT_all[k]], writes=[pA.r])
                        kb.op("act", lambda e: e.activation(out=sga.t[:], in_=pA.t[:, :], func=AF.Sigmoid),
                              writes=[sga.r, pA.r])
                        for k in range(KC):
                            kb.op("pe", lambda e, k=k: e.matmul(pB.t[:, :], wg_[i].t[:, k, 128:256], hT.t[:, k, gs],
                                                                start=(k == 0), stop=(k == KC - 1)),
                                  reads=[wg_[i].r, hT_all[k]], writes=[pB.r])
                        kb.op("act", lambda e: e.activation(out=sgb.t[:], in_=pB.t[:, :], func=AF.Sigmoid),
                              writes=[sgb.r, pB.r])
                        for k in range(8):
                            kb.op("pe", lambda e, k=k: e.matmul(pC.t[:, :], wa_[i].t[:, k, :], yaR.t[:, k, gs],
                                                                start=(k == 0), stop=(k == 7)),
                                  reads=[wa_[i].r, yaR.r], writes=[pC.r])
                        kb.op("dve", lambda e: e.tensor_tensor(out=sga.t[:], in0=pC.t[:, :], in1=sga.t[:], op=ALU.mult),
                              writes=[sga.r, pC.r])
                        for k in range(16):
                            kb.op("pe", lambda e, k=k: e.matmul(pD.t[:, :], wb_[i].t[:, k, :], ybR.t[:, k, gs],
                                                                start=(k == 0), stop=(k == 15)),
                                  reads=[wb_[i].r, ybR.r], writes=[pD.r])
                        kb.op("dve", lambda e: e.tensor_tensor(out=t1.t[:], in0=pD.t[:, :], in1=sgb.t[:], op=ALU.mult),
                              reads=[sgb.r], writes=[t1.r, pD.r])
                        m_ = mst[mi % 2]
                        mi += 1
                        kb.op("dve", lambda e, m_=m_: e.tensor_tensor(out=m_.t[:], in0=t1.t[:], in1=sga.t[:], op=ALU.add),
                              reads=[t1.r, sga.r], writes=[m_.r])
                        kb.dma("sp", mgT_d[cc][:, gs], m_.t[:], reads=[m_.r])

        def layer_norm_tile(pre, junk, st, lng, lnb):
            kb.op("dve", lambda e: e.memset(st.t[:, 0:2], 0.0), writes=[st.r])
            kb.op("act", lambda e: e.activation(out=junk.t[:], in_=pre.t[:], func=AF.Copy, accum_out=st.t[:, 0:1]),
                  reads=[pre.r], writes=[junk.r, st.r])
            kb.op("act", lambda e: e.activation(out=junk.t[:], in_=pre.t[:], func=AF.Square, accum_out=st.t[:, 1:2]),
                  reads=[pre.r], writes=[junk.r, st.r])
            kb.op("dve", lambda e: e.tensor_scalar_mul(out=st.t[:, 2:4], in0=st.t[:, 0:2], scalar1=1.0 / D),
                  reads=[st.r], writes=[st.r])
            kb.op("dve", lambda e: e.tensor_tensor(out=st.t[:, 4:5], in0=st.t[:, 2:3], in1=st.t[:, 2:3], op=ALU.mult),
                  reads=[st.r], writes=[st.r])
            kb.op("dve", lambda e: e.tensor_tensor(out=st.t[:, 5:6], in0=st.t[:, 3:4], in1=st.t[:, 4:5], op=ALU.subtract),
                  reads=[st.r], writes=[st.r])
            kb.op("act", lambda e: e.activation(out=st.t[:, 6:7], in_=st.t[:, 5:6], func=AF.Ln, bias=epsr.t[:, 1:2]),
                  reads=[st.r, epsr.r], writes=[st.r])
            kb.op("act", lambda e: e.activation(out=st.t[:, 6:7], in_=st.t[:, 6:7], func=AF.Exp, scale=-0.5),
                  reads=[st.r], writes=[st.r])
            kb.op("dve", lambda e: e.scalar_tensor_tensor(out=st.t[:, 7:8], in0=st.t[:, 2:3], scalar=-1.0, in1=st.t[:, 6:7],
                                                          op0=ALU.mult, op1=ALU.mult), reads=[st.r], writes=[st.r])
            kb.op("act", lambda e: e.activation(out=pre.t[:], in_=pre.t[:], func=AF.Identity,
                                                scale=st.t[:, 6:7], bias=st.t[:, 7:8]),
                  reads=[st.r], writes=[pre.r])
            kb.op("dve", lambda e: e.tensor_tensor(out=pre.t[:], in0=pre.t[:], in1=lng.t[:], op=ALU.mult),
                  reads=[lng.r], writes=[pre.r])
            kb.op("dve", lambda e: e.tensor_tensor(out=pre.t[:], in0=pre.t[:], in1=lnb.t[:], op=ALU.add),
                  reads=[lnb.r], writes=[pre.r])

        if stop_after not in ("p0", "p1a", "p1b", "p2"):
            kb.barrier()
            with ExitStack() as es4:
                def sb4(name, shape, dt=F32, nreg=1):
                    return T(es4.enter_context(nc.sbuf_tensor(name, list(shape), dt)), nreg)
                wout = sb4("wout", [128, KC, D], BF16)
                for g in range(4):
                    kb.dma("pool", wout.t[:, :, g * 512:(g + 1) * 512], wout_d[g].rearrange("p (k c) -> p k c", c=512),
                           writes=[wout.r])
                mgR = sb4("mgR", [128, KC, 512], BF16)
                g1b = sb4("g1b", [128, D])
                lng = sb4("lng", [128, D])
                lnb = sb4("lnb", [128, D])
                kb.dma("sp", lng.t[:], lnrep_d[0], writes=[lng.r])
                kb.dma("sp", lnb.t[:], lnrep_d[1], writes=[lnb.r])
                xt = [sb4("xt0", [128, D])]
                pres = [sb4("pre%d" % i, [128, D]) for i in range(2)]
                junk = sb4("junk3", [128, D], BF16)
                st = sb4("st3", [128, 8])
                dgt = sb4("dgt", [128, 128])
                for k4 in range(4):
                    pb = nb()
                    for kk in range(4):
                        k = k4 * 4 + kk
                        kb.op("dve", lambda e, k=k: e.tensor_scalar_mul(out=dgt.t[:], in0=CF(C_ID), scalar1=modT.t[:, 32 + k:33 + k]),
                              reads=[cst.r, modT.r], writes=[dgt.r])
                        kb.op("pe", lambda e, kk=kk, pb=pb: e.matmul(pb.t[:, kk * 128:(kk + 1) * 128], CF(C_ONES), dgt.t[:],
                                                                     start=True, stop=True), reads=[cst.r, dgt.r], writes=[pb.r])
                    kb.op("dve", lambda e, k4=k4, pb=pb: e.tensor_copy(out=g1b.t[:, k4 * 512:(k4 + 1) * 512], in_=pb.t[:, :]),
                          writes=[g1b.r, pb.r])
                def p3_main(t):
                    g = t // 4
                    if t % 4 == 0:
                        kb.dma("sp", mgR.t[:], mgT_d[:, :, g * 512:(g + 1) * 512].rearrange("k p c -> p k c"), writes=[mgR.r])
                    x_ = xt[0]
                    pre = pres[t % 2]
                    kb.dma("sp", x_.t[:], x_d[t], writes=[x_.r])
                    tl = slice((t % 4) * 128, (t % 4 + 1) * 128)
                    for cg in range(4):
                        pb = nb()
                        for k in range(KC):
                            kb.op("pe", lambda e, k=k, cg=cg, pb=pb: e.matmul(
                                pb.t[:, :], mgR.t[:, k, tl], wout.t[:, k, cg * 512:(cg + 1) * 512],
                                start=(k == 0), stop=(k == KC - 1)), reads=[mgR.r, wout.r], writes=[pb.r])
                        cs_ = slice(cg * 512, (cg + 1) * 512)
                        kb.op("dve", lambda e, pb=pb, cs_=cs_, pre=pre: e.tensor_tensor(out=pre.t[:, cs_], in0=pb.t[:, :], in1=g1b.t[:, cs_],
                                                                              op=ALU.mult), reads=[g1b.r], writes=[pre.r, pb.r])
                    kb.op("dve", lambda e, x_=x_, pre=pre: e.scalar_tensor_tensor(out=pre.t[:], in0=x_.t[:], scalar=ALPHA, in1=pre.t[:],
                                                                         op0=ALU.mult, op1=ALU.add), reads=[x_.r], writes=[pre.r])
                    layer_norm_tile(pre, junk, st, lng, lnb)
                    kb.dma("pool", x1_d[t], pre.t[:], reads=[pre.r])
                    return pre
                def p3_tr(t, pre):
                    for k4 in range(4):
                        pb = nb()
                        for kk in range(4):
                            k = k4 * 4 + kk
                            kb.op("pe", lambda e, k=k, kk=kk, pb=pb, pre=pre: e.transpose(
                                pb.t[:, kk * 128:(kk + 1) * 128], pre.t[:, k * 128:(k + 1) * 128], CF(C_ID)),
                                reads=[pre.r, cst.r], writes=[pb.r])
                        for kk in range(4):
                            k = k4 * 4 + kk
                            kb.op("act", lambda e, k=k, kk=kk, pb=pb, t=t: e.activation(
                                out=hT.t[:, k, t * 128:(t + 1) * 128], in_=pb.t[:, kk * 128:(kk + 1) * 128], func=AF.Identity,
                                scale=modT.t[:, 64 + k:65 + k], bias=modT.t[:, 48 + k:49 + k]),
                                reads=[modT.r], writes=[hT_all[k], pb.r])
                pend = None
                for t in range(NT + 1):
                    cur = (t, p3_main(t)) if t < NT else None
                    if pend is not None:
                        p3_tr(*pend)
                    pend = cur

            kb.barrier()
            with ExitStack() as es5:
                def sb5(name, shape, dt=F32, nreg=1):
                    return T(es5.enter_context(nc.sbuf_tensor(name, list(shape), dt)), nreg)
                GT = 1024
                actT = sb5("actT", [128, FC, GT], BF16, nreg=2)
                wfi = [sb5("wfi%d" % i, [128, KC, 256], BF16) for i in range(2)]
                wfo = [sb5("wfo%d" % i, [128, FC, 128], BF16) for i in range(2)]
                sgs = [sb5("sg%d" % i, [128, 512]) for i in range(2)]
                y2c = [sb5("y2c%d" % i, [128, 512]) for i in range(2)]
                y2st = [sb5("y2st%d" % i, [128, 4, 128]) for i in range(1)]
                si = 0
                yi = 0
                pend4 = None

                def p4_tr(yc, ys, oc, t0):
                    pT = nb()
                    for tt in range(4):
                        kb.op("pe", lambda e, tt=tt: e.transpose(
                            pT.t[:, tt * 128:(tt + 1) * 128], yc.t[:, tt * 128:(tt + 1) * 128], CF(C_ID)),
                            reads=[yc.r, cst.r], writes=[pT.r])
                    kb.op("dve", lambda e: e.tensor_copy(
                        out=ys.t[:], in_=pT.t[:, :].rearrange("p (a c) -> p a c", c=128)), writes=[ys.r, pT.r])
                    kb.dma("sp", y2_d[t0:t0 + 4, :, oc * 128:(oc + 1) * 128].rearrange("a p c -> p a c"), ys.t[:],
                           reads=[ys.r])
                for g in range(S // GT):
                    for fc in range(FC):
                        w_ = wfi[fc % 2]
                        kb.dma("pool", w_.t[:].rearrange("p k c -> p (k c)"), wffi_d[fc], writes=[w_.r])
                        for hf in range(GT // 512):
                            gs = slice(g * GT + hf * 512, g * GT + (hf + 1) * 512)
                            pG, pU = nb(), nb()
                            sg_ = sgs[si % 2]
                            si += 1
                            for k in range(KC):
                                kb.op("pe", lambda e, k=k, w_=w_, pG=pG, gs=gs: e.matmul(pG.t[:, :], w_.t[:, k, 0:128], hT.t[:, k, gs],
                                                                                        start=(k == 0), stop=(k == KC - 1)),
                                      reads=[w_.r, hT_all[k]], writes=[pG.r])
                            kb.op("act", lambda e, pG=pG, sg_=sg_: e.activation(out=sg_.t[:], in_=pG.t[:, :], func=AF.Silu),
                                  writes=[sg_.r, pG.r])
                            for k in range(KC):
                                kb.op("pe", lambda e, k=k, w_=w_, pU=pU, gs=gs: e.matmul(pU.t[:, :], w_.t[:, k, 128:256], hT.t[:, k, gs],
                                                                                        start=(k == 0), stop=(k == KC - 1)),
                                      reads=[w_.r, hT_all[k]], writes=[pU.r])
                            kb.op("dve", lambda e, pU=pU, fc=fc, hf=hf, sg_=sg_: e.tensor_tensor(
                                out=actT.t[:, fc, hf * 512:(hf + 1) * 512], in0=pU.t[:, :], in1=sg_.t[:], op=ALU.mult),
                                reads=[sg_.r], writes=[actT.regs[hf], pU.r])
                    for oc in range(16):
                        w_ = wfo[oc % 2]
                        kb.dma("pool", w_.t[:].rearrange("p k c -> p (k c)"), wffo_d[oc], writes=[w_.r])
                        for hf in range(GT // 512):
                            pY = nb()
                            for fc in range(FC):
                                kb.op("pe", lambda e, fc=fc, w_=w_, pY=pY, hf=hf: e.matmul(
                                    pY.t[:, :], w_.t[:, fc, :], actT.t[:, fc, hf * 512:(hf + 1) * 512],
                                    start=(fc == 0), stop=(fc == FC - 1)), reads=[w_.r, actT.regs[hf]], writes=[pY.r])
                            yc = y2c[yi % 2]
                            kb.op("act", lambda e, pY=pY, yc=yc, oc=oc: e.activation(out=yc.t[:], in_=pY.t[:, :], func=AF.Identity,
                                                                                     scale=modT.t[:, 80 + oc:81 + oc]),
                                  reads=[modT.r], writes=[yc.r, pY.r])
                            if pend4 is not None:
                                p4_tr(*pend4)
                            pend4 = (yc, y2st[0], oc, (g * GT + hf * 512) // 128)
                            yi += 1
                            continue
                            pT = nb()
                            for tt in range(4):
                                kb.op("pe", lambda e, tt=tt, yc=yc, pT=pT: e.transpose(
                                    pT.t[:, tt * 128:(tt + 1) * 128], yc.t[:, tt * 128:(tt + 1) * 128], CF(C_ID)),
                                    reads=[yc.r, cst.r], writes=[pT.r])
                            ys = y2st[yi % 2]
                            yi += 1
                            kb.op("dve", lambda e, pT=pT, ys=ys: e.tensor_copy(
                                out=ys.t[:], in_=pT.t[:, :].rearrange("p (a c) -> p a c", c=128)), writes=[ys.r, pT.r])
                            t0 = (g * GT + hf * 512) // 128
                            kb.dma("sp", y2_d[t0:t0 + 4, :, oc * 128:(oc + 1) * 128].rearrange("a p c -> p a c"), ys.t[:],
                                   reads=[ys.r])
                if pend4 is not None:
                    p4_tr(*pend4)
                    pend4 = None
            kb.barrier()
            with ExitStack() as es6:
                def sb6(name, shape, dt=F32, nreg=1):
                    return T(es6.enter_context(nc.sbuf_tensor(name, list(shape), dt)), nreg)
                lng2 = sb6("lng2", [128, D])
                lnb2 = sb6("lnb2", [128, D])
                junk2 = sb6("junk6", [128, D], BF16)
                st2 = sb6("st6", [128, 8])
                xqs = [sb6("xq%d" % i, [128, D]) for i in range(2)]
                y2t = [sb6("y2t%d" % i, [128, D]) for i in range(2)]
                kb.dma("sp", lng2.t[:], lnrep_d[2], writes=[lng2.r])
                kb.dma("sp", lnb2.t[:], lnrep_d[3], writes=[lnb2.r])
                for t in range(NT):
                    xq = xqs[t % 2]
                    yt_ = y2t[t % 2]
                    kb.dma("sp", xq.t[:], x1_d[t], writes=[xq.r])
                    kb.dma("sp", yt_.t[:], y2_d[t], writes=[yt_.r])
                    kb.op("dve", lambda e, xq=xq, yt_=yt_: e.scalar_tensor_tensor(out=xq.t[:], in0=xq.t[:], scalar=ALPHA, in1=yt_.t[:],
                                                                                 op0=ALU.mult, op1=ALU.add),
                          reads=[yt_.r], writes=[xq.r])
                    layer_norm_tile(xq, junk2, st2, lng2, lnb2)
                    final_toks.append(kb.dma("pool", out_d[t], xq.t[:], reads=[xq.r]))

        if dbg:
            dbg_outs["yaT"] = nc.dram_tensor("dbg_yaT", [NH_M, 128, S], BF16, kind="ExternalOutput").ap()
            with ExitStack() as esd:
                tmpd = T(esd.enter_context(nc.sbuf_tensor("tmpd", [128, S], BF16)))
                for h in range(NH_M):
                    kb.dma("sp", tmpd.t[:], yaT_d[h], writes=[tmpd.r])
                    final_toks.append(kb.dma("sp", dbg_outs["yaT"][h], tmpd.t[:], reads=[tmpd.r]))
        if dbg and stop_after not in ("p0", "p1a"):
            dbg_outs["ybT"] = nc.dram_tensor("dbg_ybT", [HV, 128, S], BF16, kind="ExternalOutput").ap()
            with ExitStack() as esd:
                tmpd = T(esd.enter_context(nc.sbuf_tensor("tmpd2", [128, S], BF16)))
                for h in range(HV):
                    kb.dma("sp", tmpd.t[:], ybT_d[h], writes=[tmpd.r])
                    final_toks.append(kb.dma("sp", dbg_outs["ybT"][h], tmpd.t[:], reads=[tmpd.r]))
        for tok in final_toks:
            kb.wait_tok("sp", tok)
        for key, n in kb.dcnt.items():
            if n > 0:
                kb.wait_tok("sp", (key, 16 * n))
        print("instructions:", kb.n_inst, "waits:", kb.n_wait, {k: v for k, v in kb.cnt.items()})
    return nc


def _klay(w):
    K, C = w.shape
    return np.ascontiguousarray(w.reshape(K // 128, 128, C).transpose(1, 0, 2))


def _rel_bucket_np(n):
    n = np.asarray(n)
    max_exact = 16
    nn = np.maximum(n, 0)
    nf = np.maximum(nn, 1).astype(np.float32)
    large = max_exact + (np.log(nf / np.float32(max_exact)) / np.float32(math.log(128 / max_exact))
                         * np.float32(32 - max_exact)).astype(np.int32)
    large = np.minimum(large, 31)
    return np.where(nn < max_exact, nn, large)


def make_consts():
    p = np.arange(128)[:, None]
    f = np.arange(128)[None, :]
    c = np.zeros((128, NCONST, 128), np.float32)
    c[:, C_ID] = (p == f)
    c[:, C_TRI] = (p <= f)
    c[:, C_AST] = (p > f)
    bd = (p // 32) == (f // 32)
    c[:, C_SUBD] = (f > p) & bd
    c[:, C_SLBD] = (f < p) & bd
    c[:, C_M1] = ((p // 32) % 2 == 1) & ((f // 32) == (p // 32) - 1)
    c[:, C_M2] = ((p // 32) >= 2) & ((f // 32) < 2)
    c[:, C_UI] = (f >= p)
    c[:, C_ONES] = 1.0
    return c


def prep_shared(inp):
    f32 = np.float32
    sh = {}
    w_ada = inp["w_ada"][0]
    wl = w_ada.reshape(KC, 128, 96, 128).transpose(2, 1, 0, 3)
    wl = wl.reshape(24, 4, 128, KC, 128).transpose(0, 2, 1, 3, 4)
    sh["wada_lay"] = np.ascontiguousarray(wl).reshape(24, 128, 4 * KC * 128)
    sh["bada_lay"] = np.ascontiguousarray(inp["b_ada"][0].reshape(96, 128).T)
    sh["consts"] = make_consts()
    w_in = inp["w_in"][0]
    o1 = 3072
    o2 = o1 + 4096
    o3 = o2 + 2048
    o5 = o3 + 32
    wm = np.empty((NH_M, 128, KC, 384), f32)
    for h in range(NH_M):
        cols = np.concatenate([np.arange(h * 128, (h + 1) * 128), 1024 + np.arange(h * 128, (h + 1) * 128),
                               2048 + np.arange(h * 128, (h + 1) * 128)])
        wm[h] = _klay(w_in[:, cols])
    sh["wmoba_lay"] = wm.reshape(NH_M, 128, KC * 384)
    rb = inp["rel_bias"]
    i = np.arange(128)[:, None]
    j = np.arange(128)[None, :]
    bd = _rel_bucket_np(j - i)
    bo = _rel_bucket_np(128 + j - i)
    mb = np.empty((128, 16, 128), f32)
    for h in range(NH_M):
        mb[:, h, :] = rb[bd, h]
        mb[:, 8 + h, :] = rb[bo, h]
    sh["mbias_lay"] = mb
    sh["c31_rep"] = np.ascontiguousarray(np.broadcast_to(rb[31][None, :], (128, NH_M))).astype(f32)
    wg = np.empty((8, 128, KC, 768), f32)
    for jh in range(8):
        cols = np.concatenate([o1 + np.arange(jh * 128, (jh + 1) * 128),
                               o1 + 1024 + np.arange(jh * 128, (jh + 1) * 128),
                               o1 + 2048 + np.arange(jh * 256, (jh + 1) * 256),
                               o2 + np.arange(jh * 256, (jh + 1) * 256)])
        wg[jh] = _klay(w_in[:, cols])
    sh["wgdn_lay"] = wg.reshape(8, 128, KC * 768)
    sh["wab_lay"] = _klay(w_in[:, o3:o5]).reshape(128, KC * 32)
    cw = inp["conv_w"][0]
    sh["conv_lay"] = np.ascontiguousarray(cw.reshape(4, 32, 128).transpose(2, 1, 0)).reshape(128, 128)
    sh["alog_rep"] = np.ascontiguousarray(np.broadcast_to(inp["a_log"][0][None, :], (128, HV))).astype(f32)
    sh["dtb_rep"] = np.ascontiguousarray(np.broadcast_to(inp["dt_bias"][0][None, :], (128, HV))).astype(f32)
    sh["normw_rep"] = np.ascontiguousarray(np.broadcast_to(inp["gdn_norm_w"][0][None, :], (128, 128))).astype(f32)
    wgate = w_in[:, o5:o5 + 4096]
    wgl = np.empty((16, 128, KC, 256), f32)
    for cc in range(16):
        cols = np.concatenate([np.arange(cc * 128, (cc + 1) * 128), 2048 + np.arange(cc * 128, (cc + 1) * 128)])
        wgl[cc] = _klay(wgate[:, cols])
    sh["wgate_lay"] = wgl.reshape(16, 128, KC * 256)
    wpm = inp["w_proj_moba"][0]
    wpg = inp["w_proj_gdn"][0]
    sh["wpm_lay"] = np.stack([_klay(wpm[:, cc * 128:(cc + 1) * 128]) for cc in range(16)]).reshape(16, 128, 8 * 128)
    sh["wpg_lay"] = np.stack([_klay(wpg[:, cc * 128:(cc + 1) * 128]) for cc in range(16)]).reshape(16, 128, 16 * 128)
    wo = inp["w_out"][0]
    sh["wout_lay"] = np.stack([_klay(wo[:, g * 512:(g + 1) * 512]) for g in range(4)]).reshape(4, 128, KC * 512)
    sh["ln_rep"] = np.stack([np.broadcast_to(inp[k][0][None, :], (128, D)) for k in
                             ("ln1_g", "ln1_b", "ln2_g", "ln2_b")]).astype(f32)
    wfi = inp["w_ffn_in"][0]
    wfl = np.empty((FC, 128, KC, 256), f32)
    for fc in range(FC):
        cols = np.concatenate([np.arange(fc * 128, (fc + 1) * 128), DFF + np.arange(fc * 128, (fc + 1) * 128)])
        wfl[fc] = _klay(wfi[:, cols])
    sh["wffi_lay"] = wfl.reshape(FC, 128, KC * 256)
    wfo = inp["w_ffn_out"][0]
    sh["wffo_lay"] = np.stack([_klay(wfo[:, oc * 128:(oc + 1) * 128]) for oc in range(16)]).reshape(16, 128, FC * 128)
    return sh


def prep_core(inp, b):
    x = inp["x"][b]
    return {
        "xT": np.ascontiguousarray(x.T).reshape(KC, 128, S),
        "x": np.ascontiguousarray(x).reshape(NT, 128, D),
        "c_lay": np.ascontiguousarray(inp["c"][b].reshape(KC, 128).T),
    }


_NC_CACHE = {}


def kernel(**inputs):
    inp = {k: np.asarray(v, dtype=np.float32) for k, v in inputs.items()}
    if "nc" not in _NC_CACHE:
        _NC_CACHE["nc"] = build_nc()
    nc = _NC_CACHE["nc"]
    shared = prep_shared(inp)
    in_maps = []
    for b in range(8):
        m = dict(shared)
        m.update(prep_core(inp, b))
        in_maps.append(m)
    res = run_bass_kernel_spmd(nc, in_maps, core_ids=list(range(8)))
    out = np.stack([np.asarray(r["out"]).reshape(S, D) for r in res.results], axis=0)
    return out.astype(np.float32)
```

```python
import math
from contextlib import ExitStack

import numpy as np
import concourse.bass as bass
import concourse.mybir as mybir
from concourse.bass_utils import run_bass_kernel_spmd

F32 = mybir.dt.float32
BF16 = mybir.dt.bfloat16
AF = mybir.ActivationFunctionType
ALU = mybir.AluOpType
AX = mybir.AxisListType

D = 2048
S = 2048
NT = 16
KC = 16
NH_M = 8
HV = 16
DFF = 5632
FC = DFF // 128
N_IN = 13344
ALPHA = 2.0 ** 0.25
LN_EPS = 1e-5
RMS_EPS = 1e-6
SCALE_M = 128 ** -0.5

C_ID, C_TRI, C_AST, C_SUBD, C_SLBD, C_M1, C_M2, C_UI, C_ONES = range(9)
NCONST = 9


class Reg:
    __slots__ = ("w", "r")

    def __init__(self):
        self.w = None
        self.r = {}


class KB:
    def __init__(self, nc, es):
        self.nc = nc
        self.engs = {"pe": nc.tensor, "act": nc.scalar, "dve": nc.vector, "pool": nc.gpsimd, "sp": nc.sync}
        self.semobj = {}
        self.cnt = {}
        self.waited = {}
        for name in self.engs:
            self.semobj[name] = es.enter_context(nc.semaphore("s_" + name))
            self.cnt[name] = 0
            self.waited[name] = {}
        self.nd = 12
        self.dnext = {"sp": 0, "pool": 0}
        self.dcnt = {}
        for q in ("sp", "pool"):
            for k in range(self.nd):
                key = ("d", q, k)
                self.semobj[key] = es.enter_context(nc.semaphore("d_%s%d" % (q, k)))
                self.dcnt[key] = 0
        self.n_wait = 0
        self.n_inst = 0

    def _collect(self, reads, writes):
        deps = {}
        for r in reads:
            if r.w is not None:
                k, v = r.w
                if deps.get(k, 0) < v:
                    deps[k] = v
        for w in writes:
            if w.w is not None:
                k, v = w.w
                if deps.get(k, 0) < v:
                    deps[k] = v
            for k, v in w.r.items():
                if deps.get(k, 0) < v:
                    deps[k] = v
        return deps

    def _wait(self, eng, deps, attach=False):
        wd = self.waited[eng]
        need = []
        for k, v in deps.items():
            if eng == "pe" and k == "pe":
                continue
            if wd.get(k, 0) < v:
                need.append((k, v))
                wd[k] = v
        pend = None
        if attach and need:
            pend = need.pop()
        for k, v in need:
            self.engs[eng].wait_ge(self.semobj[k], v)
            self.n_wait += 1
        return pend

    def _update(self, tok, reads, writes):
        k, v = tok
        for w in writes:
            w.w = tok
            w.r = {}
        for r in reads:
            if r.r.get(k, 0) < v:
                r.r[k] = v

    def op(self, eng, fn, reads=(), writes=()):
        pend = self._wait(eng, self._collect(reads, writes), attach=True)
        inst = fn(self.engs[eng])
        if pend is not None:
            inst._wait_ge(self.semobj[pend[0]], pend[1])
        inst.then_inc(self.semobj[eng], 1)
        self.cnt[eng] += 1
        self.n_inst += 1
        self._update((eng, self.cnt[eng]), reads, writes)

    def dma(self, q, out, in_, reads=(), writes=()):
        k = self.dnext[q]
        self.dnext[q] = (k + 1) % self.nd
        key = ("d", q, k)
        deps = self._collect(reads, writes)
        if self.dcnt[key] > 0:
            deps[key] = max(deps.get(key, 0), 16 * self.dcnt[key])
        pend = self._wait(q, deps, attach=True)
        inst = self.engs[q].dma_start(out=out, in_=in_)
        if pend is not None:
            inst._wait_ge(self.semobj[pend[0]], pend[1])
        inst.then_inc(self.semobj[key], 16)
        self.dcnt[key] += 1
        self.n_inst += 1
        tok = (key, 16 * self.dcnt[key])
        self._update(tok, reads, writes)
        return tok

    def barrier(self):
        deps = {}
        for name in self.engs:
            if self.cnt[name] > 0:
                deps[name] = self.cnt[name]
        for key, n in self.dcnt.items():
            if n > 0:
                deps[key] = 16 * n
        for eng in self.engs:
            self._wait(eng, dict(deps))

    def wait_tok(self, eng, tok):
        self._wait(eng, {tok[0]: tok[1]})


class T:
    def __init__(self, t, nreg=1):
        self.t = t
        self.regs = [Reg() for _ in range(nreg)]

    @property
    def r(self):
        return self.regs[0]


class View:
    def __init__(self, ap, reg):
        self.t = ap
        self.regs = [reg]

    @property
    def r(self):
        return self.regs[0]


def build_nc(stop_after=None, dbg=False):
    nc = bass.Bass("TRN2", target_bir_lowering=False)

    def din(name, shape, dt=F32):
        return nc.dram_tensor(name, list(shape), dt, kind="ExternalInput").ap()

    xT_d = din("xT", [KC, 128, S])
    x_d = din("x", [NT, 128, D])
    c_d = din("c_lay", [128, KC])
    wada_d = din("wada_lay", [24, 128, 4 * KC * 128])
    bada_d = din("bada_lay", [128, 96])
    consts_d = din("consts", [128, NCONST, 128])
    wmoba_d = din("wmoba_lay", [NH_M, 128, KC * 384])
    mbias_d = din("mbias_lay", [128, 16, 128])
    c31_d = din("c31_rep", [128, NH_M])
    wgdn_d = din("wgdn_lay", [8, 128, KC * 768])
    wab_d = din("wab_lay", [128, KC * 32])
    conv_d = din("conv_lay", [128, 32 * 4])
    alog_d = din("alog_rep", [128, HV])
    dtb_d = din("dtb_rep", [128, HV])
    normw_d = din("normw_rep", [128, 128])
    wgate_d = din("wgate_lay", [16, 128, KC * 256])
    wpm_d = din("wpm_lay", [16, 128, 8 * 128])
    wpg_d = din("wpg_lay", [16, 128, 16 * 128])
    wout_d = din("wout_lay", [4, 128, KC * 512])
    lnrep_d = din("ln_rep", [4, 128, D])
    wffi_d = din("wffi_lay", [FC, 128, KC * 256])
    wffo_d = din("wffo_lay", [16, 128, FC * 128])

    out_d = nc.dram_tensor("out", [NT, 128, D], F32, kind="ExternalOutput").ap()

    def dscr(name, shape, dt):
        return nc.dram_tensor(name, list(shape), dt, kind="Internal").ap()

    yaT_d = dscr("yaT_s", [NH_M, 128, S], BF16)
    ybT_d = dscr("ybT_s", [HV, 128, S], BF16)
    sgT_d = dscr("sgT_s", [32, 128, S], BF16)
    mgT_d = dscr("mgT_s", [KC, 128, S], BF16)
    x1_d = dscr("x1_s", [NT, 128, D], F32)
    y2_d = dscr("y2_s", [NT, 128, D], F32)
    gq_d = dscr("gq_s", [8, 128, S], BF16)
    gk_d = dscr("gk_s", [8, 128, S], BF16)
    gkt_d = dscr("gkt_s", [8, 128, NT * 128], BF16)
    gvt_d = dscr("gvt_s", [8, 128, NT * 256], BF16)
    gzs_d = dscr("gzs_s", [8, 128, NT * 256], BF16)
    dbg_outs = {}

    with ExitStack() as es:
        kb = KB(nc, es)

        def sb(name, shape, dt=F32, nreg=1):
            return T(es.enter_context(nc.sbuf_tensor(name, list(shape), dt)), nreg)

        banks = [T(es.enter_context(nc.psum_tensor("ps%d" % i, [128, 512], F32))) for i in range(8)]

        cst = sb("cst", [128, NCONST, 128])
        kb.dma("sp", cst.t[:], consts_d, writes=[cst.r])
        cstb = sb("cstb", [128, NCONST, 128], BF16)
        kb.op("dve", lambda e: e.tensor_copy(out=cstb.t[:], in_=cst.t[:]), reads=[cst.r], writes=[cstb.r])

        epsr = sb("epsr", [128, 2])
        kb.op("dve", lambda e: e.memset(epsr.t[:, 0:1], RMS_EPS), writes=[epsr.r])
        kb.op("dve", lambda e: e.memset(epsr.t[:, 1:2], LN_EPS), writes=[epsr.r])

        def CF(i):
            return cst.t[:, i, :]

        def CB(i):
            return cstb.t[:, i, :]

        c_sb = sb("c_sb", [128, KC])
        sc_bf = sb("sc_bf", [128, KC], BF16)
        kb.dma("sp", c_sb.t[:], c_d, writes=[c_sb.r])
        kb.op("act", lambda e: e.activation(out=sc_bf.t[:], in_=c_sb.t[:], func=AF.Silu),
              reads=[c_sb.r], writes=[sc_bf.r])
        bada = sb("bada", [128, 96])
        kb.dma("sp", bada.t[:], bada_d, writes=[bada.r])
        modT = sb("modT", [128, 96])
        hT = sb("hT", [128, KC, S], BF16, nreg=KC)

        es0 = ExitStack()

        def sb0(name, shape, dt=F32, nreg=1):
            return T(es0.enter_context(nc.sbuf_tensor(name, list(shape), dt)), nreg)
        wab = [sb0("wada%d" % i, [128, 4, KC, 128], BF16) for i in range(2)]
        xb = [sb0("xTb%d" % i, [128, S]) for i in range(2)]

        def mod_group(g, pm, col0):
            wb = wab[g % 2]
            kb.dma("pool", wb.t[:].rearrange("p a k c -> p (a k c)"), wada_d[g], writes=[wb.r])
            for a in range(4):
                for k in range(KC):
                    kb.op("pe", lambda e, a=a, k=k: e.matmul(
                        pm.t[:, col0 + a:col0 + a + 1], wb.t[:, a, k, :], sc_bf.t[:, k:k + 1],
                        start=(k == 0), stop=(k == KC - 1)),
                        reads=[wb.r, sc_bf.r], writes=[pm.r])
            kb.op("dve", lambda e: e.tensor_tensor(out=modT.t[:, 4 * g:4 * g + 4], in0=pm.t[:, col0:col0 + 4],
                                                   in1=bada.t[:, 4 * g:4 * g + 4], op=ALU.add),
                  reads=[bada.r], writes=[modT.r, pm.r])
        for g in range(8):
            mod_group(g, banks[0], 4 * g)
        kb.op("dve", lambda e: e.tensor_scalar_add(out=modT.t[:, 16:32], in0=modT.t[:, 16:32], scalar1=1.0),
              reads=[modT.r], writes=[modT.r])
        for k in range(KC):
            b_ = xb[k % 2]
            kb.dma("sp", b_.t[:], xT_d[k], writes=[b_.r])
            kb.op("act", lambda e, k=k, b_=b_: e.activation(
                out=hT.t[:, k, :], in_=b_.t[:], func=AF.Identity,
                scale=modT.t[:, 16 + k:17 + k], bias=modT.t[:, k:k + 1]),
                reads=[b_.r, modT.r], writes=[hT.regs[k]])

        if dbg:
            dbg_outs["modT"] = nc.dram_tensor("dbg_modT", [128, 96], F32, kind="ExternalOutput").ap()
            kb.dma("sp", dbg_outs["modT"], modT.t[:], reads=[modT.r])

        hT_all = hT.regs
        final_toks = []

        if stop_after != "p0":
            with ExitStack() as es1:
                def sb1(name, shape, dt=F32, nreg=1):
                    return T(es1.enter_context(nc.sbuf_tensor(name, list(shape), dt)), nreg)
                wm = [sb1("wm%d" % i, [128, KC, 384], BF16) for i in range(2)]
                qT = sb1("qT", [128, S], BF16, nreg=4)
                kT = sb1("kT", [128, S], BF16, nreg=4)
                V1 = sb1("V1", [128, NT, 129], BF16, nreg=5)
                kmf = sb1("kmf", [128, 8])
                kmT = sb1("kmT", [128, 8], BF16)
                PT = [sb1("PT%d" % i, [128, 256], BF16) for i in range(4)]
                tmpE = [sb1("tmpE%d" % i, [128, 256], BF16) for i in range(2)]
                acc = [sb1("acc%d" % i, [128, 129]) for i in range(2)]
                rt = [sb1("rt%d" % i, [128, 8]) for i in range(2)]
                cmpb = [sb1("cmp%d" % i, [128, 8, 8]) for i in range(2)]
                rank = [sb1("rank%d" % i, [128, 8]) for i in range(2)]
                sel = [sb1("sel%d" % i, [128, 8]) for i in range(2)]
                rec = [sb1("rec%d" % i, [128, 1]) for i in range(2)]
                ya = [sb1("ya%d" % i, [128, 128], BF16) for i in range(2)]
                yaT = [sb1("yaT%d" % i, [128, S], BF16) for i in range(2)]
                mb = sb1("mb", [128, 16, 128])
                mbe = sb1("mbe", [128, 16, 128])
                Ed = sb1("Ed", [128, 8, 128], BF16)
                Eo = sb1("Eo", [128, 8, 128], BF16)
                c31 = sb1("c31", [128, NH_M])

                kb.dma("sp", mb.t[:], mbias_d, writes=[mb.r])
                kb.dma("sp", c31.t[:], c31_d, writes=[c31.r])
                kb.op("act", lambda e: e.activation(out=mbe.t[:], in_=mb.t[:], func=AF.Exp),
                      reads=[mb.r], writes=[mbe.r])
                kb.op("dve", lambda e: e.tensor_tensor(
                    out=Ed.t[:], in0=mbe.t[:, 0:8, :],
                    in1=cst.t[:, C_UI:C_UI + 1, :].broadcast_to([128, 8, 128]), op=ALU.mult),
                    reads=[mbe.r, cst.r], writes=[Ed.r])
                kb.op("dve", lambda e: e.tensor_copy(out=Eo.t[:], in_=mbe.t[:, 8:16, :]),
                      reads=[mbe.r], writes=[Eo.r])
                kb.op("dve", lambda e: e.memset(V1.t[:, :, 128:129], 1.0), writes=[V1.regs[4]])

                kb.dma("pool", wm[0].t[:].rearrange("p k c -> p (k c)"), wmoba_d[0], writes=[wm[0].r])
                s_slot = 0
                o_slot = 0
                p_slot = 0
                for h in range(NH_M):
                    w = wm[h % 2]
                    if h + 1 < NH_M:
                        wn = wm[(h + 1) % 2]
                        kb.dma("pool", wn.t[:].rearrange("p k c -> p (k c)"), wmoba_d[h + 1], writes=[wn.r])
                    for which, dst in ((0, qT), (1, kT)):
                        for g in range(4):
                            pb = banks[g % 2]
                            for k in range(KC):
                                kb.op("pe", lambda e, k=k, g=g, pb=pb, which=which: e.matmul(
                                    pb.t[:, :], w.t[:, k, which * 128:(which + 1) * 128],
                                    hT.t[:, k, g * 512:(g + 1) * 512], start=(k == 0), stop=(k == KC - 1)),
                                    reads=[w.r, hT_all[k]], writes=[pb.r])
                            kb.op("act", lambda e, g=g, pb=pb, dst=dst: e.copy(
                                out=dst.t[:, g * 512:(g + 1) * 512], in_=pb.t[:, :]),
                                writes=[dst.regs[g], pb.r])
                    for g in range(4):
                        pb = banks[g % 2]
                        for tt in range(4):
                            t = g * 4 + tt
                            for k in range(KC):
                                kb.op("pe", lambda e, k=k, t=t, tt=tt, pb=pb: e.matmul(
                                    pb.t[:, tt * 128:(tt + 1) * 128], hT.t[:, k, t * 128:(t + 1) * 128],
                                    w.t[:, k, 256:384], start=(k == 0), stop=(k == KC - 1)),
                                    reads=[w.r, hT_all[k]], writes=[pb.r])
                        kb.op("dve", lambda e, g=g, pb=pb: e.tensor_copy(
                            out=V1.t[:, g * 4:(g + 1) * 4, 0:128],
                            in_=pb.t[:, :].rearrange("p (a c) -> p a c", c=128)),
                            writes=[V1.regs[g], pb.r])
                    kb.op("dve", lambda e: e.tensor_reduce(
                        out=kmf.t[:], in_=kT.t[:].rearrange("p (n b) -> p n b", b=256), axis=AX.X, op=ALU.add),
                        reads=kT.regs, writes=[kmf.r])
                    kb.op("dve", lambda e: e.tensor_scalar_mul(out=kmT.t[:], in0=kmf.t[:], scalar1=1.0 / 256.0),
                          reads=[kmf.r], writes=[kmT.r])
                    yT = yaT[h % 2]
                    for qb in range(8):
                        q0 = qb * 256
                        if qb >= 4:
                            for qi in range(2):
                                rp = banks[7]
                                kb.op("pe", lambda e, qi=qi: e.matmul(
                                    rp.t[:, qi * 8:qi * 8 + 8], qT.t[:, q0 + qi * 128:q0 + (qi + 1) * 128],
                                    kmT.t[:, :], start=True, stop=True),
                                    reads=[qT.regs[qb // 2], kmT.r], writes=[rp.r])
                                kb.op("dve", lambda e, qi=qi: e.tensor_copy(out=rt[qi].t[:], in_=rp.t[:, qi * 8:qi * 8 + 8]),
                                      writes=[rt[qi].r, rp.r])
                                kb.op("dve", lambda e, qi=qi: e.tensor_tensor(
                                    out=cmpb[qi].t[:, 0:qb, 0:qb],
                                    in0=rt[qi].t[:, 0:qb].unsqueeze(1).broadcast_to([128, qb, qb]),
                                    in1=rt[qi].t[:, 0:qb].unsqueeze(2).broadcast_to([128, qb, qb]),
                                    op=ALU.is_gt), reads=[rt[qi].r], writes=[cmpb[qi].r])
                                kb.op("dve", lambda e, qi=qi: e.tensor_reduce(
                                    out=rank[qi].t[:, 0:qb], in_=cmpb[qi].t[:, 0:qb, 0:qb], axis=AX.X, op=ALU.add),
                                    reads=[cmpb[qi].r], writes=[rank[qi].r])
                                kb.op("dve", lambda e, qi=qi: e.tensor_single_scalar(
                                    out=sel[qi].t[:, 0:qb], in_=rank[qi].t[:, 0:qb], scalar=2.5, op=ALU.is_lt),
                                    reads=[rank[qi].r], writes=[sel[qi].r])
                        order = [qb] + list(range(qb))
                        staged = []

                        def emit_scores(n):
                            nonlocal s_slot, p_slot
                            res = []
                            for kt in (2 * n, 2 * n + 1):
                                sbank = banks[s_slot % 4]
                                sreg = sbank.r
                                soff = 0
                                s_slot += 1
                                pt = PT[p_slot % 4]
                                p_slot += 1
                                te = tmpE[p_slot % 2]
                                ksl = slice(kt * 128, (kt + 1) * 128)
                                kreg = kT.regs[kt // 4]
                                qreg = qT.regs[qb // 2]
                                if n < qb:
                                    kb.op("pe", lambda e, ksl=ksl, soff=soff, sbank=sbank: e.matmul(
                                        sbank.t[:, soff:soff + 256], kT.t[:, ksl], qT.t[:, q0:q0 + 256],
                                        start=True, stop=True), reads=[kreg, qreg], writes=[sreg])
                                    if kt == 2 * qb - 1:
                                        kb.op("act", lambda e, soff=soff, sbank=sbank, te=te: e.activation(
                                            out=te.t[:, 0:128], in_=sbank.t[:, soff:soff + 128], func=AF.Exp,
                                            scale=SCALE_M), writes=[te.r, sreg])
                                        kb.op("dve", lambda e, te=te, pt=pt: e.tensor_tensor(
                                            out=pt.t[:, 0:128], in0=te.t[:, 0:128], in1=Eo.t[:, h, :], op=ALU.mult),
                                            reads=[te.r, Eo.r], writes=[pt.r])
                                        kb.op("act", lambda e, soff=soff, sbank=sbank, pt=pt: e.activation(
                                            out=pt.t[:, 128:256], in_=sbank.t[:, soff + 128:soff + 256], func=AF.Exp,
                                            scale=SCALE_M, bias=c31.t[:, h:h + 1]), reads=[c31.r], writes=[pt.r, sreg])
                                    else:
                                        kb.op("act", lambda e, soff=soff, sbank=sbank, pt=pt: e.activation(
                                            out=pt.t[:, :], in_=sbank.t[:, soff:soff + 256], func=AF.Exp,
                                            scale=SCALE_M, bias=c31.t[:, h:h + 1]), reads=[c31.r], writes=[pt.r, sreg])
                                elif kt == 2 * qb:
                                    kb.op("pe", lambda e, ksl=ksl, soff=soff, sbank=sbank: e.matmul(
                                        sbank.t[:, soff:soff + 256], kT.t[:, ksl], qT.t[:, q0:q0 + 256],
                                        start=True, stop=True), reads=[kreg, qreg], writes=[sreg])
                                    kb.op("act", lambda e, soff=soff, sbank=sbank, te=te: e.activation(
                                        out=te.t[:, :], in_=sbank.t[:, soff:soff + 256], func=AF.Exp,
                                        scale=SCALE_M), writes=[te.r, sreg])
                                    kb.op("dve", lambda e, te=te, pt=pt: e.tensor_tensor(
                                        out=pt.t[:, 0:128], in0=te.t[:, 0:128], in1=Ed.t[:, h, :], op=ALU.mult),
                                        reads=[te.r, Ed.r], writes=[pt.r])
                                    kb.op("dve", lambda e, te=te, pt=pt: e.tensor_tensor(
                                        out=pt.t[:, 128:256], in0=te.t[:, 128:256], in1=Eo.t[:, h, :], op=ALU.mult),
                                        reads=[te.r, Eo.r], writes=[pt.r])
                                else:
                                    kb.op("pe", lambda e, ksl=ksl, soff=soff, sbank=sbank: e.matmul(
                                        sbank.t[:, soff + 128:soff + 256], kT.t[:, ksl], qT.t[:, q0 + 128:q0 + 256],
                                        start=True, stop=True), reads=[kreg, qreg], writes=[sreg])
                                    kb.op("act", lambda e, soff=soff, sbank=sbank, te=te: e.activation(
                                        out=te.t[:, 128:256], in_=sbank.t[:, soff + 128:soff + 256], func=AF.Exp,
                                        scale=SCALE_M), writes=[te.r, sreg])
                                    kb.op("dve", lambda e, te=te, pt=pt: e.tensor_tensor(
                                        out=pt.t[:, 128:256], in0=te.t[:, 128:256], in1=Ed.t[:, h, :], op=ALU.mult),
                                        reads=[te.r, Ed.r], writes=[pt.r])
                                res.append((kt, pt))
                            return res

                        def emit_pv(n, pts):
                            nonlocal o_slot
                            for qi in range(2):
                                obank = banks[4 + o_slot % 3]
                                oreg = obank.r
                                ooff = 0
                                o_slot += 1
                                use = [(kt, pt) for (kt, pt) in pts if kt <= 2 * qb + qi]
                                for i, (kt, pt) in enumerate(use):
                                    kb.op("pe", lambda e, kt=kt, pt=pt, i=i, qi=qi, ooff=ooff, obank=obank, use=use: e.matmul(
                                        obank.t[:, ooff:ooff + 129], pt.t[:, qi * 128:(qi + 1) * 128], V1.t[:, kt, :],
                                        start=(i == 0), stop=(i == len(use) - 1)),
                                        reads=[pt.r, V1.regs[kt // 4], V1.regs[4]], writes=[oreg])
                                if n == qb:
                                    kb.op("act", lambda e, qi=qi, ooff=ooff, obank=obank: e.copy(
                                        out=acc[qi].t[:, :], in_=obank.t[:, ooff:ooff + 129]),
                                        writes=[acc[qi].r, oreg])
                                elif qb >= 4:
                                    kb.op("dve", lambda e, qi=qi, ooff=ooff, obank=obank, n=n: e.scalar_tensor_tensor(
                                        out=acc[qi].t[:, :], in0=obank.t[:, ooff:ooff + 129], scalar=sel[qi].t[:, n:n + 1],
                                        in1=acc[qi].t[:, :], op0=ALU.mult, op1=ALU.add),
                                        reads=[sel[qi].r], writes=[acc[qi].r, oreg])
                                else:
                                    kb.op("dve", lambda e, qi=qi, ooff=ooff, obank=obank: e.tensor_tensor(
                                        out=acc[qi].t[:, :], in0=obank.t[:, ooff:ooff + 129], in1=acc[qi].t[:, :], op=ALU.add),
                                        writes=[acc[qi].r, oreg])

                        prev = None
                        for n in order:
                            cur = (n, emit_scores(n))
                            if prev is not None:
                                emit_pv(*prev)
                            prev = cur
                        emit_pv(*prev)
                        for qi in range(2):
                            qt = 2 * qb + qi
                            kb.op("dve", lambda e, qi=qi: e.reciprocal(out=rec[qi].t[:], in_=acc[qi].t[:, 128:129]),
                                  reads=[acc[qi].r], writes=[rec[qi].r])
                            kb.op("dve", lambda e, qi=qi: e.tensor_scalar_mul(
                                out=ya[qi].t[:], in0=acc[qi].t[:, 0:128], scalar1=rec[qi].t[:, 0:1]),
                                reads=[acc[qi].r, rec[qi].r], writes=[ya[qi].r])
                            tb = banks[7]
                            kb.op("pe", lambda e, qi=qi: e.matmul(
                                tb.t[:, qi * 128:(qi + 1) * 128], ya[qi].t[:], CB(C_ID), start=True, stop=True),
                                reads=[ya[qi].r, cstb.r], writes=[tb.r])
                            kb.op("act", lambda e, qi=qi, qt=qt: e.copy(
                                out=yT.t[:, qt * 128:(qt + 1) * 128], in_=tb.t[:, qi * 128:(qi + 1) * 128]),
                                writes=[yT.r, tb.r])
                    kb.dma("sp", yaT_d[h], yT.t[:], reads=[yT.r])
                    mod_group(8 + 2 * h, banks[7], 384)
                    mod_group(9 + 2 * h, banks[7], 384)
                kb.op("dve", lambda e: e.tensor_scalar_add(out=modT.t[:, 64:80], in0=modT.t[:, 64:80], scalar1=1.0),
                      reads=[modT.r], writes=[modT.r])
        es0.close()


        kb.barrier()
        bankctr = [0]

        def nb():
            b = banks[bankctr[0] % 8]
            bankctr[0] += 1
            return b

        if stop_after not in ("p0", "p1a"):
            with ExitStack() as es2:
                def sb2(name, shape, dt=F32, nreg=1):
                    return T(es2.enter_context(nc.sbuf_tensor(name, list(shape), dt)), nreg)
                print("sbuf remaining at GDN start", nc.sbuf_bytes_remaining)
                beta = sb2("beta", [128, NT, HV])
                lb = sb2("lb", [128, NT, HV])
                gg = sb2("gg", [128, NT, HV])
                nea = sb2("nea", [128, HV])
                egc = sb2("egc", [128, NT, HV])
                cdd = sb2("cdd", [128, NT, HV])
                ekd = sb2("ekd", [128, NT, HV])
                bkk = sb2("bkk", [128, NT, HV])
                ghl = sb2("ghl", [128, 2, NT, HV])
                lhl = sb2("lhl", [128, 2, NT, HV])
                cw = sb2("cw", [128, 32, 4])
                alog = sb2("alog", [128, HV])
                dtb = sb2("dtb", [128, HV])
                normw = sb2("normw", [128, 128])
                es2a = ExitStack()

                def sb2a(name, shape, dt=F32, nreg=1):
                    return T(es2a.enter_context(nc.sbuf_tensor(name, list(shape), dt)), nreg)
                wabt = sb2a("wabt", [128, KC, 32], BF16)
                kb.dma("pool", wabt.t[:].rearrange("p k c -> p (k c)"), wab_d, writes=[wabt.r])
                kb.dma("sp", cw.t[:].rearrange("p g t -> p (g t)"), conv_d, writes=[cw.r])
                kb.dma("sp", alog.t[:], alog_d, writes=[alog.r])
                kb.dma("sp", dtb.t[:], dtb_d, writes=[dtb.r])
                kb.dma("sp", normw.t[:], normw_d, writes=[normw.r])
                ab = sb2a("ab", [128, NT, 32])
                bk_ = nb()
                for t in range(NT):
                    for k in range(KC):
                        kb.op("pe", lambda e, t=t, k=k: e.matmul(
                            bk_.t[:, t * 32:(t + 1) * 32], hT.t[:, k, t * 128:(t + 1) * 128], wabt.t[:, k, :],
                            start=(k == 0), stop=(k == KC - 1)), reads=[wabt.r, hT_all[k]], writes=[bk_.r])
                kb.op("dve", lambda e: e.tensor_copy(out=ab.t[:], in_=bk_.t[:, :].rearrange("p (t c) -> p t c", c=32)),
                      writes=[ab.r, bk_.r])
                tmpa = sb2a("tmpa", [128, NT, HV])
                gcs = sb2a("gcs", [128, NT, 32])
                kb.op("act", lambda e: e.activation(out=beta.t[:], in_=ab.t[:, :, 0:16], func=AF.Sigmoid),
                      reads=[ab.r], writes=[beta.r])
                kb.op("act", lambda e: e.activation(out=lb.t[:], in_=beta.t[:], func=AF.Ln),
                      reads=[beta.r], writes=[lb.r])
                kb.op("dve", lambda e: e.tensor_tensor(
                    out=tmpa.t[:], in0=ab.t[:, :, 16:32], in1=dtb.t[:].unsqueeze(1).broadcast_to([128, NT, HV]),
                    op=ALU.add), reads=[ab.r, dtb.r], writes=[tmpa.r])
                kb.op("act", lambda e: e.activation(out=tmpa.t[:], in_=tmpa.t[:], func=AF.Exp),
                      reads=[tmpa.r], writes=[tmpa.r])
                kb.op("act", lambda e: e.activation(out=tmpa.t[:], in_=tmpa.t[:], func=AF.Ln, bias=1.0),
                      reads=[tmpa.r], writes=[tmpa.r])
                kb.op("act", lambda e: e.activation(out=nea.t[:], in_=alog.t[:], func=AF.Exp),
                      reads=[alog.r], writes=[nea.r])
                kb.op("dve", lambda e: e.scalar_tensor_tensor(
                    out=gg.t[:], in0=tmpa.t[:], scalar=-1.0, in1=nea.t[:].unsqueeze(1).broadcast_to([128, NT, HV]),
                    op0=ALU.mult, op1=ALU.mult), reads=[tmpa.r, nea.r], writes=[gg.r])
                bk_ = nb()
                for t in range(NT):
                    kb.op("pe", lambda e, t=t: e.matmul(bk_.t[:, t * 32:t * 32 + 16], CF(C_TRI), gg.t[:, t, :],
                                                        start=True, stop=True), reads=[cst.r, gg.r], writes=[bk_.r])
                    kb.op("pe", lambda e, t=t: e.matmul(bk_.t[:, t * 32 + 16:t * 32 + 32], CF(C_ONES), gg.t[:, t, :],
                                                        start=True, stop=True), reads=[cst.r, gg.r], writes=[bk_.r])
                kb.op("dve", lambda e: e.tensor_copy(out=gcs.t[:], in_=bk_.t[:, :].rearrange("p (t c) -> p t c", c=32)),
                      writes=[gcs.r, bk_.r])
                kb.op("act", lambda e: e.activation(out=egc.t[:], in_=gcs.t[:, :, 0:16], func=AF.Exp),
                      reads=[gcs.r], writes=[egc.r])
                kb.op("act", lambda e: e.activation(out=cdd.t[:], in_=gcs.t[:, :, 16:32], func=AF.Exp),
                      reads=[gcs.r], writes=[cdd.r])
                kb.op("dve", lambda e: e.tensor_tensor(out=ekd.t[:], in0=gcs.t[:, :, 16:32], in1=gcs.t[:, :, 0:16],
                                                       op=ALU.subtract), reads=[gcs.r], writes=[ekd.r])
                kb.op("act", lambda e: e.activation(out=ekd.t[:], in_=ekd.t[:], func=AF.Exp),
                      reads=[ekd.r], writes=[ekd.r])
                kb.op("dve", lambda e: e.tensor_tensor(out=bkk.t[:], in0=beta.t[:], in1=egc.t[:], op=ALU.mult),
                      reads=[beta.r, egc.r], writes=[bkk.r])

                tmpb = sb2a("tmpb", [128, NT, HV], BF16)
                for src, dst in ((gg, ghl), (lb, lhl)):
                    kb.op("dve", lambda e, src=src: e.tensor_copy(out=tmpb.t[:], in_=src.t[:]), reads=[src.r], writes=[tmpb.r])
                    kb.op("dve", lambda e, dst=dst: e.tensor_copy(out=dst.t[:, 0, :, :], in_=tmpb.t[:]), reads=[tmpb.r], writes=[dst.r])
                    kb.op("dve", lambda e, src=src, dst=dst: e.tensor_tensor(out=dst.t[:, 1, :, :], in0=src.t[:], in1=dst.t[:, 0, :, :],
                                                                             op=ALU.subtract), reads=[src.r], writes=[dst.r])
                kb.barrier()
                es2a.close()
                es2b = ExitStack()

                def sb2b(name, shape, dt=F32, nreg=1):
                    return T(es2b.enter_context(nc.sbuf_tensor(name, list(shape), dt)), nreg)
                wgb = [sb2b("wg%d" % i, [128, KC, 768], BF16) for i in range(2)]
                xpre = sb2b("xpre", [128, 4, 3 + S], BF16, nreg=4)
                kb.op("dve", lambda e: e.memset(xpre.t[:, :, 0:3], 0.0), writes=xpre.regs)
                dg = sb2b("dg", [128, 16, 128], BF16, nreg=16)
                qTg = sb2b("qTg", [128, S], BF16)
                kTg = sb2b("kTg", [128, S], BF16)
                ktok = sb2b("ktok", [128, NT, 128], BF16)
                vtok = sb2b("vtok", [128, NT, 256], BF16)
                zs = sb2b("zs", [128, NT, 256], BF16)
                cs = [sb2b("cs%d" % i, [128, 512]) for i in range(2)]
                sqs = [sb2b("sq%d" % i, [128, 512], BF16) for i in range(2)]
                rinv = sb2b("rinv", [128, 512])
                vTt = [sb2b("vTt%d" % i, [128, 512], BF16) for i in range(2)]
                zt = [sb2b("zt%d" % i, [128, 512]) for i in range(2)]

                kb.dma("pool", wgb[0].t[:].rearrange("p k c -> p (k c)"), wgdn_d[0], writes=[wgb[0].r])
                ev = [0]

                def evac(out, in_, reads, writes):
                    ev[0] += 1
                    if ev[0] % 2:
                        kb.op("act", lambda e: e.copy(out=out, in_=in_), reads=reads, writes=writes)
                    else:
                        kb.op("dve", lambda e: e.tensor_copy(out=out, in_=in_), reads=reads, writes=writes)

                for j in range(8):
                    w = wgb[j % 2]
                    if j + 1 < 8:
                        wn_ = wgb[(j + 1) % 2]
                        kb.dma("pool", wn_.t[:].rearrange("p k c -> p (k c)"), wgdn_d[j + 1], writes=[wn_.r])
                    grps = [j, 8 + j, 16 + 2 * j, 17 + 2 * j]
                    for gi in range(4):
                        for tap in range(4):
                            kb.op("dve", lambda e, gi=gi, tap=tap: e.tensor_scalar_mul(
                                out=dg.t[:, gi * 4 + tap, :], in0=CF(C_ID), scalar1=cw.t[:, grps[gi], tap:tap + 1]),
                                reads=[cst.r, cw.r], writes=[dg.regs[gi * 4 + tap]])
                    for gi in range(4):
                        for g in range(4):
                            pb = nb()
                            for k in range(KC):
                                kb.op("pe", lambda e, k=k, g=g, gi=gi, pb=pb: e.matmul(
                                    pb.t[:, :], w.t[:, k, gi * 128:(gi + 1) * 128], hT.t[:, k, g * 512:(g + 1) * 512],
                                    start=(k == 0), stop=(k == KC - 1)), reads=[w.r, hT_all[k]], writes=[pb.r])
                            evac(xpre.t[:, gi, 3 + g * 512:3 + (g + 1) * 512], pb.t[:, :], [], [xpre.regs[gi], pb.r])
                    def z_unit(t2, w=w):
                        pb = nb()
                        for a in range(2):
                            t = t2 * 2 + a
                            for k in range(KC):
                                kb.op("pe", lambda e, k=k, t=t, a=a, pb=pb: e.matmul(
                                    pb.t[:, a * 256:(a + 1) * 256], hT.t[:, k, t * 128:(t + 1) * 128], w.t[:, k, 512:768],
                                    start=(k == 0), stop=(k == KC - 1)), reads=[w.r, hT_all[k]], writes=[pb.r])
                        z_ = zt[t2 % 2]
                        kb.op("act", lambda e, pb=pb, z_=z_: e.activation(out=z_.t[:], in_=pb.t[:, :], func=AF.Silu),
                              writes=[z_.r, pb.r])
                        kb.op("dve", lambda e, z_=z_, t2=t2: e.tensor_tensor(
                            out=zs.t[:, t2 * 2:t2 * 2 + 2, :].rearrange("p a (h c) -> p (a h) c", c=128),
                            in0=z_.t[:].rearrange("p (a c) -> p a c", c=128),
                            in1=normw.t[:].unsqueeze(1).broadcast_to([128, 4, 128]), op=ALU.mult),
                            reads=[z_.r, normw.r], writes=[zs.r])
                    def cv_s1(gi, g):
                        pb = nb()
                        for tap in range(4):
                            kb.op("pe", lambda e, tap=tap: e.matmul(
                                pb.t[:, :], dg.t[:, gi * 4 + tap, :], xpre.t[:, gi, g * 512 + tap:g * 512 + tap + 512],
                                start=(tap == 0), stop=(tap == 3)), reads=[dg.regs[gi * 4 + tap], xpre.regs[gi]], writes=[pb.r])
                        if gi < 2:
                            c_ = cs[g % 2]
                            sq_ = sqs[g % 2]
                            kb.op("act", lambda e: e.activation(out=c_.t[:], in_=pb.t[:, :], func=AF.Silu), writes=[c_.r, pb.r])
                            kb.op("dve", lambda e: e.tensor_tensor(out=sq_.t[:], in0=c_.t[:], in1=c_.t[:], op=ALU.mult),
                                  reads=[c_.r], writes=[sq_.r])
                        else:
                            vt_ = vTt[g % 2]
                            kb.op("act", lambda e: e.activation(out=vt_.t[:], in_=pb.t[:, :], func=AF.Silu), writes=[vt_.r, pb.r])

                    def cv_s2(gi, g):
                        if gi < 2:
                            c_ = cs[g % 2]
                            sq_ = sqs[g % 2]
                            pb2 = nb()
                            kb.op("pe", lambda e: e.matmul(pb2.t[:, :], CB(C_ONES), sq_.t[:], start=True, stop=True),
                                  reads=[cstb.r, sq_.r], writes=[pb2.r])
                            kb.op("act", lambda e: e.activation(out=rinv.t[:], in_=pb2.t[:, :], func=AF.Ln, bias=epsr.t[:, 0:1]),
                                  reads=[epsr.r], writes=[rinv.r, pb2.r])
                            kb.op("act", lambda e: e.activation(out=rinv.t[:], in_=rinv.t[:], func=AF.Exp, scale=-0.5),
                                  reads=[rinv.r], writes=[rinv.r])
                            dstT = qTg if gi == 0 else kTg
                            scl = SCALE_M if gi == 0 else 1.0
                            kb.op("dve", lambda e: e.scalar_tensor_tensor(
                                out=dstT.t[:, g * 512:(g + 1) * 512], in0=c_.t[:], scalar=scl, in1=rinv.t[:],
                                op0=ALU.mult, op1=ALU.mult), reads=[c_.r, rinv.r], writes=[dstT.r])
                        else:
                            vt_ = vTt[g % 2]
                            pb3 = nb()
                            for tt in range(4):
                                kb.op("pe", lambda e, tt=tt: e.matmul(
                                    pb3.t[:, tt * 128:(tt + 1) * 128], vt_.t[:, tt * 128:(tt + 1) * 128], CB(C_ID),
                                    start=True, stop=True), reads=[vt_.r, cstb.r], writes=[pb3.r])
                            evac(vtok.t[:, g * 4:(g + 1) * 4, (gi - 2) * 128:(gi - 1) * 128],
                                 pb3.t[:, :].rearrange("p (a c) -> p a c", c=128), [], [vtok.r, pb3.r])

                    def cv_s3(gi, g):
                        if gi != 1:
                            return
                        pb3 = nb()
                        for tt in range(4):
                            t = g * 4 + tt
                            kb.op("pe", lambda e, tt=tt, t=t: e.matmul(
                                pb3.t[:, tt * 128:(tt + 1) * 128], kTg.t[:, t * 128:(t + 1) * 128], CB(C_ID),
                                start=True, stop=True), reads=[kTg.r, cstb.r], writes=[pb3.r])
                        evac(ktok.t[:, g * 4:(g + 1) * 4, :], pb3.t[:, :].rearrange("p (a c) -> p a c", c=128),
                             [], [ktok.r, pb3.r])
                    units = [(gi, g) for gi in range(4) for g in range(4)]
                    cv_s1(*units[0])
                    for ui, (gi, g) in enumerate(units):
                        if ui + 1 < len(units):
                            cv_s1(*units[ui + 1])
                        cv_s2(gi, g)
                        if ui >= 1:
                            cv_s3(*units[ui - 1])
                        if ui % 2 == 1:
                            z_unit(ui // 2)
                    cv_s3(*units[-1])
                    kb.dma("sp", gq_d[j], qTg.t[:], reads=[qTg.r])
                    kb.dma("sp", gk_d[j], kTg.t[:], reads=[kTg.r])
                    kb.dma("sp", gkt_d[j], ktok.t[:].rearrange("p a c -> p (a c)"), reads=[ktok.r])
                    kb.dma("sp", gvt_d[j], vtok.t[:].rearrange("p a c -> p (a c)"), reads=[vtok.r])
                    kb.dma("sp", gzs_d[j], zs.t[:].rearrange("p a c -> p (a c)"), reads=[zs.r])

                kb.barrier()
                es2b.close()
                class HS_:
                    pass
                INB = []
                for i in range(3):
                    b_ = HS_()
                    b_.q = sb2("inq%d" % i, [128, 256], BF16)
                    b_.k = sb2("ink%d" % i, [128, 256], BF16)
                    b_.kt = sb2("inkt%d" % i, [128, 2, 128], BF16)
                    b_.vt = sb2("invt%d" % i, [128, 2, 256], BF16)
                    b_.zs = sb2("inzs%d" % i, [128, 2, 256], BF16)
                    INB.append(b_)

                def load_inputs(gcp):
                    j_, cp_ = divmod(gcp, 8)
                    b_ = INB[gcp % 3]
                    kb.dma("sp", b_.q.t[:], gq_d[j_][:, cp_ * 256:(cp_ + 1) * 256], writes=[b_.q.r])
                    kb.dma("sp", b_.k.t[:], gk_d[j_][:, cp_ * 256:(cp_ + 1) * 256], writes=[b_.k.r])
                    kb.dma("sp", b_.kt.t[:].rearrange("p a c -> p (a c)"), gkt_d[j_][:, cp_ * 256:(cp_ + 1) * 256], writes=[b_.kt.r])
                    kb.dma("sp", b_.vt.t[:].rearrange("p a c -> p (a c)"), gvt_d[j_][:, cp_ * 512:(cp_ + 1) * 512], writes=[b_.vt.r])
                    kb.dma("sp", b_.zs.t[:].rearrange("p a c -> p (a c)"), gzs_d[j_][:, cp_ * 512:(cp_ + 1) * 512], writes=[b_.zs.r])
                KK = [sb2("kkm%d" % i, [128, 4, 128]) for i in range(2)]
                QK = [sb2("qkm%d" % i, [128, 128]) for i in range(2)]

                class HS:
                    pass
                SLOT = []
                for sl in range(2):
                    o_ = HS()
                    n = "s%d_" % sl
                    o_.LAp = sb2(n + "LAp", [128, 2, 2, 256], BF16)
                    o_.LDp = sb2(n + "LDp", [128, 2, 2, 256], BF16)
                    o_.LEp = sb2(n + "LEp", [128, 2, 2, 128], BF16)
                    o_.LFp = sb2(n + "LFp", [128, 2, 256], BF16)
                    o_.LGp = sb2(n + "LGp", [128, 2, 2, 128], BF16)
                    o_.LHp = sb2(n + "LHp", [128, 2, 256], BF16)
                    o_.TTp = [sb2(n + "TTp%d" % i, [128, 2, 128], BF16) for i in range(2)]
                    SLOT.append(o_)
                IT = []
                for sl in range(2):
                    row = []
                    for vh in range(2):
                        o_ = HS()
                        n = "i%d%d_" % (sl, vh)
                        o_.Bh = sb2(n + "Bh", [128, 2, 128], BF16); o_.Dl = sb2(n + "Dl", [128, 2, 128], BF16)
                        o_.E3 = sb2(n + "E3", [128, 384])
                        o_.V32 = sb2(n + "V32", [128, 512])
                        o_.Vall = sb2(n + "Vall", [128, 2, 512], BF16)
                        SP_ = SLOT[sl]
                        o_.LA = View(SP_.LAp.t[:, :, vh, :], SP_.LAp.r); o_.LB = sb2(n + "LB", [128, 2, 384], BF16)
                        o_.LC = sb2(n + "LC", [128, 2, 384], BF16); o_.LD = View(SP_.LDp.t[:, :, vh, :], SP_.LDp.r)
                        o_.LE = View(SP_.LEp.t[:, :, vh, :], SP_.LEp.r); o_.LF = View(SP_.LFp.t[:, vh, :], SP_.LFp.r)
                        o_.LG = View(SP_.LGp.t[:, :, vh, :], SP_.LGp.r); o_.LH = View(SP_.LHp.t[:, vh, :], SP_.LHp.r)
                        o_.kbg = sb2(n + "kbg", [128, 128], BF16)
                        for nm in ("attnT", "vbeta", "kd", "nwT"):
                            setattr(o_, nm, [sb2(n + nm + str(i), [128, 128], BF16) for i in range(2)])
                        o_.TT = [View(SP_.TTp[i].t[:, vh, :], SP_.TTp[i].r) for i in range(2)]
                        row.append(o_)
                    IT.append(row)
                HQ = []
                for vh in range(2):
                    o_ = HS()
                    n = "q%d_" % vh
                    o_.vnew = sb2(n + "vnew", [128, 128], BF16)
                    o_.S32 = sb2(n + "S32", [128, 128]); o_.Sbf = sb2(n + "Sbf", [128, 128], BF16)
                    o_.tmpo = [sb2(n + "tmpo%d" % i, [128, 128]) for i in range(2)]
                    o_.o = [sb2(n + "o%d" % i, [128, 128]) for i in range(2)]
                    o_.junk = o_.tmpo
                    o_.ssq = [sb2(n + "ssq%d" % i, [128, 1]) for i in range(2)]
                    o_.r1 = [sb2(n + "r1%d" % i, [128, 1]) for i in range(2)]
                    o_.r2 = [sb2(n + "r2%d" % i, [128, 1]) for i in range(2)]
                    o_.yb = [sb2(n + "yb%d" % i, [128, 128], BF16) for i in range(2)]
                    o_.ybT = sb2(n + "ybT", [128, 512], BF16)
                    HQ.append(o_)

                if True:
                    IDF = CF(C_ID)
                    IDB = CB(C_ID)
                    ASTB = CB(C_AST)
                    def make_stages(j, c, b_):
                        csl = slice(c * 128, (c + 1) * 128)
                        cl = c % 2
                        lsl = slice(cl * 128, (cl + 1) * 128)
                        pp = (c // 2) % 2
                        kkm = KK[c % 2]
                        qkm = QK[c % 2]

                        def st_shared():
                          pk = nb()
                          if True:
                            kb.op("pe", lambda e, pk=pk: e.matmul(pk.t[:, 0:128], b_.k.t[:, lsl], b_.k.t[:, lsl], start=True, stop=True),
                                  reads=[b_.k.r], writes=[pk.r])
                            kb.op("pe", lambda e, pk=pk: e.matmul(pk.t[:, 128:256], b_.k.t[:, lsl], b_.q.t[:, lsl], start=True, stop=True),
                                  reads=[b_.k.r, b_.q.r], writes=[pk.r])
                            kb.op("dve", lambda e, pk=pk: e.tensor_tensor(
                                out=kkm.t[:], in0=pk.t[:, 0:128].unsqueeze(1).broadcast_to([128, 4, 128]),
                                in1=cst.t[:, C_SUBD:C_SUBD + 4, :], op=ALU.mult), reads=[cst.r], writes=[kkm.r, pk.r])
                            kb.op("dve", lambda e, pk=pk: e.tensor_tensor(
                                out=qkm.t[:], in0=pk.t[:, 128:256], in1=CF(C_UI), op=ALU.mult),
                                reads=[cst.r], writes=[qkm.r, pk.r])

                        pdb = {}

                        def st_p0(vh):
                            H = IT[c % 2][vh]; Q = HQ[vh]
                            hv = 2 * j + vh
                            kb.op("dve", lambda e: e.tensor_scalar_mul(out=H.Bh.t[:, 0, :], in0=CF(C_TRI), scalar1=ghl.t[:, 0, c, hv:hv + 1]),
                                  reads=[cst.r, ghl.r], writes=[H.Bh.r])
                            kb.op("dve", lambda e: e.tensor_scalar_mul(out=H.Bh.t[:, 1, :], in0=CF(C_TRI), scalar1=ghl.t[:, 1, c, hv:hv + 1]),
                                  reads=[cst.r, ghl.r], writes=[H.Bh.r])
                            kb.op("act", lambda e: e.activation(out=H.Dl.t[:, 0, :], in_=CF(C_ID), func=AF.Identity, scale=lhl.t[:, 0, c, hv:hv + 1]),
                                  reads=[cst.r, lhl.r], writes=[H.Dl.r])
                            kb.op("act", lambda e: e.activation(out=H.Dl.t[:, 1, :], in_=CF(C_ID), func=AF.Identity, scale=lhl.t[:, 1, c, hv:hv + 1]),
                                  reads=[cst.r, lhl.r], writes=[H.Dl.r])
                            kb.op("act", lambda e: e.activation(out=H.kbg.t[:], in_=b_.kt.t[:, cl, :], func=AF.Identity, scale=bkk.t[:, c, hv:hv + 1]),
                                  reads=[b_.kt.r, bkk.r], writes=[H.kbg.r])
                            kb.op("act", lambda e: e.activation(out=H.vbeta[pp].t[:], in_=b_.vt.t[:, cl, vh * 128:(vh + 1) * 128],
                                                                func=AF.Identity, scale=beta.t[:, c, hv:hv + 1]),
                                  reads=[b_.vt.r, beta.r], writes=[H.vbeta[pp].r])
                            kb.op("act", lambda e: e.activation(out=H.kd[pp].t[:], in_=b_.kt.t[:, cl, :], func=AF.Identity, scale=ekd.t[:, c, hv:hv + 1]),
                                  reads=[b_.kt.r, ekd.r], writes=[H.kd[pp].r])

                        def st_p1(vh):
                            H = IT[c % 2][vh]; Q = HQ[vh]
                            pd = nb()
                            pdb[vh] = pd
                            Bh_, Bl_ = H.Bh.t[:, 0, :], H.Bh.t[:, 1, :]
                            Dh_, Dl_ = H.Dl.t[:, 0, :], H.Dl.t[:, 1, :]
                            rB = [cstb.r, H.Bh.r]
                            rD = [cstb.r, H.Dl.r]
                            mm(pd, 0, ASTB, Bh_, rB, True, False)
                            mm(pd, 0, ASTB, Bl_, rB, False, True)
                            mm(pd, 128, ASTB, Bh_, rB, True, False)
                            mm(pd, 128, ASTB, Bl_, rB, False, False)
                            mm(pd, 128, ASTB, Dh_, rD, False, False)
                            mm(pd, 128, ASTB, Dl_, rD, False, True)
                            mm(pd, 256, Bh_, ASTB, rB, True, False)
                            mm(pd, 256, Bl_, ASTB, rB, False, False)
                            mm(pd, 256, Dh_, ASTB, rD, False, False)
                            mm(pd, 256, Dl_, ASTB, rD, False, True)

                        def st_p2(vh):
                            H = IT[c % 2][vh]; Q = HQ[vh]
                            pd = pdb[vh]
                            kb.op("act", lambda e: e.activation(out=H.E3.t[:], in_=pd.t[:, 0:384], func=AF.Exp),
                                  writes=[H.E3.r, pd.r])

                        def st_p3(vh):
                            H = IT[c % 2][vh]; Q = HQ[vh]
                            kb.op("dve", lambda e: e.tensor_tensor(out=H.attnT[pp].t[:], in0=H.E3.t[:, 0:128], in1=qkm.t[:], op=ALU.mult),
                                  reads=[H.E3.r, qkm.r], writes=[H.attnT[pp].r])
                            V32 = H.V32.t[:].rearrange("p (a c) -> p a c", c=128)
                            kb.op("dve", lambda e: e.scalar_tensor_tensor(
                                out=V32[:, 0, :], in0=H.E3.t[:, 128:256], scalar=-1.0, in1=kkm.t[:, 0, :], op0=ALU.mult, op1=ALU.mult),
                                reads=[H.E3.r, kkm.r], writes=[H.V32.r])
                            kb.op("dve", lambda e: e.scalar_tensor_tensor(
                                out=V32[:, 1:4, :], in0=H.E3.t[:, 256:384].unsqueeze(1).broadcast_to([128, 3, 128]), scalar=-1.0,
                                in1=kkm.t[:, 1:4, :], op0=ALU.mult, op1=ALU.mult),
                                reads=[H.E3.r, kkm.r], writes=[H.V32.r])

                        def st_p4(vh):
                            H = IT[c % 2][vh]; Q = HQ[vh]
                            kb.op("act", lambda e: e.copy(out=H.Vall.t[:, 0, :], in_=H.V32.t[:]),
                                  reads=[H.V32.r], writes=[H.Vall.r])
                            kb.op("dve", lambda e: e.tensor_tensor(out=H.Vall.t[:, 1, :], in0=H.V32.t[:], in1=H.Vall.t[:, 0, :], op=ALU.subtract),
                                  reads=[H.V32.r], writes=[H.Vall.r])

                        def mm(pb, lo, lhsT, rhs, rds, start=True, stop=True):
                            kb.op("pe", lambda e: e.matmul(pb.t[:, lo:lo + 128], lhsT, rhs, start=start, stop=stop),
                                  reads=rds, writes=[pb.r])

                        def mmf(pb, lo, lhsT, rhs, rds, start=True, stop=True):
                            mm(pb, lo, lhsT, rhs, rds, start, stop)

                        def mm3(pb, lo, A, B, rds, first=True, last=True):
                            (Ah, Al), (Bh, Bl) = A, B
                            mm(pb, lo, Ah, Bh, rds, first, False)
                            mm(pb, lo, Ah, Bl, rds, False, False)
                            mm(pb, lo, Al, Bh, rds, False, last)

                        def mmI(pb, lo, B, rds, first, last):
                            Bh, Bl = B
                            mm(pb, lo, IDB, Bh, rds + [cstb.r], first, False)
                            mm(pb, lo, IDB, Bl, rds + [cstb.r], False, last)

                        def mmT(pb, lo, A, rds, first=True, last=True):
                            Ah, Al = A
                            mm(pb, lo, Ah, IDB, rds + [cstb.r], first, False)
                            mm(pb, lo, Al, IDB, rds + [cstb.r], False, last)

                        def P(tile, lo, w=128):
                            return (tile.t[:, 0, lo:lo + w], tile.t[:, 1, lo:lo + w])

                        def VV(H, a):
                            return (H.Vall.t[:, 0, a * 128:(a + 1) * 128], H.Vall.t[:, 1, a * 128:(a + 1) * 128])

                        def split_psum(dst, pb, W):
                            kb.op("act", lambda e: e.copy(out=dst.t[:, 0, :], in_=pb.t[:, 0:W]), writes=[dst.r, pb.r])
                            kb.op("dve", lambda e: e.tensor_tensor(out=dst.t[:, 1, :], in0=pb.t[:, 0:W], in1=dst.t[:, 0, :],
                                                                   op=ALU.subtract), writes=[dst.r, pb.r])

                        pbs = {}

                        def pair_bank(name, vh):
                            if vh == 0:
                                pbs[name] = nb()
                            return pbs[name], vh * 256

                        def pview(pb, W):
                            return pb.t[:, :].rearrange("p (i w) -> p i w", w=256)[:, :, 0:W]

                        def split_pair(dst, pb, W):
                            kb.op("act", lambda e: e.copy(out=dst.t[:, 0, :, 0:W], in_=pview(pb, W)), writes=[dst.r, pb.r])
                            kb.op("dve", lambda e: e.tensor_tensor(out=dst.t[:, 1, :, 0:W], in0=pview(pb, W), in1=dst.t[:, 0, :, 0:W],
                                                                   op=ALU.subtract), writes=[dst.r, pb.r])

                        def st_A(vh):
                            H = IT[c % 2][vh]; Q = HQ[vh]
                            Vd, Vtd = VV(H, 0), VV(H, 1)
                            pb, off = pair_bank("A", vh)
                            mm3(pb, off + 0, Vtd, Vd, [H.Vall.r])
                            mm3(pb, off + 128, Vd, Vtd, [H.Vall.r])
                            if vh == 1:
                                split_pair(SLOT[c % 2].LAp, pb, 256)

                        def st_B(vh):
                            H = IT[c % 2][vh]; Q = HQ[vh]
                            V2, Vt2 = P(H.LA, 0), P(H.LA, 128)
                            Vd = VV(H, 0)
                            pb = nb()
                            mm3(pb, 0, Vt2, V2, [H.LA.r])
                            mm3(pb, 128, V2, Vt2, [H.LA.r])
                            mm(pb, 256, IDB, IDB, [cstb.r], True, False)
                            mmI(pb, 256, Vd, [H.Vall.r], False, False)
                            mmT(pb, 256, Vt2, [H.LA.r], False, False)
                            mm3(pb, 256, Vt2, Vd, [H.LA.r, H.Vall.r], False, True)
                            split_psum(H.LB, pb, 384)

                        def st_C(vh):
                            H = IT[c % 2][vh]; Q = HQ[vh]
                            V4, Vt4, Y1 = P(H.LB, 0), P(H.LB, 128), P(H.LB, 256)
                            pb = nb()
                            mm3(pb, 0, Vt4, V4, [H.LB.r])
                            mm3(pb, 128, V4, Vt4, [H.LB.r])
                            mm3(pb, 256, Vt4, Y1, [H.LB.r], True, False)
                            mmI(pb, 256, Y1, [H.LB.r], False, True)
                            split_psum(H.LC, pb, 384)

                        def st_D(vh):
                            H = IT[c % 2][vh]; Q = HQ[vh]
                            V8, Vt8, Y2 = P(H.LC, 0), P(H.LC, 128), P(H.LC, 256)
                            pb, off = pair_bank("D", vh)
                            mm3(pb, off + 0, V8, Vt8, [H.LC.r])
                            mm3(pb, off + 128, Vt8, Y2, [H.LC.r], True, False)
                            mmI(pb, off + 128, Y2, [H.LC.r], False, True)
                            if vh == 1:
                                split_pair(SLOT[c % 2].LDp, pb, 256)

                        def st_E(vh):
                            H = IT[c % 2][vh]; Q = HQ[vh]
                            Vt16, Y3 = P(H.LD, 0), P(H.LD, 128)
                            pb, off = pair_bank("E", vh)
                            mm3(pb, off, Vt16, Y3, [H.LD.r], True, False)
                            mmI(pb, off, Y3, [H.LD.r], False, True)
                            if vh == 1:
                                split_pair(SLOT[c % 2].LEp, pb, 128)

                        def st_F(vh):
                            H = IT[c % 2][vh]; Q = HQ[vh]
                            T0t = P(H.LE, 0)
                            pb, off = pair_bank("F", vh)
                            mm(pb, off + 0, T0t[0], IDB, [H.LE.r, cstb.r])
                            mm(pb, off + 128, VV(H, 2)[0], T0t[0], [H.Vall.r, H.LE.r])
                            if vh == 1:
                                LFp = SLOT[c % 2].LFp
                                evac(LFp.t[:, :, :], pview(pb, 256), [], [LFp.r, pb.r])

                        def st_G(vh):
                            H = IT[c % 2][vh]; Q = HQ[vh]
                            pb, off = pair_bank("G", vh)
                            mm(pb, off, H.LF.t[:, 0:128], H.LF.t[:, 128:256], [H.LF.r], True, False)
                            mmI(pb, off, P(H.LE, 0), [H.LE.r], False, True)
                            if vh == 1:
                                split_pair(SLOT[c % 2].LGp, pb, 128)

                        def st_H(vh):
                            H = IT[c % 2][vh]; Q = HQ[vh]
                            T1t = P(H.LG, 0)
                            pb, off = pair_bank("H", vh)
                            mm(pb, off + 0, T1t[0], IDB, [H.LG.r, cstb.r])
                            mm(pb, off + 128, VV(H, 3)[0], T1t[0], [H.Vall.r, H.LG.r])
                            if vh == 1:
                                LHp = SLOT[c % 2].LHp
                                evac(LHp.t[:, :, :], pview(pb, 256), [], [LHp.r, pb.r])

                        def st_I(vh):
                            H = IT[c % 2][vh]; Q = HQ[vh]
                            pb, off = pair_bank("I", vh)
                            mm(pb, off, H.LH.t[:, 0:128], H.LH.t[:, 128:256], [H.LH.r], True, False)
                            mmI(pb, off, P(H.LG, 0), [H.LG.r], False, True)
                            if vh == 1:
                                TTp = SLOT[c % 2].TTp[pp]
                                evac(TTp.t[:, :, :], pview(pb, 128), [], [TTp.r, pb.r])

                        def st_W(vh):
                            H = IT[c % 2][vh]; Q = HQ[vh]
                            pb = nb()
                            mmf(pb, 0, H.kbg.t[:], H.TT[pp].t[:], [H.kbg.r, H.TT[pp].r])
                            kb.op("act", lambda e: e.activation(out=H.nwT[pp].t[:], in_=pb.t[:, 0:128], func=AF.Copy, scale=-1.0),
                                  writes=[H.nwT[pp].r, pb.r])

                        sqb = {}

                        def st_V1(vh):
                            H = IT[c % 2][vh]; Q = HQ[vh]
                            pb = nb()
                            sqb[("V", vh)] = pb
                            mmf(pb, 0, H.TT[pp].t[:], H.vbeta[pp].t[:], [H.TT[pp].r, H.vbeta[pp].r], True, c == 0)
                            if c > 0:
                                mmf(pb, 0, H.nwT[pp].t[:], Q.Sbf.t[:], [H.nwT[pp].r, Q.Sbf.r], False, True)

                        def st_V2(vh):
                            H = IT[c % 2][vh]; Q = HQ[vh]
                            pb = sqb[("V", vh)]
                            evac(Q.vnew.t[:], pb.t[:, 0:128], [], [Q.vnew.r, pb.r])

                        def st_OS1(vh):
                            H = IT[c % 2][vh]; Q = HQ[vh]
                            pb = nb()
                            sqb[("O", vh)] = pb
                            mmf(pb, 128, H.attnT[pp].t[:], Q.vnew.t[:], [H.attnT[pp].r, Q.vnew.r])
                            if c > 0:
                                mmf(pb, 0, b_.q.t[:, lsl], Q.Sbf.t[:], [b_.q.r, Q.Sbf.r])
                            if c < NT - 1:
                                ps_ = nb()
                                sqb[("S", vh)] = ps_
                                mmf(ps_, 0, H.kd[pp].t[:], Q.vnew.t[:], [H.kd[pp].r, Q.vnew.r])

                        def st_OS2(vh):
                            H = IT[c % 2][vh]; Q = HQ[vh]
                            hv = 2 * j + vh
                            pb = sqb[("O", vh)]
                            if c > 0:
                                kb.op("act", lambda e: e.activation(out=Q.tmpo[c % 2].t[:], in_=pb.t[:, 0:128], func=AF.Identity,
                                                                    scale=egc.t[:, c, hv:hv + 1]),
                                      reads=[egc.r], writes=[Q.tmpo[c % 2].r, pb.r])
                            if c < NT - 1:
                                ps_ = sqb[("S", vh)]
                                if c > 0:
                                    kb.op("dve", lambda e: e.scalar_tensor_tensor(
                                        out=Q.S32.t[:], in0=Q.S32.t[:], scalar=cdd.t[:, c, hv:hv + 1], in1=ps_.t[:, 0:128],
                                        op0=ALU.mult, op1=ALU.add), reads=[cdd.r], writes=[Q.S32.r, ps_.r])
                                else:
                                    kb.op("dve", lambda e: e.tensor_copy(out=Q.S32.t[:], in_=ps_.t[:, 0:128]), writes=[Q.S32.r, ps_.r])

                        def st_OS3(vh):
                            H = IT[c % 2][vh]; Q = HQ[vh]
                            pb = sqb[("O", vh)]
                            if c > 0:
                                kb.op("dve", lambda e: e.tensor_tensor(out=Q.o[c % 2].t[:], in0=pb.t[:, 128:256], in1=Q.tmpo[c % 2].t[:], op=ALU.add),
                                      reads=[Q.tmpo[c % 2].r], writes=[Q.o[c % 2].r, pb.r])
                            else:
                                kb.op("dve", lambda e: e.tensor_copy(out=Q.o[c % 2].t[:], in_=pb.t[:, 128:256]), writes=[Q.o[c % 2].r, pb.r])
                            if c < NT - 1:
                                kb.op("act", lambda e: e.copy(out=Q.Sbf.t[:], in_=Q.S32.t[:]), reads=[Q.S32.r], writes=[Q.Sbf.r])

                        def st_Y1(vh):
                            H = IT[c % 2][vh]; Q = HQ[vh]
                            kb.op("dve", lambda e: e.memset(Q.ssq[c % 2].t[:], 0.0), writes=[Q.ssq[c % 2].r])
                            kb.op("act", lambda e: e.activation(out=Q.junk[c % 2].t[:], in_=Q.o[c % 2].t[:], func=AF.Square, accum_out=Q.ssq[c % 2].t[:, 0:1]),
                                  reads=[Q.o[c % 2].r], writes=[Q.junk[c % 2].r, Q.ssq[c % 2].r])

                        def st_Y2(vh):
                            H = IT[c % 2][vh]; Q = HQ[vh]
                            kb.op("dve", lambda e: e.tensor_scalar(out=Q.r1[c % 2].t[:], in0=Q.ssq[c % 2].t[:], scalar1=1.0 / 128.0, scalar2=RMS_EPS,
                                                                   op0=ALU.mult, op1=ALU.add), reads=[Q.ssq[c % 2].r], writes=[Q.r1[c % 2].r])

                        def st_Y3(vh):
                            H = IT[c % 2][vh]; Q = HQ[vh]
                            kb.op("act", lambda e: e.activation(out=Q.r2[c % 2].t[:], in_=Q.r1[c % 2].t[:], func=AF.Ln),
                                  reads=[Q.r1[c % 2].r], writes=[Q.r2[c % 2].r])
                            kb.op("act", lambda e: e.activation(out=Q.r2[c % 2].t[:], in_=Q.r2[c % 2].t[:], func=AF.Exp, scale=-0.5),
                                  reads=[Q.r2[c % 2].r], writes=[Q.r2[c % 2].r])

                        def st_Y4(vh):
                            H = IT[c % 2][vh]; Q = HQ[vh]
                            kb.op("dve", lambda e: e.scalar_tensor_tensor(
                                out=Q.yb[c % 2].t[:], in0=Q.o[c % 2].t[:], scalar=Q.r2[c % 2].t[:, 0:1], in1=b_.zs.t[:, cl, vh * 128:(vh + 1) * 128],
                                op0=ALU.mult, op1=ALU.mult), reads=[Q.o[c % 2].r, Q.r2[c % 2].r, b_.zs.r], writes=[Q.yb[c % 2].r])

                        def st_Y5(vh):
                            H = IT[c % 2][vh]; Q = HQ[vh]
                            pb = nb()
                            sqb[("Y", vh)] = pb
                            kb.op("pe", lambda e: e.matmul(pb.t[:, 0:128], Q.yb[c % 2].t[:], CB(C_ID), start=True, stop=True),
                                  reads=[Q.yb[c % 2].r, cstb.r], writes=[pb.r])

                        def st_Y6(vh):
                            H = IT[c % 2][vh]; Q = HQ[vh]
                            hv = 2 * j + vh
                            pb = sqb[("Y", vh)]
                            evac(Q.ybT.t[:, (c % 4) * 128:(c % 4 + 1) * 128], pb.t[:, 0:128], [], [Q.ybT.r, pb.r])
                            if c % 4 == 3:
                                kb.dma("sp", ybT_d[hv][:, (c - 3) * 128:(c + 1) * 128], Q.ybT.t[:], reads=[Q.ybT.r])

                        return dict(shared=st_shared, p0=st_p0, p1=st_p1, p2=st_p2, p3=st_p3, p4=st_p4, A=st_A, B=st_B, C=st_C, D=st_D, E=st_E, F=st_F, G=st_G,
                                    H=st_H, I=st_I, W=st_W,
                                    V=lambda vh: (st_V1(vh), st_V2(vh)),
                                    OS=lambda vh: (st_OS1(vh), st_OS2(vh), st_OS3(vh)),
                                    Y1=st_Y1, Y2=st_Y2, Y3=st_Y3, Y4=st_Y4,
                                    Y56=lambda vh: (st_Y5(vh), st_Y6(vh)))

                    chain_order = ["p0", "shared", "p1", "p2", "p3", "p4", "A", "B", "C", "D", "E", "F", "G", "H", "I", "W"]

                    def run_stage(stg, name):
                        if name == "shared":
                            stg[name]()
                        else:
                            for vh in range(2):
                                stg[name](vh)
                    prev_pair = None
                    NG = 8 * (NT // 2)
                    load_inputs(0)
                    for gcp in range(NG + 1):
                        if gcp + 1 < NG:
                            load_inputs(gcp + 1)
                        if gcp < NG:
                            j_, cp_ = divmod(gcp, 8)
                            cur = [make_stages(j_, 2 * cp_, INB[gcp % 3]), make_stages(j_, 2 * cp_ + 1, INB[gcp % 3])]
                        else:
                            cur = None
                        steps = []
                        if prev_pair is not None:
                            A_, B_ = prev_pair
                            steps = [[(A_, "V")], [(A_, "OS")], [(A_, "Y1"), (B_, "V")], [(A_, "Y2"), (B_, "OS")],
                                     [(A_, "Y3"), (B_, "Y1")], [(A_, "Y4"), (B_, "Y2")], [(A_, "Y56"), (B_, "Y3")],
                                     [(B_, "Y4")], [(B_, "Y56")]]
                            steps = [[], []] + steps
                        for ci, name in enumerate(chain_order):
                            if cur is not None:
                                for stg in cur:
                                    run_stage(stg, name)
                            if ci < len(steps):
                                for stg, nm in steps[ci]:
                                    run_stage(stg, nm)
                        for extra in steps[len(chain_order):]:
                            for stg, nm in extra:
                                run_stage(stg, nm)
                        prev_pair = cur


        if stop_after not in ("p0", "p1a", "p1b"):
            kb.barrier()
            with ExitStack() as es3:
                def sb3(name, shape, dt=F32, nreg=1):
                    return T(es3.enter_context(nc.sbuf_tensor(name, list(shape), dt)), nreg)
                yaR = sb3("yaR", [128, 8, S], BF16)
                ybR = sb3("ybR", [128, 16, S], BF16)
                for h in range(8):
                    kb.dma("sp", yaR.t[:, h, :], yaT_d[h], writes=[yaR.r])
                for h in range(16):
                    kb.dma("sp", ybR.t[:, h, :], ybT_d[h], writes=[ybR.r])
                wg_ = [sb3("w2g%d" % i, [128, KC, 256], BF16) for i in range(2)]
                wa_ = [sb3("w2a%d" % i, [128, 8, 128], BF16) for i in range(2)]
                wb_ = [sb3("w2b%d" % i, [128, 16, 128], BF16) for i in range(2)]
                sga = sb3("sga", [128, 512])
                sgb = sb3("sgb", [128, 512])
                t1 = sb3("t1", [128, 512])
                mst = [sb3("mst%d" % i, [128, 512], BF16) for i in range(2)]

                def load2(cc):
                    i = cc % 2
                    kb.dma("pool", wg_[i].t[:].rearrange("p k c -> p (k c)"), wgate_d[cc], writes=[wg_[i].r])
                    kb.dma("pool", wa_[i].t[:].rearrange("p k c -> p (k c)"), wpm_d[cc], writes=[wa_[i].r])
                    kb.dma("pool", wb_[i].t[:].rearrange("p k c -> p (k c)"), wpg_d[cc], writes=[wb_[i].r])
                load2(0)
                mi = 0
                for cc in range(16):
                    if cc + 1 < 16:
                        load2(cc + 1)
                    i = cc % 2
                    for g in range(4):
                        gs = slice(g * 512, (g + 1) * 512)
                        pA, pB, pC, pD = nb(), nb(), nb(), nb()
                        for k in range(KC):
                            kb.op("pe", lambda e, k=k: e.matmul(pA.t[:, :], wg_[i].t[:, k, 0:128], hT.t[:, k, gs],
                                                                start=(k == 0), stop=(k == KC - 1)),
                                  reads=[wg_[i].r, hT_all[k]], writes=[pA.r])
                        kb.op("act", lambda e: e.activation(out=sga.t[:], in_=pA.t[:, :], func=AF.Sigmoid),
                              writes=[sga.r, pA.r])
                        for k in range(KC):
                            kb.op("pe", lambda e, k=k: e.matmul(pB.t[:, :], wg_[i].t[:, k, 128:256], hT.t[:, k, gs],
                                                                start=(k == 0), stop=(k == KC - 1)),
                                  reads=[wg_[i].r, hT_all[k]], writes=[pB.r])
                        kb.op("act", lambda e: e.activation(out=sgb.t[:], in_=pB.t[:, :], func=AF.Sigmoid),
                              writes=[sgb.r, pB.r])
                        for k in range(8):
                            kb.op("pe", lambda e, k=k: e.matmul(pC.t[:, :], wa_[i].t[:, k, :], yaR.t[:, k, gs],
                                                                start=(k == 0), stop=(k == 7)),
                                  reads=[wa_[i].r, yaR.r], writes=[pC.r])
                        kb.op("dve", lambda e: e.tensor_tensor(out=sga.t[:], in0=pC.t[:, :], in1=sga.t[:], op=ALU.mult),
                              writes=[sga.r, pC.r])
                        for k in range(16):
                            kb.op("pe", lambda e, k=k: e.matmul(pD.t[:, :], wb_[i].t[:, k, :], ybR.t[:, k, gs],
                                                                start=(k == 0), stop=(k == 15)),
                                  reads=[wb_[i].r, ybR.r], writes=[pD.r])
                        kb.op("dve", lambda e: e.tensor_tensor(out=t1.t[:], in0=pD.t[:, :], in1=sgb.t[:], op=ALU.mult),
                              reads=[sgb.r], writes=[t1.r, pD.r])
                        m_ = mst[mi % 2]
                        mi += 1
                        kb.op("dve", lambda e, m_=m_: e.tensor_tensor(out=m_.t[:], in0=t1.t[:], in1=sga.t[:], op=ALU.add),
                              reads=[t1.r, sga.r], writes=[m_.r])
                        kb.dma("sp", mgT_d[cc][:, gs], m_.t[:], reads=[m_.r])

        def layer_norm_tile(pre, junk, st, lng, lnb):
            kb.op("dve", lambda e: e.memset(st.t[:, 0:2], 0.0), writes=[st.r])
            kb.op("act", lambda e: e.activation(out=junk.t[:], in_=pre.t[:], func=AF.Copy, accum_out=st.t[:, 0:1]),
                  reads=[pre.r], writes=[junk.r, st.r])
            kb.op("act", lambda e: e.activation(out=junk.t[:], in_=pre.t[:], func=AF.Square, accum_out=st.t[:, 1:2]),
                  reads=[pre.r], writes=[junk.r, st.r])
            kb.op("dve", lambda e: e.tensor_scalar_mul(out=st.t[:, 2:4], in0=st.t[:, 0:2], scalar1=1.0 / D),
                  reads=[st.r], writes=[st.r])
            kb.op("dve", lambda e: e.tensor_tensor(out=st.t[:, 4:5], in0=st.t[:, 2:3], in1=st.t[:, 2:3], op=ALU.mult),
                  reads=[st.r], writes=[st.r])
            kb.op("dve", lambda e: e.tensor_tensor(out=st.t[:, 5:6], in0=st.t[:, 3:4], in1=st.t[:, 4:5], op=ALU.subtract),
                  reads=[st.r], writes=[st.r])
            kb.op("act", lambda e: e.activation(out=st.t[:, 6:7], in_=st.t[:, 5:6], func=AF.Ln, bias=epsr.t[:, 1:2]),
                  reads=[st.r, epsr.r], writes=[st.r])
            kb.op("act", lambda e: e.activation(out=st.t[:, 6:7], in_=st.t[:, 6:7], func=AF.Exp, scale=-0.5),
                  reads=[st.r], writes=[st.r])
            kb.op("dve", lambda e: e.scalar_tensor_tensor(out=st.t[:, 7:8], in0=st.t[:, 2:3], scalar=-1.0, in1=st.t[:, 6:7],
                                                          op0=ALU.mult, op1=ALU.mult), reads=[st.r], writes=[st.r])
            kb.op("act", lambda e: e.activation(out=pre.t[:], in_=pre.t[:], func=AF.Identity,
                                                scale=st.t[:, 6:7], bias=st.t[:, 7:8]),
                  reads=[st.r], writes=[pre.r])
            kb.op("dve", lambda e: e.tensor_tensor(out=pre.t[:], in0=pre.t[:], in1=lng.t[:], op=ALU.mult),
                  reads=[lng.r], writes=[pre.r])
            kb.op("dve", lambda e: e.tensor_tensor(out=pre.t[:], in0=pre.t[:], in1=lnb.t[:], op=ALU.add),
                  reads=[lnb.r], writes=[pre.r])

        if stop_after not in ("p0", "p1a", "p1b", "p2"):
            kb.barrier()
            with ExitStack() as es4:
                def sb4(name, shape, dt=F32, nreg=1):
                    return T(es4.enter_context(nc.sbuf_tensor(name, list(shape), dt)), nreg)
                wout = sb4("wout", [128, KC, D], BF16)
                for g in range(4):
                    kb.dma("pool", wout.t[:, :, g * 512:(g + 1) * 512], wout_d[g].rearrange("p (k c) -> p k c", c=512),
                           writes=[wout.r])
                mgR = sb4("mgR", [128, KC, 512], BF16)
                g1b = sb4("g1b", [128, D])
                lng = sb4("lng", [128, D])
                lnb = sb4("lnb", [128, D])
                kb.dma("sp", lng.t[:], lnrep_d[0], writes=[lng.r])
                kb.dma("sp", lnb.t[:], lnrep_d[1], writes=[lnb.r])
                xt = [sb4("xt0", [128, D])]
                pres = [sb4("pre%d" % i, [128, D]) for i in range(2)]
                junk = sb4("junk3", [128, D], BF16)
                st = sb4("st3", [128, 8])
                dgt = sb4("dgt", [128, 128])
                for k4 in range(4):
                    pb = nb()
                    for kk in range(4):
                        k = k4 * 4 + kk
                        kb.op("dve", lambda e, k=k: e.tensor_scalar_mul(out=dgt.t[:], in0=CF(C_ID), scalar1=modT.t[:, 32 + k:33 + k]),
                              reads=[cst.r, modT.r], writes=[dgt.r])
                        kb.op("pe", lambda e, kk=kk, pb=pb: e.matmul(pb.t[:, kk * 128:(kk + 1) * 128], CF(C_ONES), dgt.t[:],
                                                                     start=True, stop=True), reads=[cst.r, dgt.r], writes=[pb.r])
                    kb.op("dve", lambda e, k4=k4, pb=pb: e.tensor_copy(out=g1b.t[:, k4 * 512:(k4 + 1) * 512], in_=pb.t[:, :]),
                          writes=[g1b.r, pb.r])
                def p3_main(t):
                    g = t // 4
                    if t % 4 == 0:
                        kb.dma("sp", mgR.t[:], mgT_d[:, :, g * 512:(g + 1) * 512].rearrange("k p c -> p k c"), writes=[mgR.r])
                    x_ = xt[0]
                    pre = pres[t % 2]
                    kb.dma("sp", x_.t[:], x_d[t], writes=[x_.r])
                    tl = slice((t % 4) * 128, (t % 4 + 1) * 128)
                    for cg in range(4):
                        pb = nb()
                        for k in range(KC):
                            kb.op("pe", lambda e, k=k, cg=cg, pb=pb: e.matmul(
                                pb.t[:, :], mgR.t[:, k, tl], wout.t[:, k, cg * 512:(cg + 1) * 512],
                                start=(k == 0), stop=(k == KC - 1)), reads=[mgR.r, wout.r], writes=[pb.r])
                        cs_ = slice(cg * 512, (cg + 1) * 512)
                        kb.op("dve", lambda e, pb=pb, cs_=cs_, pre=pre: e.tensor_tensor(out=pre.t[:, cs_], in0=pb.t[:, :], in1=g1b.t[:, cs_],
                                                                              op=ALU.mult), reads=[g1b.r], writes=[pre.r, pb.r])
                    kb.op("dve", lambda e, x_=x_, pre=pre: e.scalar_tensor_tensor(out=pre.t[:], in0=x_.t[:], scalar=ALPHA, in1=pre.t[:],
                                                                         op0=ALU.mult, op1=ALU.add), reads=[x_.r], writes=[pre.r])
                    layer_norm_tile(pre, junk, st, lng, lnb)
                    kb.dma("pool", x1_d[t], pre.t[:], reads=[pre.r])
                    return pre
                def p3_tr(t, pre):
                    for k4 in range(4):
                        pb = nb()
                        for kk in range(4):
                            k = k4 * 4 + kk
                            kb.op("pe", lambda e, k=k, kk=kk, pb=pb, pre=pre: e.transpose(
                                pb.t[:, kk * 128:(kk + 1) * 128], pre.t[:, k * 128:(k + 1) * 128], CF(C_ID)),
                                reads=[pre.r, cst.r], writes=[pb.r])
                        for kk in range(4):
                            k = k4 * 4 + kk
                            kb.op("act", lambda e, k=k, kk=kk, pb=pb, t=t: e.activation(
                                out=hT.t[:, k, t * 128:(t + 1) * 128], in_=pb.t[:, kk * 128:(kk + 1) * 128], func=AF.Identity,
                                scale=modT.t[:, 64 + k:65 + k], bias=modT.t[:, 48 + k:49 + k]),
                                reads=[modT.r], writes=[hT_all[k], pb.r])
                pend = None
                for t in range(NT + 1):
                    cur = (t, p3_main(t)) if t < NT else None
                    if pend is not None:
                        p3_tr(*pend)
                    pend = cur

            kb.barrier()
            with ExitStack() as es5:
                def sb5(name, shape, dt=F32, nreg=1):
                    return T(es5.enter_context(nc.sbuf_tensor(name, list(shape), dt)), nreg)
                GT = 1024
                actT = sb5("actT", [128, FC, GT], BF16, nreg=2)
                wfi = [sb5("wfi%d" % i, [128, KC, 256], BF16) for i in range(2)]
                wfo = [sb5("wfo%d" % i, [128, FC, 128], BF16) for i in range(2)]
                sgs = [sb5("sg%d" % i, [128, 512]) for i in range(2)]
                y2c = [sb5("y2c%d" % i, [128, 512]) for i in range(2)]
                y2st = [sb5("y2st%d" % i, [128, 4, 128]) for i in range(1)]
                si = 0
                yi = 0
                pend4 = None

                def p4_tr(yc, ys, oc, t0):
                    pT = nb()
                    for tt in range(4):
                        kb.op("pe", lambda e, tt=tt: e.transpose(
                            pT.t[:, tt * 128:(tt + 1) * 128], yc.t[:, tt * 128:(tt + 1) * 128], CF(C_ID)),
                            reads=[yc.r, cst.r], writes=[pT.r])
                    kb.op("dve", lambda e: e.tensor_copy(
                        out=ys.t[:], in_=pT.t[:, :].rearrange("p (a c) -> p a c", c=128)), writes=[ys.r, pT.r])
                    kb.dma("sp", y2_d[t0:t0 + 4, :, oc * 128:(oc + 1) * 128].rearrange("a p c -> p a c"), ys.t[:],
                           reads=[ys.r])
                for g in range(S // GT):
                    for fc in range(FC):
                        w_ = wfi[fc % 2]
                        kb.dma("pool", w_.t[:].rearrange("p k c -> p (k c)"), wffi_d[fc], writes=[w_.r])
                        for hf in range(GT // 512):
                            gs = slice(g * GT + hf * 512, g * GT + (hf + 1) * 512)
                            pG, pU = nb(), nb()
                            sg_ = sgs[si % 2]
                            si += 1
                            for k in range(KC):
                                kb.op("pe", lambda e, k=k, w_=w_, pG=pG, gs=gs: e.matmul(pG.t[:, :], w_.t[:, k, 0:128], hT.t[:, k, gs],
                                                                                        start=(k == 0), stop=(k == KC - 1)),
                                      reads=[w_.r, hT_all[k]], writes=[pG.r])
                            kb.op("act", lambda e, pG=pG, sg_=sg_: e.activation(out=sg_.t[:], in_=pG.t[:, :], func=AF.Silu),
                                  writes=[sg_.r, pG.r])
                            for k in range(KC):
                                kb.op("pe", lambda e, k=k, w_=w_, pU=pU, gs=gs: e.matmul(pU.t[:, :], w_.t[:, k, 128:256], hT.t[:, k, gs],
                                                                                        start=(k == 0), stop=(k == KC - 1)),
                                      reads=[w_.r, hT_all[k]], writes=[pU.r])
                            kb.op("dve", lambda e, pU=pU, fc=fc, hf=hf, sg_=sg_: e.tensor_tensor(
                                out=actT.t[:, fc, hf * 512:(hf + 1) * 512], in0=pU.t[:, :], in1=sg_.t[:], op=ALU.mult),
                                reads=[sg_.r], writes=[actT.regs[hf], pU.r])
                    for oc in range(16):
                        w_ = wfo[oc % 2]
                        kb.dma("pool", w_.t[:].rearrange("p k c -> p (k c)"), wffo_d[oc], writes=[w_.r])
                        for hf in range(GT // 512):
                            pY = nb()
                            for fc in range(FC):
                                kb.op("pe", lambda e, fc=fc, w_=w_, pY=pY, hf=hf: e.matmul(
                                    pY.t[:, :], w_.t[:, fc, :], actT.t[:, fc, hf * 512:(hf + 1) * 512],
                                    start=(fc == 0), stop=(fc == FC - 1)), reads=[w_.r, actT.regs[hf]], writes=[pY.r])
                            yc = y2c[yi % 2]
                            kb.op("act", lambda e, pY=pY, yc=yc, oc=oc: e.activation(out=yc.t[:], in_=pY.t[:, :], func=AF.Identity,
                                                                                     scale=modT.t[:, 80 + oc:81 + oc]),
                                  reads=[modT.r], writes=[yc.r, pY.r])
                            if pend4 is not None:
                                p4_tr(*pend4)
                            pend4 = (yc, y2st[0], oc, (g * GT + hf * 512) // 128)
                            yi += 1
                            continue
                            pT = nb()
                            for tt in range(4):
                                kb.op("pe", lambda e, tt=tt, yc=yc, pT=pT: e.transpose(
                                    pT.t[:, tt * 128:(tt + 1) * 128], yc.t[:, tt * 128:(tt + 1) * 128], CF(C_ID)),
                                    reads=[yc.r, cst.r], writes=[pT.r])
                            ys = y2st[yi % 2]
                            yi += 1
                            kb.op("dve", lambda e, pT=pT, ys=ys: e.tensor_copy(
                                out=ys.t[:], in_=pT.t[:, :].rearrange("p (a c) -> p a c", c=128)), writes=[ys.r, pT.r])
                            t0 = (g * GT + hf * 512) // 128
                            kb.dma("sp", y2_d[t0:t0 + 4, :, oc * 128:(oc + 1) * 128].rearrange("a p c -> p a c"), ys.t[:],
                                   reads=[ys.r])
                if pend4 is not None:
                    p4_tr(*pend4)
                    pend4 = None
            kb.barrier()
            with ExitStack() as es6:
                def sb6(name, shape, dt=F32, nreg=1):
                    return T(es6.enter_context(nc.sbuf_tensor(name, list(shape), dt)), nreg)
                lng2 = sb6("lng2", [128, D])
                lnb2 = sb6("lnb2", [128, D])
                junk2 = sb6("junk6", [128, D], BF16)
                st2 = sb6("st6", [128, 8])
                xqs = [sb6("xq%d" % i, [128, D]) for i in range(2)]
                y2t = [sb6("y2t%d" % i, [128, D]) for i in range(2)]
                kb.dma("sp", lng2.t[:], lnrep_d[2], writes=[lng2.r])
                kb.dma("sp", lnb2.t[:], lnrep_d[3], writes=[lnb2.r])
                for t in range(NT):
                    xq = xqs[t % 2]
                    yt_ = y2t[t % 2]
                    kb.dma("sp", xq.t[:], x1_d[t], writes=[xq.r])
                    kb.dma("sp", yt_.t[:], y2_d[t], writes=[yt_.r])
                    kb.op("dve", lambda e, xq=xq, yt_=yt_: e.scalar_tensor_tensor(out=xq.t[:], in0=xq.t[:], scalar=ALPHA, in1=yt_.t[:],
                                                                                 op0=ALU.mult, op1=ALU.add),
                          reads=[yt_.r], writes=[xq.r])
                    layer_norm_tile(xq, junk2, st2, lng2, lnb2)
                    final_toks.append(kb.dma("pool", out_d[t], xq.t[:], reads=[xq.r]))

        if dbg:
            dbg_outs["yaT"] = nc.dram_tensor("dbg_yaT", [NH_M, 128, S], BF16, kind="ExternalOutput").ap()
            with ExitStack() as esd:
                tmpd = T(esd.enter_context(nc.sbuf_tensor("tmpd", [128, S], BF16)))
                for h in range(NH_M):
                    kb.dma("sp", tmpd.t[:], yaT_d[h], writes=[tmpd.r])
                    final_toks.append(kb.dma("sp", dbg_outs["yaT"][h], tmpd.t[:], reads=[tmpd.r]))
        if dbg and stop_after not in ("p0", "p1a"):
            dbg_outs["ybT"] = nc.dram_tensor("dbg_ybT", [HV, 128, S], BF16, kind="ExternalOutput").ap()
            with ExitStack() as esd:
                tmpd = T(esd.enter_context(nc.sbuf_tensor("tmpd2", [128, S], BF16)))
                for h in range(HV):
                    kb.dma("sp", tmpd.t[:], ybT_d[h], writes=[tmpd.r])
                    final_toks.append(kb.dma("sp", dbg_outs["ybT"][h], tmpd.t[:], reads=[tmpd.r]))
        for tok in final_toks:
            kb.wait_tok("sp", tok)
        for key, n in kb.dcnt.items():
            if n > 0:
                kb.wait_tok("sp", (key, 16 * n))
        print("instructions:", kb.n_inst, "waits:", kb.n_wait, {k: v for k, v in kb.cnt.items()})
    return nc


def _klay(w):
    K, C = w.shape
    return np.ascontiguousarray(w.reshape(K // 128, 128, C).transpose(1, 0, 2))


def _rel_bucket_np(n):
    n = np.asarray(n)
    max_exact = 16
    nn = np.maximum(n, 0)
    nf = np.maximum(nn, 1).astype(np.float32)
    large = max_exact + (np.log(nf / np.float32(max_exact)) / np.float32(math.log(128 / max_exact))
                         * np.float32(32 - max_exact)).astype(np.int32)
    large = np.minimum(large, 31)
    return np.where(nn < max_exact, nn, large)


def make_consts():
    p = np.arange(128)[:, None]
    f = np.arange(128)[None, :]
    c = np.zeros((128, NCONST, 128), np.float32)
    c[:, C_ID] = (p == f)
    c[:, C_TRI] = (p <= f)
    c[:, C_AST] = (p > f)
    bd = (p // 32) == (f // 32)
    c[:, C_SUBD] = (f > p) & bd
    c[:, C_SLBD] = (f < p) & bd
    c[:, C_M1] = ((p // 32) % 2 == 1) & ((f // 32) == (p // 32) - 1)
    c[:, C_M2] = ((p // 32) >= 2) & ((f // 32) < 2)
    c[:, C_UI] = (f >= p)
    c[:, C_ONES] = 1.0
    return c


def prep_shared(inp):
    f32 = np.float32
    sh = {}
    w_ada = inp["w_ada"][0]
    wl = w_ada.reshape(KC, 128, 96, 128).transpose(2, 1, 0, 3)
    wl = wl.reshape(24, 4, 128, KC, 128).transpose(0, 2, 1, 3, 4)
    sh["wada_lay"] = np.ascontiguousarray(wl).reshape(24, 128, 4 * KC * 128)
    sh["bada_lay"] = np.ascontiguousarray(inp["b_ada"][0].reshape(96, 128).T)
    sh["consts"] = make_consts()
    w_in = inp["w_in"][0]
    o1 = 3072
    o2 = o1 + 4096
    o3 = o2 + 2048
    o5 = o3 + 32
    wm = np.empty((NH_M, 128, KC, 384), f32)
    for h in range(NH_M):
        cols = np.concatenate([np.arange(h * 128, (h + 1) * 128), 1024 + np.arange(h * 128, (h + 1) * 128),
                               2048 + np.arange(h * 128, (h + 1) * 128)])
        wm[h] = _klay(w_in[:, cols])
    sh["wmoba_lay"] = wm.reshape(NH_M, 128, KC * 384)
    rb = inp["rel_bias"]
    i = np.arange(128)[:, None]
    j = np.arange(128)[None, :]
    bd = _rel_bucket_np(j - i)
    bo = _rel_bucket_np(128 + j - i)
    mb = np.empty((128, 16, 128), f32)
    for h in range(NH_M):
        mb[:, h, :] = rb[bd, h]
        mb[:, 8 + h, :] = rb[bo, h]
    sh["mbias_lay"] = mb
    sh["c31_rep"] = np.ascontiguousarray(np.broadcast_to(rb[31][None, :], (128, NH_M))).astype(f32)
    wg = np.empty((8, 128, KC, 768), f32)
    for jh in range(8):
        cols = np.concatenate([o1 + np.arange(jh * 128, (jh + 1) * 128),
                               o1 + 1024 + np.arange(jh * 128, (jh + 1) * 128),
                               o1 + 2048 + np.arange(jh * 256, (jh + 1) * 256),
                               o2 + np.arange(jh * 256, (jh + 1) * 256)])
        wg[jh] = _klay(w_in[:, cols])
    sh["wgdn_lay"] = wg.reshape(8, 128, KC * 768)
    sh["wab_lay"] = _klay(w_in[:, o3:o5]).reshape(128, KC * 32)
    cw = inp["conv_w"][0]
    sh["conv_lay"] = np.ascontiguousarray(cw.reshape(4, 32, 128).transpose(2, 1, 0)).reshape(128, 128)
    sh["alog_rep"] = np.ascontiguousarray(np.broadcast_to(inp["a_log"][0][None, :], (128, HV))).astype(f32)
    sh["dtb_rep"] = np.ascontiguousarray(np.broadcast_to(inp["dt_bias"][0][None, :], (128, HV))).astype(f32)
    sh["normw_rep"] = np.ascontiguousarray(np.broadcast_to(inp["gdn_norm_w"][0][None, :], (128, 128))).astype(f32)
    wgate = w_in[:, o5:o5 + 4096]
    wgl = np.empty((16, 128, KC, 256), f32)
    for cc in range(16):
        cols = np.concatenate([np.arange(cc * 128, (cc + 1) * 128), 2048 + np.arange(cc * 128, (cc + 1) * 128)])
        wgl[cc] = _klay(wgate[:, cols])
    sh["wgate_lay"] = wgl.reshape(16, 128, KC * 256)
    wpm = inp["w_proj_moba"][0]
    wpg = inp["w_proj_gdn"][0]
    sh["wpm_lay"] = np.stack([_klay(wpm[:, cc * 128:(cc + 1) * 128]) for cc in range(16)]).reshape(16, 128, 8 * 128)
    sh["wpg_lay"] = np.stack([_klay(wpg[:, cc * 128:(cc + 1) * 128]) for cc in range(16)]).reshape(16, 128, 16 * 128)
    wo = inp["w_out"][0]
    sh["wout_lay"] = np.stack([_klay(wo[:, g * 512:(g + 1) * 512]) for g in range(4)]).reshape(4, 128, KC * 512)
    sh["ln_rep"] = np.stack([np.broadcast_to(inp[k][0][None, :], (128, D)) for k in
                             ("ln1_g", "ln1_b", "ln2_g", "ln2_b")]).astype(f32)
    wfi = inp["w_ffn_in"][0]
    wfl = np.empty((FC, 128, KC, 256), f32)
    for fc in range(FC):
        cols = np.concatenate([np.arange(fc * 128, (fc + 1) * 128), DFF + np.arange(fc * 128, (fc + 1) * 128)])
        wfl[fc] = _klay(wfi[:, cols])
    sh["wffi_lay"] = wfl.reshape(FC, 128, KC * 256)
    wfo = inp["w_ffn_out"][0]
    sh["wffo_lay"] = np.stack([_klay(wfo[:, oc * 128:(oc + 1) * 128]) for oc in range(16)]).reshape(16, 128, FC * 128)
    return sh


def prep_core(inp, b):
    x = inp["x"][b]
    return {
        "xT": np.ascontiguousarray(x.T).reshape(KC, 128, S),
        "x": np.ascontiguousarray(x).reshape(NT, 128, D),
        "c_lay": np.ascontiguousarray(inp["c"][b].reshape(KC, 128).T),
    }


_NC_CACHE = {}


def kernel(**inputs):
    inp = {k: np.asarray(v, dtype=np.float32) for k, v in inputs.items()}
    if "nc" not in _NC_CACHE:
        _NC_CACHE["nc"] = build_nc()
    nc = _NC_CACHE["nc"]
    shared = prep_shared(inp)
    in_maps = []
    for b in range(8):
        m = dict(shared)
        m.update(prep_core(inp, b))
        in_maps.append(m)
    res = run_bass_kernel_spmd(nc, in_maps, core_ids=list(range(8)))
    out = np.stack([np.asarray(r["out"]).reshape(S, D) for r in res.results], axis=0)
    return out.astype(np.float32)
```
